# Optimizing a Trainium2 kernel written in Bass

```python
import math
import jax, jax.numpy as jnp
from jax import lax
import numpy as np

D_MODEL = 1024
BATCH = 32
SEQ = 2048
DEPTH = 4

GRID_W = 64
CTX_LEN = 256
N_EVEN = (DEPTH + 1) // 2
N_ODD = DEPTH // 2
D_FF = 4 * D_MODEL
Q_BLOCK = 128
ROPE_THETA = 10000.0
RMS_EPS = 1e-6
N_MOD = 6

HEAD_DIM = 64
DIFF_QK_DIM = HEAD_DIM
DIFF_V_DIM = 2 * HEAD_DIM
DIFF_HEADS = (D_MODEL // 2) // DIFF_V_DIM
DIFF_SCALE = DIFF_QK_DIM ** -0.5
GQA_DIM = HEAD_DIM
GQA_Q_HEADS = (D_MODEL // 2) // GQA_DIM
GQA_KV_HEADS = 2
GQA_GROUP = GQA_Q_HEADS // GQA_KV_HEADS
GQA_SCALE = GQA_DIM ** -0.5
AB_SPLITS = (DIFF_HEADS * 2 * DIFF_QK_DIM, DIFF_HEADS * 2 * DIFF_QK_DIM, DIFF_HEADS * DIFF_V_DIM,
             GQA_Q_HEADS * GQA_DIM, GQA_KV_HEADS * GQA_DIM, GQA_KV_HEADS * GQA_DIM)
AB_IN_W = sum(AB_SPLITS)
AB_OUT_W = DIFF_HEADS * DIFF_V_DIM + GQA_Q_HEADS * GQA_DIM
MLA_NOPE = 64
MLA_ROPE = 32
MLA_QK_DIM = MLA_NOPE + MLA_ROPE
MLA_V = 64
MLA_HEADS = D_MODEL // MLA_V
MLA_Q_RANK = D_MODEL // 2
MLA_KV_RANK = D_MODEL // 4
MLA_SPLITS = (MLA_Q_RANK, MLA_KV_RANK, MLA_ROPE)
MLA_IN_W = sum(MLA_SPLITS)
MLA_OUT_W = MLA_HEADS * MLA_V
MLA_SCALE = MLA_QK_DIM ** -0.5

kernel_name = 'hybrid_diffattn_gqa_mla_prefix_dit'


def rmsnorm(x, g):
    xf = x.astype(jnp.float32)
    y = xf * lax.rsqrt(jnp.mean(xf * xf, axis=-1, keepdims=True) + RMS_EPS)
    return (y * g.astype(jnp.float32)).astype(x.dtype)


def modulate(h, shift, scale):
    return h * (1 + scale) + shift


def lambda_init(layer):
    return 0.8 - 0.6 * math.exp(-0.3 * layer)


def split_cols(t, widths):
    idx = np.cumsum(widths)[:-1].tolist()
    return jnp.split(t, idx, axis=-1)


def axial_rope(n_tokens, rot_dim):
    rows = n_tokens // GRID_W
    n_freq = rot_dim // 4
    inv_freq = ROPE_THETA ** (-jnp.arange(n_freq, dtype=jnp.float32) / n_freq)
    row = jnp.repeat(jnp.arange(rows, dtype=jnp.float32), GRID_W)
    col = jnp.tile(jnp.arange(GRID_W, dtype=jnp.float32), rows)
    ang = jnp.concatenate([row[:, None] * inv_freq, col[:, None] * inv_freq], axis=-1)
    return jnp.cos(ang), jnp.sin(ang)


def apply_rope(x, cos, sin):
    half = x.shape[-1] // 2
    x1, x2 = x[..., :half], x[..., half:]
    cos, sin = cos.astype(x.dtype), sin.astype(x.dtype)
    return jnp.concatenate([x1 * cos - x2 * sin, x2 * cos + x1 * sin], axis=-1)


def blockwise(fn, q):
    *lead, n_q, d = q.shape
    nb = n_q // Q_BLOCK
    qb = jnp.moveaxis(q.reshape(*lead, nb, Q_BLOCK, d), -3, 0)
    ob = jnp.moveaxis(lax.map(fn, qb), 0, -3)
    return ob.reshape(*ob.shape[:-3], n_q, ob.shape[-1])


def gqa_attend(q, k, v, scale):
    s = jnp.einsum('bhgqd,bhkd->bhgqk', q, k).astype(jnp.float32) * scale
    p = jax.nn.softmax(s, axis=-1).astype(v.dtype)
    return jnp.einsum('bhgqk,bhkd->bhgqd', p, v)


def diff_attend(q, k, v, lam):
    s = jnp.einsum('bhmqd,bhmkd->bhmqk', q, k).astype(jnp.float32) * DIFF_SCALE
    p = jax.nn.softmax(s, axis=-1)
    a = (p[:, :, 0] - lam * p[:, :, 1]).astype(v.dtype)
    return jnp.einsum('bhqk,bhkd->bhqd', a, v)


def prefix_attention(attend, q_lat, k_lat, v_lat, q_ctx, k_ctx, v_ctx, compute_ctx):
    k_all = jnp.concatenate([k_ctx, k_lat], axis=-2)
    v_all = jnp.concatenate([v_ctx, v_lat], axis=-2)
    out_lat = blockwise(lambda qb: attend(qb, k_all, v_all), q_lat)
    out_ctx = attend(q_ctx, k_ctx, v_ctx) if compute_ctx else None
    return out_lat, out_ctx


def diff_gqa_mixer(h_lat, h_ctx, rope, w_in, w_out, diff_qk_norm, diff_lambda, diff_subln,
                   gqa_qk_norm, lam_init, compute_ctx):
    def project(h, rope_cs):
        b, n, _ = h.shape
        aq, ak, av, bq, bk, bv = split_cols(h @ w_in, AB_SPLITS)
        aq = rmsnorm(aq.reshape(b, n, DIFF_HEADS, 2, DIFF_QK_DIM).transpose(0, 2, 3, 1, 4), diff_qk_norm[0])
        ak = rmsnorm(ak.reshape(b, n, DIFF_HEADS, 2, DIFF_QK_DIM).transpose(0, 2, 3, 1, 4), diff_qk_norm[1])
        av = av.reshape(b, n, DIFF_HEADS, DIFF_V_DIM).transpose(0, 2, 1, 3)
        bq = rmsnorm(bq.reshape(b, n, GQA_KV_HEADS, GQA_GROUP, GQA_DIM).transpose(0, 2, 3, 1, 4), gqa_qk_norm[0])
        bk = rmsnorm(bk.reshape(b, n, GQA_KV_HEADS, GQA_DIM).transpose(0, 2, 1, 3), gqa_qk_norm[1])
        bv = bv.reshape(b, n, GQA_KV_HEADS, GQA_DIM).transpose(0, 2, 1, 3)
        if rope_cs is not None:
            cos, sin = rope_cs
            aq, ak, bq, bk = (apply_rope(t, cos, sin) for t in (aq, ak, bq, bk))
        return aq, ak, av, bq, bk, bv

    aq, ak, av, bq, bk, bv = project(h_lat, rope)
    caq, cak, cav, cbq, cbk, cbv = project(h_ctx, None)
    lq1, lk1, lq2, lk2 = diff_lambda.astype(jnp.float32)
    lam = jnp.exp(jnp.sum(lq1 * lk1)) - jnp.exp(jnp.sum(lq2 * lk2)) + lam_init
    a_lat, a_ctx = prefix_attention(lambda q, k, v: diff_attend(q, k, v, lam),
                                    aq, ak, av, caq, cak, cav, compute_ctx)
    b_lat, b_ctx = prefix_attention(lambda q, k, v: gqa_attend(q, k, v, GQA_SCALE),
                                    bq, bk, bv, cbq, cbk, cbv, compute_ctx)

    def merge(a, bo):
        bsz, _, n, _ = a.shape
        a = (rmsnorm(a, diff_subln) * (1.0 - lam_init)).transpose(0, 2, 1, 3).reshape(bsz, n, DIFF_HEADS * DIFF_V_DIM)
        bo = bo.transpose(0, 3, 1, 2, 4).reshape(bsz, n, GQA_Q_HEADS * GQA_DIM)
        return jnp.concatenate([a, bo], axis=-1) @ w_out

    y_ctx = merge(a_ctx, b_ctx) if compute_ctx else None
    return merge(a_lat, b_lat), y_ctx


def mla_mixer(h_lat, h_ctx, rope, w_in, q_norm, w_q_up, kv_norm, w_kv_up, qk_norm, w_out, compute_ctx):
    def project(h, rope_cs):
        b, n, _ = h.shape
        q_c, kv_c, k_r = split_cols(h @ w_in, MLA_SPLITS)
        q = (rmsnorm(q_c, q_norm) @ w_q_up).reshape(b, n, MLA_HEADS, MLA_QK_DIM).transpose(0, 2, 1, 3)
        kv = (rmsnorm(kv_c, kv_norm) @ w_kv_up).reshape(b, n, MLA_HEADS, MLA_NOPE + MLA_V).transpose(0, 2, 1, 3)
        k_nope, v = kv[..., :MLA_NOPE], kv[..., MLA_NOPE:]
        k_rope = jnp.broadcast_to(k_r[:, None], (b, MLA_HEADS, n, MLA_ROPE))
        k = jnp.concatenate([k_nope, k_rope], axis=-1)
        q = rmsnorm(q, qk_norm[0])
        k = rmsnorm(k, qk_norm[1])
        if rope_cs is not None:
            cos, sin = rope_cs
            q = jnp.concatenate([q[..., :MLA_NOPE], apply_rope(q[..., MLA_NOPE:], cos, sin)], axis=-1)
            k = jnp.concatenate([k[..., :MLA_NOPE], apply_rope(k[..., MLA_NOPE:], cos, sin)], axis=-1)
        return q[:, :, None], k, v

    q, k, v = project(h_lat, rope)
    cq, ck, cv = project(h_ctx, None)
    o_lat, o_ctx = prefix_attention(lambda qq, kk, vv: gqa_attend(qq, kk, vv, MLA_SCALE),
                                    q, k, v, cq, ck, cv, compute_ctx)

    def out(o):
        bsz, _, _, n, _ = o.shape
        return o[:, :, 0].transpose(0, 2, 1, 3).reshape(bsz, n, MLA_OUT_W) @ w_out

    y_ctx = out(o_ctx) if compute_ctx else None
    return out(o_lat), y_ctx


def sqrelu_mlp(h, w1, w2):
    return jnp.square(jax.nn.relu(h @ w1)) @ w2


def setup_inputs(seed: int = 0) -> dict:
    key = jax.random.key(seed)
    ks = jax.random.split(key, 23)
    f32 = jnp.float32

    def w(k, shape, fan_in, gain=1.0):
        return jax.random.normal(k, shape, f32) * (gain * fan_in ** -0.5)

    def g(k, shape):
        return 1.0 + 0.02 * jax.random.normal(k, shape, f32)

    return {
        'x': jax.random.normal(ks[0], (BATCH, SEQ, D_MODEL), f32),
        'c': jax.random.normal(ks[1], (BATCH, D_MODEL), f32),
        'ctx': jax.random.normal(ks[2], (BATCH, CTX_LEN, D_MODEL), f32),
        'c_ctx': jax.random.normal(ks[3], (D_MODEL,), f32),
        'ada_w': w(ks[4], (DEPTH, D_MODEL, N_MOD * D_MODEL), D_MODEL, 0.5),
        'ada_b': 0.02 * jax.random.normal(ks[5], (DEPTH, N_MOD * D_MODEL), f32),
        'norm_mix': g(ks[6], (DEPTH, D_MODEL)),
        'norm_mlp': g(ks[7], (DEPTH, D_MODEL)),
        'mlp_w1': w(ks[8], (DEPTH, D_MODEL, D_FF), D_MODEL),
        'mlp_w2': w(ks[9], (DEPTH, D_FF, D_MODEL), D_FF),
        'ab_w_in': w(ks[10], (N_EVEN, D_MODEL, AB_IN_W), D_MODEL),
        'ab_w_out': w(ks[11], (N_EVEN, AB_OUT_W, D_MODEL), AB_OUT_W),
        'diff_qk_norm': g(ks[12], (N_EVEN, 2, DIFF_QK_DIM)),
        'diff_lambda': 0.1 * jax.random.normal(ks[13], (N_EVEN, 4, DIFF_QK_DIM), f32),
        'diff_subln': g(ks[14], (N_EVEN, DIFF_V_DIM)),
        'gqa_qk_norm': g(ks[15], (N_EVEN, 2, GQA_DIM)),
        'mla_w_in': w(ks[16], (N_ODD, D_MODEL, MLA_IN_W), D_MODEL),
        'mla_q_norm': g(ks[17], (N_ODD, MLA_Q_RANK)),
        'mla_w_q_up': w(ks[18], (N_ODD, MLA_Q_RANK, MLA_HEADS * MLA_QK_DIM), MLA_Q_RANK),
        'mla_kv_norm': g(ks[19], (N_ODD, MLA_KV_RANK)),
        'mla_w_kv_up': w(ks[20], (N_ODD, MLA_KV_RANK, MLA_HEADS * (MLA_NOPE + MLA_V)), MLA_KV_RANK),
        'mla_qk_norm': g(ks[21], (N_ODD, 2, MLA_QK_DIM)),
        'mla_w_out': w(ks[22], (N_ODD, MLA_OUT_W, D_MODEL), MLA_OUT_W),
    }


def reference(x, c, ctx, c_ctx, ada_w, ada_b, norm_mix, norm_mlp, mlp_w1, mlp_w2,
              ab_w_in, ab_w_out, diff_qk_norm, diff_lambda, diff_subln, gqa_qk_norm,
              mla_w_in, mla_q_norm, mla_w_q_up, mla_kv_norm, mla_w_kv_up, mla_qk_norm, mla_w_out):
    n_lat = x.shape[1]
    rope_hd = axial_rope(n_lat, HEAD_DIM)
    rope_mla = axial_rope(n_lat, MLA_ROPE)
    silu_c = jax.nn.silu(c)
    silu_cc = jax.nn.silu(c_ctx)
    for layer in range(DEPTH):
        compute_ctx = layer < DEPTH - 1
        i = layer // 2
        mod_lat = (silu_c @ ada_w[layer] + ada_b[layer])[:, None, :]
        mod_ctx = silu_cc @ ada_w[layer] + ada_b[layer]
        sh1, sc1, g1, sh2, sc2, g2 = jnp.split(mod_lat, N_MOD, axis=-1)
        csh1, csc1, cg1, csh2, csc2, cg2 = jnp.split(mod_ctx, N_MOD, axis=-1)
        h_lat = modulate(rmsnorm(x, norm_mix[layer]), sh1, sc1)
        h_ctx = modulate(rmsnorm(ctx, norm_mix[layer]), csh1, csc1)
        if layer % 2 == 0:
            y_lat, y_ctx = diff_gqa_mixer(h_lat, h_ctx, rope_hd, ab_w_in[i], ab_w_out[i], diff_qk_norm[i],
                                          diff_lambda[i], diff_subln[i], gqa_qk_norm[i],
                                          lambda_init(layer), compute_ctx)
        else:
            y_lat, y_ctx = mla_mixer(h_lat, h_ctx, rope_mla, mla_w_in[i], mla_q_norm[i], mla_w_q_up[i],
                                     mla_kv_norm[i], mla_w_kv_up[i], mla_qk_norm[i], mla_w_out[i],
                                     compute_ctx)
        x = x + g1 * y_lat
        x = x + g2 * sqrelu_mlp(modulate(rmsnorm(x, norm_mlp[layer]), sh2, sc2), mlp_w1[layer], mlp_w2[layer])
        if compute_ctx:
            ctx = ctx + cg1 * y_ctx
            ctx = ctx + cg2 * sqrelu_mlp(modulate(rmsnorm(ctx, norm_mlp[layer]), csh2, csc2),
                                         mlp_w1[layer], mlp_w2[layer])
    return x
```

```python
import math
from contextlib import ExitStack
import numpy as np
import concourse.bass as bass
import concourse.mybir as mybir
from concourse.bass_utils import run_bass_kernel_spmd

F32 = mybir.dt.float32
BF16 = mybir.dt.bfloat16
ALU = mybir.AluOpType
AF = mybir.ActivationFunctionType

D = 1024
NT = 2304
NCTX = 256
NLAT = 2048
DEPTH = 4
EPS = 1e-6
TB = [(0, 256), (256, 512), (768, 512), (1280, 512), (1792, 512)]
PIECE = 4096
NRING = 3
N_CORES = 8
SEQ_PER_CORE = 4

PP_NMIX = 0
PP_NMLP = 32
PP_ADAB = 64
PP_GE = 256
PP_SUBLN = 264
PP_DLAM = 266
PP_MQ = 778
PP_MKV = 786
PP_MQK = 790
NPP = 794


def lambda_init(layer):
    return 0.8 - 0.6 * math.exp(-0.3 * layer)


class Op:
    __slots__ = ("eng", "fn", "deps", "odeps", "signal", "sem", "val", "isdma", "epoch", "idx", "dur", "ndep", "users", "rdy", "st", "fin", "done")


class Prog:
    ENGS = ("pe", "act", "dve", "pool", "sp")
    WIN = 48

    def __init__(self, nc, stack):
        self.nc = nc
        self.stack = stack
        self.ops = {e: [] for e in self.ENGS}
        self.res = {}
        self.epochs = []
        self.dsems = {}
        self.nsem = 0
        self.new_epoch()

    def _sem(self, name):
        self.nsem += 1
        return self.stack.enter_context(self.nc.semaphore(name))

    def new_epoch(self):
        i = len(self.epochs)
        self.epochs.append({e: self._sem(f"e{i}_{e}") for e in self.ENGS})

    def add(self, eng, fn, r=(), w=(), dma=None):
        op = Op()
        op.eng = eng
        op.fn = fn
        op.deps = []
        op.signal = False
        op.isdma = dma is not None
        op.epoch = len(self.epochs) - 1
        op.sem = None
        op.val = 0
        cand = []
        for n in r:
            st = self.res.setdefault(n, [None, []])
            if st[0] is not None:
                cand.append((st[0], "raw"))
        for n in w:
            st = self.res.setdefault(n, [None, []])
            if st[0] is not None:
                cand.append((st[0], "waw"))
            for rd in st[1]:
                cand.append((rd, "war"))
        seen = set()
        op.odeps = []
        op.idx = len(self.ops[eng])
        op.dur = est_dur(eng, fn, op.isdma)
        for d, t in cand:
            if d is op or id(d) in seen:
                continue
            seen.add(id(d))
            if (not d.isdma) and (not op.isdma) and d.eng == eng:
                if eng == "pe":
                    op.odeps.append(d)
                    continue
            d.signal = True
            op.deps.append(d)
        for n in w:
            st = self.res[n]
            st[0] = op
            st[1] = []
        wset = set(w)
        lim = op.idx - self.WIN - 2
        for n in r:
            if n in wset:
                continue
            st = self.res[n]
            if not op.isdma:
                st[1] = [x for x in st[1] if x.isdma or x.eng != eng or x.idx > lim]
            st[1].append(op)
        if dma is not None:
            ent = self.dsems.get(dma)
            if ent is None:
                ent = [self._sem("d_" + dma), 0]
                self.dsems[dma] = ent
            ent[1] += 16
            op.sem = ent[0]
            op.val = ent[1]
        self.ops[eng].append(op)
        return op

    def fence(self, old, new):
        evs = []
        for n in old:
            st = self.res.get(n)
            if st is None:
                continue
            if st[0] is not None:
                evs.append(st[0])
            evs.extend(st[1])
        uniq = []
        seen = set()
        for e in evs:
            if id(e) not in seen:
                seen.add(id(e))
                uniq.append(e)
        for n in new:
            st = self.res.setdefault(n, [None, []])
            st[1] = list(st[1]) + uniq

    def schedule(self):
        import heapq
        W = self.WIN
        ENGS = self.ENGS
        for e in ENGS:
            for op in self.ops[e]:
                op.users = []
                op.ndep = 0
                op.rdy = 0.0
                op.done = False
        for e in ENGS:
            for op in self.ops[e]:
                for d in op.deps:
                    d.users.append(op)
                    op.ndep += 1
                for d in op.odeps:
                    d.users.append(op)
                    op.ndep += 1
        lists = {e: list(self.ops[e]) for e in ENGS}
        head = {e: 0 for e in ENGS}
        free = {e: 0.0 for e in ENGS}
        dma_free = [0.0]
        order = {e: [] for e in ENGS}
        remaining = sum(len(v) for v in lists.values())

        def best_of(e):
            lst = lists[e]
            n = len(lst)
            i = head[e]
            while i < n and lst[i].done:
                i += 1
            head[e] = i
            w = 1 if e == "sp" else W
            best = None
            bst = 0.0
            fr = free[e]
            iend = min(n, i + w)
            while i < iend:
                op = lst[i]
                if not op.done:
                    if op.ndep == 0:
                        st = op.rdy if op.rdy > fr else fr
                        if best is None or st < bst:
                            best = op
                            bst = st
                            if st <= fr:
                                break
                i += 1
            return best, bst

        cache = {e: best_of(e) for e in ENGS}
        while remaining > 0:
            be = None
            bop = None
            bst = 0.0
            for e in ENGS:
                op, st = cache[e]
                if op is None:
                    continue
                if bop is None or st < bst:
                    be, bop, bst = e, op, st
            if bop is None:
                raise RuntimeError("scheduler deadlock")
            bop.done = True
            bop.st = bst
            if bop.isdma:
                free[be] = bst + 60.0
                t0 = max(bst, dma_free[0])
                dma_free[0] = t0 + bop.dur
                fin = t0 + bop.dur + 2000.0
            else:
                fin = bst + bop.dur
                free[be] = fin
            bop.fin = fin
            order[be].append(bop)
            remaining -= 1
            dirty = {be}
            for u in bop.users:
                u.ndep -= 1
                lat = fin + (0.0 if (u.eng == be and not bop.isdma) else 120.0)
                if lat > u.rdy:
                    u.rdy = lat
                if u.ndep == 0:
                    dirty.add(u.eng)
            for e in dirty:
                cache[e] = best_of(e)
        self.ops = order
        self.sim_time = max(free.values())

    def emit(self, final_waits):
        self.schedule()
        for e in self.ENGS:
            cnt = {}
            for op in self.ops[e]:
                if op.isdma:
                    continue
                if op.signal:
                    cnt[op.epoch] = cnt.get(op.epoch, 0) + 1
                    op.val = cnt[op.epoch]
                    op.sem = self.epochs[op.epoch][e]
        ops = self.ops

        def run(e, eng):
            waited = {}
            for op in ops[e]:
                for d in op.deps:
                    k = d.sem.num
                    if waited.get(k, 0) >= d.val:
                        continue
                    eng.wait_ge(d.sem, d.val)
                    waited[k] = d.val
                fns = op.fn if isinstance(op.fn, list) else [op.fn]
                ins = None
                for (mname, a, k) in fns:
                    ins = getattr(eng, mname)(*a, **k)
                if op.isdma:
                    ins.then_inc(op.sem, 16)
                elif op.signal:
                    ins.then_inc(op.sem, 1)
            if e == "sp":
                for d in final_waits:
                    eng.wait_ge(d.sem, d.val)

        with self.nc.Block() as block:
            @block.tensor
            def _(eng):
                run("pe", eng)

            @block.scalar
            def _(eng):
                run("act", eng)

            @block.vector
            def _(eng):
                run("dve", eng)

            @block.gpsimd
            def _(eng):
                run("pool", eng)

            @block.sync
            def _(eng):
                run("sp", eng)


def _free_elems(ap):
    n = 1
    for d in ap.shape[1:]:
        n *= int(d)
    return n


def est_dur(eng, fn, isdma):
    fns = fn if isinstance(fn, list) else [fn]
    tot = 0.0
    for (mname, a, k) in fns:
        out = k.get("out", a[0] if a else None)
        fe = _free_elems(out) if out is not None else 512
        if isdma:
            bpe = 2 if out.dtype == BF16 else 4
            tot += fe * int(out.shape[0]) * bpe / 180.0
        elif eng == "pe":
            tot += 45.0 + 0.41 * max(fe, 64)
        elif eng == "act":
            tot += 220.0 + 0.83 * fe
        elif eng == "dve":
            tot += (300.0 + 5.4 * fe) if mname == "reciprocal" else (250.0 + 1.04 * fe)
        else:
            tot += 300.0 + 1.45 * fe
    return tot


def I(mname, *a, **k):
    return (mname, a, k)


class Rot:
    def __init__(self, items):
        self.items = list(items)
        self.i = 0

    def next(self):
        it = self.items[self.i % len(self.items)]
        self.i += 1
        return it


import os
class Builder:
    def __init__(self, nseq, nlayers):
        self.nseq = nseq
        self.nlayers = nlayers
        self.nc = bass.Bass("TRN2", target_bir_lowering=False)
        self.pieces = []

    def dram_in(self, name, shape, dt=F32):
        return self.nc.dram_tensor(name, list(shape), dt, kind="ExternalInput").ap()

    def add_piece(self, src, length):
        self.pieces.append((src, length))
        return len(self.pieces) - 1

    def build(self):
        nc = self.nc
        nseq, nlayers = self.nseq, self.nlayers
        with ExitStack() as stack:
            self.stack = stack
            P = self.P = Prog(nc, stack)
            xTd = self.dram_in("xTd", [nseq, 128, 8, NT])
            cTd = self.dram_in("cTd", [128, 8, 5])
            ppd = self.dram_in("ppd", [128, NPP])
            cstd = self.dram_in("cstd", [128, 640])
            roped = self.dram_in("roped", [2, 128, 4096])
            ada_w = self.dram_in("ada_w", [4, 1024, 6144])
            mlp_w1 = self.dram_in("mlp_w1", [4, 1024, 4096])
            mlp_w2 = self.dram_in("mlp_w2", [4, 4096, 1024])
            w_in_e = self.dram_in("w_in_e", [2, 1024, 2304])
            w_out_e = self.dram_in("w_out_e", [2, 1024, 1024])
            w_in_m = self.dram_in("w_in_m", [2, 1024, 864])
            w_q_up = self.dram_in("w_q_up", [2, 512, 1536])
            w_kv_up = self.dram_in("w_kv_up", [2, 256, 2048])
            w_out_m = self.dram_in("w_out_m", [2, 1024, 1024])
            outT = nc.dram_tensor("outT", [nseq, 128, 8, NLAT], F32, kind="ExternalOutput").ap()

            def kmaj(w2d, c0, W):
                return w2d[:, c0:c0 + W].rearrange("(k p) w -> p k w", p=128)

            pc = {}
            pc["cst"] = self.add_piece(cstd[:, :], 640)
            pc["rope_e"] = self.add_piece(roped[0], 4096)
            pc["rope_m"] = self.add_piece(roped[1], 4096)
            for l in range(nlayers):
                for g in range(12):
                    pc[("ada", l, g)] = self.add_piece(kmaj(ada_w[l], g * 512, 512), 4096)
            for l in range(nlayers):
                i = l // 2
                if l % 2 == 0:
                    pc[("E0", l)] = self.add_piece(kmaj(w_in_e[i], 0, 512), 4096)
                    pc[("E1", l)] = self.add_piece(kmaj(w_in_e[i], 512, 256), 2048)
                    pc[("E2", l)] = self.add_piece(kmaj(w_in_e[i], 768, 512), 4096)
                    pc[("E3", l)] = self.add_piece(kmaj(w_in_e[i], 1280, 512), 4096)
                    pc[("E4", l)] = self.add_piece(kmaj(w_in_e[i], 1792, 512), 4096)
                    pc[("E5", l)] = self.add_piece(kmaj(w_out_e[i], 0, 512), 4096)
                    pc[("E6", l)] = self.add_piece(kmaj(w_out_e[i], 512, 512), 4096)
                else:
                    pc[("M0", l)] = self.add_piece(kmaj(w_in_m[i], 0, 512), 4096)
                    pc[("M1", l)] = self.add_piece(kmaj(w_in_m[i], 512, 352), 2816)
                    for hg in range(8):
                        pc[("M2", l, hg)] = self.add_piece(kmaj(w_kv_up[i], hg * 256, 256), 512)
                        pc[("M3", l, hg)] = self.add_piece(kmaj(w_q_up[i], hg * 192, 192), 768)
                    for hp in range(4):
                        pc[("M5", l, hp)] = self.add_piece(kmaj(w_out_m[i, hp * 256:(hp + 1) * 256, :], 0, 1024), 2048)
                for g in range(8):
                    pc[("W1", l, g)] = self.add_piece(kmaj(mlp_w1[l], g * 512, 512), 4096)
                for o in range(8):
                    pc[("W2", l, o)] = self.add_piece(kmaj(mlp_w2[l], o * 128, 128), 4096)
            self.pc = pc
            npieces = len(self.pieces)
            wbf = nc.dram_tensor("wbf", [npieces, 128, PIECE], BF16).ap()
            self.wbf = wbf

            def sb(name, shape, dt):
                return stack.enter_context(nc.sbuf_tensor(name, list(shape), dt))

            def ps(name, shape):
                return stack.enter_context(nc.psum_tensor(name, list(shape), F32))

            xT = self.xT = sb("xT", [128, 8, NT], F32)
            ARENA = 24192
            self.arena = sb("arena", [128, ARENA], BF16)
            self.ring = [sb(f"ring{i}", [128, PIECE], BF16) for i in range(NRING)]
            self.ropeT = sb("ropeT", [128, 2, 2048], BF16)
            self.hT = sb("hT", [128, 8, 512], BF16)
            self.QT = sb("QT", [128, 8, 512], BF16)
            self.OT = sb("OT", [128, 8, 512], BF16)
            self.PT = sb("PT", [128, 3, 1024], BF16)
            self.NTF = 6
            self.tmpf = sb("tmpf", [128, self.NTF, 512], F32)
            self.tmpb = sb("tmpb", [128, 4, 512], BF16)
            pp = self.pp = sb("pp", [128, NPP], F32)
            self.modT = sb("modT", [128, 4, 48, 5], F32)
            cmat = self.cmat = sb("cmat", [128, 640], BF16)
            cT = sb("cT", [128, 8, 5], F32)
            scT = sb("scT", [128, 8, 5], BF16)
            self.cols = sb("cols", [128, 32], F32)
            self.lamc = sb("lamc", [128, 16], F32)

            self.ones128 = cmat[:, 0:128]
            self.blk64 = cmat[:, 128:256]
            self.ones96 = cmat[:, 256:384]
            self.perm_e = cmat[:, 384:512]
            self.perm_m = cmat[:, 512:640]

            g0 = ps("g0", [128, 512]); g1 = ps("g1", [128, 512])
            s0 = ps("s0", [128, 1024]); s1 = ps("s1", [128, 1024])
            o0 = ps("o0", [128, 512]); o1 = ps("o1", [128, 512])
            self.bank = {"g0": g0[:, :], "g1": g1[:, :], "o0": o0[:, :], "o1": o1[:, :],
                         "s0a": s0[:, 0:512], "s0b": s0[:, 512:1024],
                         "s1a": s1[:, 0:512], "s1b": s1[:, 512:1024]}
            self.s_tiles = [(s0, ["s0a", "s0b"]), (s1, ["s1a", "s1b"])]
            self.gen_small = Rot(["g0", "g1"])
            self.gen_big = Rot(["g0", "g1", "o0", "o1", "s0a", "s0b", "s1a", "s1b"])
            self.gen = self.gen_big
            self.tf = Rot(list(range(self.NTF)))
            self.tb = Rot(list(range(4)))
            self.ptr = Rot(list(range(3)))
            self.ring_i = 0
            self.arena_names = []

            def dma_simple(out_ap, in_ap, r, w, key):
                return P.add("sp", I("dma_start", out=out_ap, in_=in_ap), r=r, w=w, dma=key)

            dma_simple(pp[:, :], ppd[:, :], [], ["pp"], "c_pp")
            dma_simple(cT[:, :, :], cTd[:, :, :], [], ["cT"], "c_cT")

            self.prepass()

            dma_simple(cmat[:, :], wbf[pc["cst"], :, 0:640], [f"wbf{pc['cst']}"], ["cmat"], "c_cm")

            self.preamble(cT, scT)

            store_ops = []
            for s in range(nseq):
                if s > 0:
                    P.new_epoch()
                for b, (t0, n) in enumerate(TB):
                    dma_simple(xT[:, :, t0:t0 + n], xTd[s, :, :, t0:t0 + n], [], [f"x{b}"], f"xl{b}")
                for l in range(nlayers):
                    if l == 2:
                        P.new_epoch()
                    cc = l < DEPTH - 1
                    self.layer_cols(l, s)
                    if l % 2 == 0:
                        self.even_layer(l, s, cc)
                    else:
                        self.mla_layer(l, s, cc)
                    self.ffn(l, s, cc)
                for b in range(1, 5):
                    t0, n = TB[b]
                    op = dma_simple(outT[s, :, :, t0 - NCTX:t0 - NCTX + n], xT[:, :, t0:t0 + n],
                                    [f"x{b}"], [], f"xs{b}")
                    store_ops.append(op)
            P.emit(store_ops)
        return nc

    def set_arena(self, names):
        self.P.fence(self.arena_names, names)
        self.arena_names = list(names)

    def aview(self, off, shape):
        n = int(np.prod(shape))
        ap = self.arena[:, off:off + n]
        if len(shape) == 2:
            return ap.rearrange("p (a b) -> p a b", a=shape[0])
        return ap

    def tmpF(self):
        i = self.tf.next()
        return self.tmpf[:, i, :], f"tf{i}"

    def tmpB(self):
        i = self.tb.next()
        return self.tmpb[:, i, :], f"tb{i}"

    def load_piece(self, key):
        pid = self.pc[key]
        ln = self.pieces[pid][1]
        slot = self.ring_i % NRING
        self.ring_i += 1
        t = self.ring[slot]
        self.P.add("sp", I("dma_start", out=t[:, 0:ln], in_=self.wbf[pid, :, 0:ln]),
                   r=[f"wbf{pid}"], w=[f"ring{slot}"], dma=f"ring{slot}")
        return t, f"ring{slot}"

    def prepass(self):
        P = self.P
        xflat = self.xT[:, :, :].rearrange("p a b -> p (a b)")
        import os
        NST = int(os.environ.get('NST', '4'))
        cast_engs = Rot(["dve", "act", "pool"])
        stg_names = [f"stg{i}" for i in range(NST)]
        cb_names = [f"cb{i}" for i in range(NST)]
        for pid, (src, ln) in enumerate(self.pieces):
            sl = pid % NST
            stg = xflat[:, sl * PIECE: sl * PIECE + ln]
            cb = self.arena[:, sl * PIECE: sl * PIECE + ln]
            if len(src.shape) == 3:
                stg_v = stg.rearrange("p (k w) -> p k w", k=src.shape[1])
            else:
                stg_v = stg
            P.add("sp", I("dma_start", out=stg_v, in_=src), r=[], w=[stg_names[sl]], dma=f"stl{sl}")
            ce = cast_engs.next()
            if ce == "act":
                P.add("act", I("activation", cb, stg, AF.Copy), r=[stg_names[sl]], w=[cb_names[sl]])
            else:
                P.add(ce, I("tensor_copy", cb, stg), r=[stg_names[sl]], w=[cb_names[sl]])
            P.add("sp", I("dma_start", out=self.wbf[pid, :, 0:ln], in_=cb),
                  r=[cb_names[sl]], w=[f"wbf{pid}"], dma=f"sts{sl}")
        P.fence(stg_names, [f"x{b}" for b in range(5)])
        self.arena_names = cb_names

    def preamble(self, cT, scT):
        P = self.P
        pp, lamc, modT = self.pp, self.lamc, self.modT
        nlayers = self.nlayers
        X = mybir.AxisListType.X
        for i in range(2):
            l = 2 * i
            if l >= nlayers:
                continue
            li = lambda_init(l)
            base = PP_DLAM + i * 256
            tA, nA = self.tmpF()
            P.add("dve", [I("tensor_tensor", tA[:, 0:64], pp[:, base:base + 64], pp[:, base + 64:base + 128], ALU.mult),
                          I("tensor_tensor", tA[:, 64:128], pp[:, base + 128:base + 192], pp[:, base + 192:base + 256], ALU.mult)],
                  r=["pp"], w=[nA])
            P.add("dve", [I("reduce_sum", lamc[:, 8 + 2 * i:9 + 2 * i], tA[:, 0:64], X),
                          I("reduce_sum", lamc[:, 9 + 2 * i:10 + 2 * i], tA[:, 64:128], X)],
                  r=[nA], w=["lamc_s"])
            P.add("act", I("activation", lamc[:, 12 + 2 * i:14 + 2 * i], lamc[:, 8 + 2 * i:10 + 2 * i], AF.Exp),
                  r=["lamc_s"], w=["lamc_e"])
            P.add("dve", I("tensor_tensor", lamc[:, 4 + i:5 + i], lamc[:, 13 + 2 * i:14 + 2 * i],
                           lamc[:, 12 + 2 * i:13 + 2 * i], ALU.subtract), r=["lamc_e"], w=["lamc_d"])
            P.add("dve", I("tensor_scalar", lamc[:, i:i + 1], lamc[:, 4 + i:5 + i], -li, None, ALU.add),
                  r=["lamc_d"], w=["lamc"])
            P.add("dve", I("tensor_scalar", lamc[:, 2 + i:3 + i], pp[:, PP_SUBLN + i:PP_SUBLN + i + 1], 1.0 - li, None, ALU.mult),
                  r=["pp"], w=["lamc"])
        P.add("act", I("activation", scT[:, :, :], cT[:, :, :], AF.Silu), r=["cT"], w=["scT"])
        for l in range(nlayers):
            bank = self.gen.next()
            bk = self.bank[bank]
            for g in range(12):
                t, rn = self.load_piece(("ada", l, g))
                tv = t[:, 0:4096].rearrange("p (k w) -> p k w", k=8)
                ins = []
                for c in range(4):
                    j = g * 4 + c
                    for k in range(8):
                        ins.append(I("matmul", bk[:, j * 5:j * 5 + 5], tv[:, k, c * 128:(c + 1) * 128],
                                     scT[:, k, :], start=(k == 0), stop=(k == 7)))
                P.add("pe", ins, r=[rn, "scT"], w=[bank])
            ins = []
            for j in range(48):
                ins.append(I("tensor_scalar", modT[:, l, j, :], bk[:, j * 5:j * 5 + 5],
                             pp[:, PP_ADAB + l * 48 + j:PP_ADAB + l * 48 + j + 1], None, ALU.add))
            P.add("dve", ins, r=[bank, "pp"], w=["modT"])

    def layer_cols(self, l, s):
        pp, modT, cols = self.pp, self.modT, self.cols
        specs = [(0, 8, s, PP_NMIX), (8, 8, 4, PP_NMIX), (16, 32, s, PP_NMLP), (24, 32, 4, PP_NMLP)]
        ins = []
        for (co, j0, col, nb) in specs:
            ins.append(I("scalar_tensor_tensor", cols[:, co:co + 8], modT[:, l, j0:j0 + 8, col], 1.0,
                         pp[:, nb + l * 8: nb + l * 8 + 8], ALU.add, ALU.mult))
        self.P.add("dve", ins, r=["modT", "pp"], w=["cols"])

    def modcol(self, l, m, k, col):
        return self.modT[:, l, m * 8 + k, col:col + 1]

    def hbuf(self, alt):
        if alt:
            return self.QT, [f"QT{c}" for c in range(8)]
        return self.hT, ["hT"]

    def norm_mod(self, l, s, b, which, alt=False):
        P = self.P
        t0, n = TB[b]
        isctx = (b == 0)
        col = 4 if isctx else s
        a_off = (0 if which == 1 else 16) + (8 if isctx else 0)
        m_shift = 0 if which == 1 else 3
        xT = self.xT
        hT, hres = self.hbuf(alt)
        xr = f"x{b}"
        P.add("act", I("activation", hT[:, :, 0:n], xT[:, :, t0:t0 + n], AF.Square), r=[xr], w=hres)
        bank = self.gen.next()
        bk = self.bank[bank]
        P.add("pe", [I("matmul", bk[:, 0:n], self.ones128, hT[:, k, 0:n], start=(k == 0), stop=(k == 7))
                     for k in range(8)], r=hres + ["cmat"], w=[bank])
        rt, rtn = self.tmpF()
        P.add("act", I("activation", rt[:, 0:n], bk[:, 0:n], AF.Ln, bias=EPS, scale=1.0 / D), r=[bank], w=[rtn])
        P.add("act", I("activation", rt[:, 0:n], rt[:, 0:n], AF.Exp, scale=-0.5), r=[rtn], w=[rtn])
        tks = [self.tmpF(), self.tmpF()]
        for k in range(8):
            tk, tkn = tks[k % 2]
            P.add("dve", I("scalar_tensor_tensor", tk[:, 0:n], xT[:, k, t0:t0 + n],
                           self.cols[:, a_off + k:a_off + k + 1], rt[:, 0:n], ALU.mult, ALU.mult),
                  r=[xr, "cols", rtn], w=[tkn])
            P.add("act", I("activation", hT[:, k, 0:n], tk[:, 0:n], AF.Identity,
                           bias=self.modcol(l, m_shift, k, col), scale=1.0), r=[tkn, "modT"], w=hres)

    def proj(self, wt, wr, nk, W, c0, M, rhs_list, rhs_res, n):
        bank = self.gen.next()
        bk = self.bank[bank]
        tv = wt[:, 0:nk * W].rearrange("p (k w) -> p k w", k=nk)
        self.P.add("pe", [I("matmul", bk[0:M, 0:n], tv[:, k, c0:c0 + M], rhs_list[k], start=(k == 0), stop=(k == nk - 1))
                          for k in range(nk)], r=[wr] + list(rhs_res), w=[bank])
        return bank

    def qknorm_rope(self, bank, R, n, gain, ones_mat, dim, rope, out_ap, out_res, extra_rows=None, mode="P"):
        P = self.P
        bk = self.bank[bank]
        qg, qgn = self.tmpF()
        Rp = R if extra_rows is None else extra_rows[0]
        if mode == "A":
            P.add("act", I("activation", qg[0:Rp, 0:n], bk[0:Rp, 0:n], AF.Identity, scale=gain[0:Rp, :]),
                  r=[bank, "pp"], w=[qgn])
        else:
            P.add("dve", I("tensor_scalar", qg[0:Rp, 0:n], bk[0:Rp, 0:n], gain[0:Rp, :], None, ALU.mult),
                  r=[bank, "pp"], w=[qgn])
        if extra_rows is not None:
            r0, r1, src, srcres = extra_rows
            P.add("pool", I("tensor_copy", qg[r0:r1, 0:n], src), r=[srcres], w=[qgn])
        sq, sqn = self.tmpB()
        if mode == "A":
            P.add("act", I("activation", sq[0:R, 0:n], qg[0:R, 0:n], AF.Square), r=[qgn], w=[sqn])
        else:
            P.add("pool", I("tensor_tensor", sq[0:R, 0:n], qg[0:R, 0:n], qg[0:R, 0:n], ALU.mult), r=[qgn], w=[sqn])
        b2 = self.gen.next()
        bk2 = self.bank[b2]
        P.add("pe", I("matmul", bk2[0:R, 0:n], ones_mat[0:R, 0:R], sq[0:R, 0:n], start=True, stop=True),
              r=[sqn, "cmat"], w=[b2])
        rt, rtn = self.tmpF()
        P.add("act", I("activation", rt[0:R, 0:n], bk2[0:R, 0:n], AF.Ln, bias=EPS, scale=1.0 / dim), r=[b2], w=[rtn])
        P.add("act", I("activation", rt[0:R, 0:n], rt[0:R, 0:n], AF.Exp, scale=-0.5), r=[rtn], w=[rtn])
        if rope is not None:
            perm, r0, r1, tok0 = rope
            qb, qbn = self.tmpB()
            P.add("dve" if mode == "A" else "pool", I("tensor_copy", qb[0:R, 0:n], qg[0:R, 0:n]), r=[qgn], w=[qbn])
            b3 = self.gen.next()
            bk3 = self.bank[b3]
            P.add("pe", I("matmul", bk3[0:R, 0:n], perm[0:R, 0:R], qb[0:R, 0:n], start=True, stop=True),
                  r=[qbn, "cmat"], w=[b3])
            cosT = self.ropeT[r0:r1, 0, tok0:tok0 + n]
            sinT = self.ropeT[r0:r1, 1, tok0:tok0 + n]
            t2, t2n = self.tmpF()
            P.add("dve", I("tensor_tensor", t2[r0:r1, 0:n], bk3[r0:r1, 0:n], sinT, ALU.mult), r=[b3, "rope"], w=[t2n])
            P.add("pool", I("tensor_tensor", qg[r0:r1, 0:n], qg[r0:r1, 0:n], cosT, ALU.mult), r=[qgn, "rope"], w=[qgn])
            P.add("pool", I("tensor_tensor", qg[r0:r1, 0:n], qg[r0:r1, 0:n], t2[r0:r1, 0:n], ALU.add),
                  r=[qgn, t2n], w=[qgn])
        P.add("dve", I("tensor_tensor", out_ap, qg[0:R, 0:n], rt[0:R, 0:n], ALU.mult), r=[qgn, rtn], w=[out_res])

    def load_rope(self, which):
        pid = self.pc[which]
        self.P.add("sp", I("dma_start", out=self.ropeT[:, :, :].rearrange("p a b -> p (a b)"), in_=self.wbf[pid, :, :]),
                   r=[f"wbf{pid}"], w=["rope"], dma="rope")

    def attention(self, maps, n, nchunks, scale):
        P = self.P
        npairs = nchunks // 2
        units = []
        for mi, m in enumerate(maps):
            for p in range(npairs):
                units.append((mi, p))
        opool = Rot(["o0", "o1"])
        state = {}

        def rec_S(u):
            mi, p = units[u]
            m = maps[mi]
            st, snames = self.s_tiles[u % 2]
            ins = []
            for h in range(2):
                ch = 2 * p + h
                ins.append(I("matmul", st[:, h * 512:h * 512 + n], m["kt"](ch), m["q"], start=True, stop=True))
            P.add("pe", ins, r=[m["kres"](2 * p), m["kres"](2 * p + 1), m["qres"]], w=snames)

        def rec_exp(u):
            st, snames = self.s_tiles[u % 2]
            pi = self.ptr.next()
            state[u] = pi
            src = st[:, :].rearrange("p (h c) -> p h c", h=2)[:, :, 0:n]
            dst = self.PT[:, pi, :].rearrange("p (h c) -> p h c", h=2)[:, :, 0:n]
            P.add("act", I("activation", dst, src, AF.Exp, scale=scale), r=snames, w=[f"pt{pi}"])

        def rec_PV(u):
            mi, p = units[u]
            m = maps[mi]
            pi = state[u]
            if p == 0:
                m["obanks"] = [opool.next() for _ in m["units"]]
            obanks = m["obanks"]
            ins = []
            for h in range(2):
                ch = 2 * p + h
                for ui, (lf, _) in enumerate(m["units"]):
                    ins.append(I("matmul", self.bank[obanks[ui]][:, 0:n], lf(ch), self.PT[:, pi, h * 512:h * 512 + n],
                                 start=(ch == 0), stop=(ch == nchunks - 1)))
            rr = [f"pt{pi}"]
            for (_, vf) in m["units"]:
                rr += [vf(2 * p), vf(2 * p + 1)]
            P.add("pe", ins, r=rr, w=list(obanks))
            if p == npairs - 1:
                m["done"](obanks)
                if m.get("after") is not None:
                    m["after"]()

        nu = len(units)
        PVD = int(os.environ.get("PVD", "1"))
        rec_S(0)
        for u in range(nu):
            if u + 1 < nu:
                rec_S(u + 1)
            rec_exp(u)
            if u - PVD >= 0:
                rec_PV(u - PVD)
        for u in range(max(0, nu - PVD), nu):
            rec_PV(u)

    def o_norm(self, obank, vrows, srows, n, out_ap, out_res, scratch=None):
        P = self.P
        bk = self.bank[obank]
        v0, v1 = vrows
        if scratch is None:
            t, tn = self.tmpF()
            tap = t[v0:v1, 0:n]
        else:
            tap, tn = scratch
        P.add("dve", I("reciprocal", tap, bk[srows[0]:srows[1], 0:n]), r=[obank], w=[tn])
        P.add("dve", I("tensor_tensor", out_ap, bk[v0:v1, 0:n], tap, ALU.mult), r=[obank, tn], w=[out_res])

    @staticmethod
    def blk_of_chunk(ch):
        t = ch * 128
        for b, (t0, n) in enumerate(TB):
            if t0 <= t < t0 + n:
                return b

    def unit_AP(self, first_off, second_off):
        ARW = self.arena.shape[1]
        return bass.AP(self.arena, first_off, [[ARW, 128], [second_off - first_off, 2], [1, 64]])

    def even_layer(self, l, s, cc):
        P = self.P
        i = l // 2
        pp = self.pp
        KT = self.aview(0, [5, NT])
        VOFF = 5 * NT
        VCH = 704
        Vt = self.arena
        V3 = self.arena[:, VOFF:VOFF + 18 * VCH].rearrange("p (c w) -> p c w", c=18)
        ktn = [f"KT{b}" for b in range(5)]
        vn = [f"V{b}" for b in range(5)]
        self.set_arena(ktn + vn)
        self.load_rope("rope_e")
        P.add("pool", I("memset", V3[:, :, 576:640], 1.0), r=[], w=vn)
        g_aq = pp[:, PP_GE + i * 4 + 0:PP_GE + i * 4 + 1]
        g_ak = pp[:, PP_GE + i * 4 + 1:PP_GE + i * 4 + 2]
        g_bq = pp[:, PP_GE + i * 4 + 2:PP_GE + i * 4 + 3]
        g_bk = pp[:, PP_GE + i * 4 + 3:PP_GE + i * 4 + 4]
        boc = self.blk_of_chunk

        self.gen = self.gen_big
        self.norm_mod(l, s, 0, 1, False)
        for b, (t0, n) in enumerate(TB):
            alt = (b % 2 == 1)
            if b + 1 < 5:
                self.norm_mod(l, s, b + 1, 1, not alt)
            hTb, hres = self.hbuf(alt)
            rope = None if b == 0 else (self.perm_e, 0, 128, t0 - NCTX)
            hrhs = [hTb[:, k, 0:n] for k in range(8)]
            wt, wr = self.load_piece(("E0", l))
            for c in range(4):
                bank = self.proj(wt, wr, 8, 512, c * 128, 128, hrhs, hres, n)
                self.qknorm_rope(bank, 128, n, g_ak, self.blk64, 64, rope, KT[:, c, t0:t0 + n], ktn[b], mode="A")
            wt1, wr1 = self.load_piece(("E1", l))
            bank = self.proj(wt1, wr1, 8, 256, 0, 128, hrhs, hres, n)
            self.qknorm_rope(bank, 128, n, g_bk, self.blk64, 64, rope, KT[:, 4, t0:t0 + n], ktn[b], mode="A")
            wt2, wr2 = self.load_piece(("E2", l))
            tv1 = wt1[:, 0:2048].rearrange("p (k w) -> p k w", k=8)
            tv2 = wt2[:, 0:4096].rearrange("p (k w) -> p k w", k=8)
            for u in range(n // 128):
                ch = (t0 + u * 128) // 128
                vbase = VOFF + ch * VCH
                bA = self.gen.next()
                bB = self.gen.next()
                ins = []
                for k in range(8):
                    ins.append(I("matmul", self.bank[bA][:, 0:512], hTb[:, k, u * 128:(u + 1) * 128], tv2[:, k, :],
                                 start=(k == 0), stop=(k == 7)))
                for k in range(8):
                    ins.append(I("matmul", self.bank[bB][:, 0:128], hTb[:, k, u * 128:(u + 1) * 128], tv1[:, k, 128:256],
                                 start=(k == 0), stop=(k == 7)))
                P.add("pe", ins, r=[wr1, wr2] + hres, w=[bA, bB])
                P.add("dve", I("tensor_copy", Vt[:, vbase:vbase + 512], self.bank[bA][:, 0:512]), r=[bA], w=[vn[b]])
                P.add("act", I("activation", self.unit_AP(vbase + 512, vbase + 640),
                               self.bank[bB][:, 0:128].rearrange("p (h c) -> p h c", h=2), AF.Copy),
                      r=[bB], w=[vn[b]])

        self.gen = self.gen_small
        neglam = self.lamc[:, i:i + 1]
        sublnp = self.lamc[:, 2 + i:3 + i]
        blocks = [b for b in range(5) if not (b == 0 and not cc)]
        qstate = {}

        def qbuild_chunk(b, cc_):
            t0, n = TB[b]
            rope = None if b == 0 else (self.perm_e, 0, 128, t0 - NCTX)
            hrhs = [self.hT[:, k, 0:n] for k in range(8)]
            if cc_ % 4 == 0:
                qstate["w"] = self.load_piece(("E3" if cc_ < 4 else "E4", l))
            wt, wr = qstate["w"]
            gq = g_aq if cc_ < 4 else g_bq
            bank = self.proj(wt, wr, 8, 512, (cc_ % 4) * 128, 128, hrhs, ["hT"], n)
            self.qknorm_rope(bank, 128, n, gq, self.blk64, 64, rope, self.QT[:, cc_, 0:n], f"QT{cc_}")

        self.norm_mod(l, s, blocks[0], 1)
        for c_ in range(8):
            qbuild_chunk(blocks[0], c_)
        for bi, b in enumerate(blocks):
            t0, n = TB[b]
            nchunks = 2 if b == 0 else 18
            nxt = blocks[bi + 1] if bi + 1 < len(blocks) else None
            if nxt is not None:
                self.norm_mod(l, s, nxt, 1)
            maps = []
            dtmp = {}
            for h in range(4):
                for m in range(2):
                    rows = (64 * m, 64 * m + 64)

                    def kt(ch, h=h, rows=rows):
                        return KT[rows[0]:rows[1], h, ch * 128:(ch + 1) * 128]

                    def uA(ch, h=h):
                        return V3[:, ch, h * 128:(h + 1) * 128]

                    def uB(ch):
                        return self.ones128

                    def done(obanks, h=h, m=m, n=n):
                        d, dn = self.tmpF()
                        dtmp[(h, m)] = (d, dn)
                        P.add("act", I("activation", d[:, 0:n], self.bank[obanks[1]][:, 0:n], AF.Ln), r=[obanks[1]], w=[dn])
                        P.add("act", I("activation", d[:, 0:n], d[:, 0:n], AF.Exp, scale=-1.0), r=[dn], w=[dn])
                        P.add("dve", I("tensor_tensor", d[:, 0:n], self.bank[obanks[0]][:, 0:n], d[:, 0:n], ALU.mult),
                              r=[obanks[0], dn], w=[dn])
                        if m == 1:
                            d0, d0n = dtmp[(h, 0)]
                            P.add("dve", I("scalar_tensor_tensor", d0[:, 0:n], d[:, 0:n], neglam, d0[:, 0:n], ALU.mult, ALU.add),
                                  r=[dn, d0n, "lamc"], w=[d0n])
                            sq, sqn = self.tmpB()
                            P.add("act", I("activation", sq[:, 0:n], d0[:, 0:n], AF.Square), r=[d0n], w=[sqn])
                            b2 = self.gen.next()
                            P.add("pe", I("matmul", self.bank[b2][:, 0:n], self.ones128, sq[:, 0:n], start=True, stop=True),
                                  r=[sqn, "cmat"], w=[b2])
                            rt, rtn = self.tmpF()
                            P.add("act", I("activation", rt[:, 0:n], self.bank[b2][:, 0:n], AF.Ln, bias=EPS, scale=1.0 / 128),
                                  r=[b2], w=[rtn])
                            P.add("act", I("activation", rt[:, 0:n], rt[:, 0:n], AF.Exp, scale=-0.5), r=[rtn], w=[rtn])
                            P.add("dve", I("scalar_tensor_tensor", self.OT[:, h, 0:n], d0[:, 0:n], sublnp, rt[:, 0:n],
                                           ALU.mult, ALU.mult), r=[d0n, rtn, "lamc"], w=[f"OT{h}"])
                    after = None
                    if m == 1 and nxt is not None:
                        after = (lambda nxt=nxt, h=h: qbuild_chunk(nxt, h))
                    maps.append(dict(kt=kt, kres=lambda ch: ktn[boc(ch)],
                                     q=self.QT[rows[0]:rows[1], h, 0:n], qres=f"QT{h}",
                                     units=[(uA, lambda ch: vn[boc(ch)]), (uB, lambda ch: "cmat")],
                                     done=done, after=after))
            self.attention(maps, n, nchunks, 64 ** -0.5)
            maps = []
            for g in range(4):
                for kv in range(2):
                    rows = (64 * kv, 64 * kv + 64)

                    def kt(ch, rows=rows):
                        return KT[rows[0]:rows[1], 4, ch * 128:(ch + 1) * 128]
                    if kv == 0:
                        def uf(ch):
                            return V3[:, ch, 512:640]
                        vrows, srows = (0, 64), (64, 128)
                    else:
                        def uf(ch):
                            return V3[:, ch, 576:704]
                        vrows, srows = (64, 128), (0, 64)

                    def done(obanks, g=g, vrows=vrows, srows=srows, n=n):
                        self.o_norm(obanks[0], vrows, srows, n, self.OT[vrows[0]:vrows[1], 4 + g, 0:n], f"OT{4 + g}")
                    after = None
                    if kv == 1 and nxt is not None:
                        after = (lambda nxt=nxt, g=g: qbuild_chunk(nxt, 4 + g))
                    maps.append(dict(kt=kt, kres=lambda ch: ktn[boc(ch)],
                                     q=self.QT[rows[0]:rows[1], 4 + g, 0:n], qres=f"QT{4 + g}",
                                     units=[(uf, lambda ch: vn[boc(ch)])], done=done, after=after))
            self.attention(maps, n, nchunks, 64 ** -0.5)
            col = 4 if b == 0 else s
            orhs = [self.OT[:, k, 0:n] for k in range(8)]
            ores = [f"OT{k}" for k in range(8)]
            for half, pk in enumerate(["E5", "E6"]):
                wt, wr = self.load_piece((pk, l))
                for c in range(4):
                    o = half * 4 + c
                    bank = self.proj(wt, wr, 8, 512, c * 128, 128, orhs, ores, n)
                    P.add("dve", I("scalar_tensor_tensor", self.xT[:, o, t0:t0 + n], self.bank[bank][:, 0:n],
                                   self.modcol(l, 2, o, col), self.xT[:, o, t0:t0 + n], ALU.mult, ALU.add),
                          r=[bank, "modT", f"x{b}"], w=[f"x{b}"])

    def mla_layer(self, l, s, cc):
        import os
        if int(os.environ.get("MLA_STOP", "99")) <= 0:
            return
        P = self.P
        i = l // 2
        pp = self.pp
        QCN = self.aview(0, [4, NT])
        KVCN = self.aview(4 * NT, [2, NT])
        KRG = self.arena[:, 6 * NT:7 * NT]
        KTm = self.aview(7 * NT, [2, NT])
        VOFF = 9 * NT
        VCH = 192
        V3 = self.arena[:, VOFF:VOFF + 18 * VCH].rearrange("p (c w) -> p c w", c=18)
        qn_ = [f"QCN{b}" for b in range(5)]
        kvn_ = [f"KVCN{b}" for b in range(5)]
        krn_ = [f"KRG{b}" for b in range(5)]
        ktn = [f"KTm{b}" for b in range(5)]
        vn = [f"Vm{b}" for b in range(5)]
        self.set_arena(qn_ + kvn_ + krn_ + ktn + vn)
        self.load_rope("rope_m")
        Vt = self.arena
        P.add("pool", I("memset", V3[:, :, 64:128], 1.0), r=[], w=vn)
        g_q = pp[:, PP_MQK + i * 2:PP_MQK + i * 2 + 1]
        g_k = pp[:, PP_MQK + i * 2 + 1:PP_MQK + i * 2 + 2]
        boc = self.blk_of_chunk

        self.gen = self.gen_big
        self.norm_mod(l, s, 0, 1, False)
        for b, (t0, n) in enumerate(TB):
            alt = (b % 2 == 1)
            if b + 1 < 5:
                self.norm_mod(l, s, b + 1, 1, not alt)
            hTb, hres = self.hbuf(alt)
            hrhs = [hTb[:, k, 0:n] for k in range(8)]
            for (pk, W, nch, dst, dres, gbase, dim) in [("M0", 512, 4, QCN, qn_[b], PP_MQ + i * 4, 512),
                                                        ("M1", 352, 2, KVCN, kvn_[b], PP_MKV + i * 2, 256)]:
                wt, wr = self.load_piece((pk, l))
                b2 = self.gen.next()
                for c in range(nch):
                    bank = self.proj(wt, wr, 8, W, c * 128, 128, hrhs, hres, n)
                    bk = self.bank[bank]
                    P.add("act", I("activation", dst[:, c, t0:t0 + n], bk[:, 0:n], AF.Copy), r=[bank], w=[dres])
                    sq, sqn = self.tmpB()
                    P.add("act", I("activation", sq[:, 0:n], bk[:, 0:n], AF.Square), r=[bank], w=[sqn])
                    P.add("pe", I("matmul", self.bank[b2][:, 0:n], self.ones128, sq[:, 0:n],
                                  start=(c == 0), stop=(c == nch - 1)), r=[sqn, "cmat"], w=[b2])
                rt, rtn = self.tmpF()
                P.add("act", I("activation", rt[:, 0:n], self.bank[b2][:, 0:n], AF.Ln, bias=EPS, scale=1.0 / dim),
                      r=[b2], w=[rtn])
                P.add("act", I("activation", rt[:, 0:n], rt[:, 0:n], AF.Exp, scale=-0.5), r=[rtn], w=[rtn])
                for c in range(nch):
                    P.add("dve", I("scalar_tensor_tensor", dst[:, c, t0:t0 + n], dst[:, c, t0:t0 + n],
                                   pp[:, gbase + c:gbase + c + 1], rt[:, 0:n], ALU.mult, ALU.mult),
                          r=[dres, rtn, "pp"], w=[dres])
                if pk == "M1":
                    bank = self.proj(wt, wr, 8, W, 256, 96, hrhs, hres, n)
                    bk = self.bank[bank]
                    qg, qgn = self.tmpF()
                    P.add("act", I("activation", qg[0:96, 0:n], bk[0:96, 0:n], AF.Identity, scale=g_k[0:96, :]),
                          r=[bank, "pp"], w=[qgn])
                    if b > 0:
                        tok0 = t0 - NCTX
                        qb, qbn = self.tmpB()
                        P.add("act", I("activation", qb[0:96, 0:n], qg[0:96, 0:n], AF.Copy), r=[qgn], w=[qbn])
                        b3 = self.gen.next()
                        P.add("pe", I("matmul", self.bank[b3][0:96, 0:n], self.perm_m[0:96, 0:96], qb[0:96, 0:n],
                                      start=True, stop=True), r=[qbn, "cmat"], w=[b3])
                        t2, t2n = self.tmpF()
                        P.add("dve", I("tensor_tensor", t2[64:96, 0:n], self.bank[b3][64:96, 0:n],
                                       self.ropeT[64:96, 1, tok0:tok0 + n], ALU.mult), r=[b3, "rope"], w=[t2n])
                        P.add("pool", I("tensor_tensor", qg[64:96, 0:n], qg[64:96, 0:n],
                                        self.ropeT[64:96, 0, tok0:tok0 + n], ALU.mult), r=[qgn, "rope"], w=[qgn])
                        P.add("pool", I("tensor_tensor", qg[64:96, 0:n], qg[64:96, 0:n], t2[64:96, 0:n], ALU.add),
                              r=[qgn, t2n], w=[qgn])
                    P.add("dve", I("tensor_copy", KRG[64:96, t0:t0 + n], qg[64:96, 0:n]), r=[qgn], w=[krn_[b]])

        import os
        STOP = int(os.environ.get("MLA_STOP", "99"))
        if STOP <= 1:
            return
        KVMODE = os.environ.get("KVMODE", "P")
        for hg in range(8):
            if STOP <= 5 and hg > 0:
                return
            self.gen = self.gen_big
            for b, (t0, n) in enumerate(TB):
                wt, wr = self.load_piece(("M2", l, hg))
                tv = wt[:, 0:512].rearrange("p (k w) -> p k w", k=2)
                kvrhs = [KVCN[:, k, t0:t0 + n] for k in range(2)]
                for hh in range(2):
                    bank = self.proj(wt, wr, 2, 256, hh * 128, 64, kvrhs, [kvn_[b]], n)
                    self.qknorm_rope(bank, 96, n, g_k, self.ones96, 96, None, KTm[0:96, hh, t0:t0 + n], ktn[b],
                                     extra_rows=(64, 96, KRG[64:96, t0:t0 + n], krn_[b]), mode=KVMODE)
                for u in range(n // 128):
                    ch = (t0 + u * 128) // 128
                    vbase = VOFF + ch * VCH
                    bA = self.gen.next()
                    ins = []
                    for k in range(2):
                        rv = tv[:, k, :].rearrange("p (h c) -> p h c", h=2)[:, :, 64:128]
                        ins.append(I("matmul", self.bank[bA][:, 0:128].rearrange("p (h c) -> p h c", h=2),
                                     KVCN[:, k, t0 + u * 128:t0 + (u + 1) * 128], rv, start=(k == 0), stop=(k == 1)))
                    P.add("pe", ins, r=[wr, kvn_[b]], w=[bA])
                    P.add("dve", I("tensor_copy", self.unit_AP(vbase, vbase + 128),
                                   self.bank[bA][:, 0:128].rearrange("p (h c) -> p h c", h=2)),
                          r=[bA], w=[vn[b]])
            self.gen = self.gen_small
            blocks = [b for b in range(5) if not (b == 0 and not cc)]

            def qbuild(b, slot):
                t0, n = TB[b]
                rope = None if b == 0 else (self.perm_m, 64, 96, t0 - NCTX)
                wt, wr = self.load_piece(("M3", l, hg))
                qrhs = [QCN[:, k, t0:t0 + n] for k in range(4)]
                for hh in range(2):
                    bank = self.proj(wt, wr, 4, 192, hh * 96, 96, qrhs, [qn_[b]], n)
                    self.qknorm_rope(bank, 96, n, g_q, self.ones96, 96, rope,
                                     self.QT[0:96, slot * 2 + hh, 0:n], f"QT{slot * 2 + hh}")

            qbuild(blocks[0], 0)
            for bi, b in enumerate(blocks):
                t0, n = TB[b]
                nchunks = 2 if b == 0 else 18
                slot = bi % 2
                if bi + 1 < len(blocks):
                    qbuild(blocks[bi + 1], (bi + 1) % 2)
                odd = (hg % 2 == 1)
                oslot = 0 if odd else 1 + b
                maps = []
                for hh in range(2):
                    def kt(ch, hh=hh):
                        return KTm[0:96, hh, ch * 128:(ch + 1) * 128]
                    if hh == 0:
                        def uf(ch):
                            return V3[:, ch, 0:128]
                        vrows, srows = (0, 64), (64, 128)
                    else:
                        def uf(ch):
                            return V3[:, ch, 64:192]
                        vrows, srows = (64, 128), (0, 64)

                    def done(obanks, vrows=vrows, srows=srows, n=n, oslot=oslot):
                        self.o_norm(obanks[0], vrows, srows, n, self.OT[vrows[0]:vrows[1], oslot, 0:n], f"OT{oslot}")
                    maps.append(dict(kt=kt, kres=lambda ch: ktn[boc(ch)],
                                     q=self.QT[0:96, slot * 2 + hh, 0:n], qres=f"QT{slot * 2 + hh}",
                                     units=[(uf, lambda ch: vn[boc(ch)])], done=done))
                self.attention(maps, n, nchunks, 96 ** -0.5)
                if not odd:
                    continue
                col = 4 if b == 0 else s
                wt, wr = self.load_piece(("M5", l, hg // 2))
                tv5 = wt[:, 0:2048].rearrange("p (k w) -> p k w", k=2)
                for o in range(8):
                    bank = self.gen.next()
                    P.add("pe", [I("matmul", self.bank[bank][:, 0:n], tv5[:, 0, o * 128:(o + 1) * 128], self.OT[:, 1 + b, 0:n],
                                   start=True, stop=False),
                                 I("matmul", self.bank[bank][:, 0:n], tv5[:, 1, o * 128:(o + 1) * 128], self.OT[:, 0, 0:n],
                                   start=False, stop=True)],
                          r=[wr, "OT0", f"OT{1 + b}"], w=[bank])
                    P.add("dve", I("scalar_tensor_tensor", self.xT[:, o, t0:t0 + n], self.bank[bank][:, 0:n],
                                   self.modcol(l, 2, o, col), self.xT[:, o, t0:t0 + n], ALU.mult, ALU.add),
                          r=[bank, "modT", f"x{b}"], w=[f"x{b}"])

    def ffn(self, l, s, cc):
        P = self.P
        HID = self.aview(0, [32, 512])
        hn = [f"hid{j}" for j in range(32)]
        self.set_arena(hn)
        self.gen = self.gen_big
        fblocks = [b for b in range(5) if not (b == 0 and not cc)]
        self.norm_mod(l, s, fblocks[0], 2, fblocks[0] % 2 == 1)
        for bi, b in enumerate(fblocks):
            t0, n = TB[b]
            col = 4 if b == 0 else s
            alt = (b % 2 == 1)
            if bi + 1 < len(fblocks):
                self.norm_mod(l, s, fblocks[bi + 1], 2, not alt)
            hTb, hres = self.hbuf(alt)
            hrhs = [hTb[:, k, 0:n] for k in range(8)]
            for g in range(8):
                wt, wr = self.load_piece(("W1", l, g))
                for c in range(4):
                    j = g * 4 + c
                    bank = self.proj(wt, wr, 8, 512, c * 128, 128, hrhs, hres, n)
                    t, tn = self.tmpF()
                    P.add("act", I("activation", t[:, 0:n], self.bank[bank][:, 0:n], AF.Relu), r=[bank], w=[tn])
                    P.add("pool", I("tensor_tensor", HID[:, j, 0:n], t[:, 0:n], t[:, 0:n], ALU.mult), r=[tn], w=[hn[j]])
            hidrhs = [HID[:, k, 0:n] for k in range(32)]
            for o in range(8):
                wt, wr = self.load_piece(("W2", l, o))
                bank = self.proj(wt, wr, 32, 128, 0, 128, hidrhs, hn, n)
                P.add("dve", I("scalar_tensor_tensor", self.xT[:, o, t0:t0 + n], self.bank[bank][:, 0:n],
                               self.modcol(l, 5, o, col), self.xT[:, o, t0:t0 + n], ALU.mult, ALU.add),
                      r=[bank, "modT", f"x{b}"], w=[f"x{b}"])


def _rope_tables():
    tabs = np.zeros((2, 128, 2, NLAT), np.float32)
    rows = (np.arange(NLAT) // 64).astype(np.float32)
    colsp = (np.arange(NLAT) % 64).astype(np.float32)
    for which, (rot_dim, p0, p1, per) in enumerate([(64, 0, 128, 64), (32, 64, 96, 32)]):
        n_freq = rot_dim // 4
        inv = (10000.0 ** (-np.arange(n_freq, dtype=np.float32) / n_freq)).astype(np.float32)
        ang = np.concatenate([rows[:, None] * inv[None, :], colsp[:, None] * inv[None, :]], axis=1)
        half = rot_dim // 2
        for p in range(p0, p1):
            d = (p - p0) % per
            j = d % half
            sgn = -1.0 if d < half else 1.0
            tabs[which, p, 0, :] = np.cos(ang[:, j])
            tabs[which, p, 1, :] = sgn * np.sin(ang[:, j])
    return tabs.reshape(2, 128, 2 * NLAT)


def _const_mats():
    cm = np.zeros((128, 640), np.float32)
    cm[:, 0:128] = 1.0
    cm[0:64, 128:192] = 1.0
    cm[64:128, 192:256] = 1.0
    cm[0:96, 256:352] = 1.0
    for m in range(128):
        d = m % 64
        k = m + 32 if d < 32 else m - 32
        cm[k, 384 + m] = 1.0
    for m in range(64, 96):
        k = m + 16 if m < 80 else m - 16
        cm[k, 512 + m] = 1.0
    return cm


def _cols(v):
    return np.ascontiguousarray(v.reshape(-1, 128).T)


def _prep_shared(inp, nlayers):
    f = np.float32
    ab_w_in = np.asarray(inp["ab_w_in"], f)
    ab_w_out = np.asarray(inp["ab_w_out"], f)
    bq_perm = np.concatenate([np.arange(1536 + (kv * 4 + g) * 64, 1536 + (kv * 4 + g) * 64 + 64)
                              for g in range(4) for kv in range(2)])
    perm = np.concatenate([np.arange(512, 1024), np.arange(2048, 2176), np.arange(2176, 2304),
                           np.arange(1024, 1536), np.arange(0, 512), bq_perm])
    w_in_e = np.ascontiguousarray(ab_w_in[:, :, perm])
    rperm = np.concatenate([np.arange(0, 512)] +
                           [np.arange(512 + (kv * 4 + g) * 64, 512 + (kv * 4 + g) * 64 + 64)
                            for g in range(4) for kv in range(2)])
    w_out_e = np.ascontiguousarray(ab_w_out[:, rperm, :])
    mla_w_in = np.asarray(inp["mla_w_in"], f)
    w_in_m = np.zeros((2, 1024, 864), f)
    w_in_m[:, :, 0:768] = mla_w_in[:, :, 0:768]
    w_in_m[:, :, 832:864] = mla_w_in[:, :, 768:800]
    pp = np.zeros((128, NPP), f)
    nmix = np.asarray(inp["norm_mix"], f)
    nmlp = np.asarray(inp["norm_mlp"], f)
    adab = np.asarray(inp["ada_b"], f)
    for l in range(4):
        pp[:, PP_NMIX + l * 8:PP_NMIX + l * 8 + 8] = _cols(nmix[l])
        pp[:, PP_NMLP + l * 8:PP_NMLP + l * 8 + 8] = _cols(nmlp[l])
        pp[:, PP_ADAB + l * 48:PP_ADAB + l * 48 + 48] = _cols(adab[l])
    dqk = np.asarray(inp["diff_qk_norm"], f)
    gqk = np.asarray(inp["gqa_qk_norm"], f)
    subln = np.asarray(inp["diff_subln"], f)
    dlam = np.asarray(inp["diff_lambda"], f)
    mq = np.asarray(inp["mla_q_norm"], f)
    mkv = np.asarray(inp["mla_kv_norm"], f)
    mqk = np.asarray(inp["mla_qk_norm"], f)
    for i in range(2):
        pp[:, PP_GE + i * 4 + 0] = np.tile(dqk[i, 0], 2)
        pp[:, PP_GE + i * 4 + 1] = np.tile(dqk[i, 1], 2)
        pp[:, PP_GE + i * 4 + 2] = np.tile(gqk[i, 0], 2)
        pp[:, PP_GE + i * 4 + 3] = np.tile(gqk[i, 1], 2)
        pp[:, PP_SUBLN + i] = subln[i]
        pp[:, PP_DLAM + i * 256:PP_DLAM + (i + 1) * 256] = np.broadcast_to(dlam[i].reshape(1, 256), (128, 256))
        pp[:, PP_MQ + i * 4:PP_MQ + i * 4 + 4] = _cols(mq[i])
        pp[:, PP_MKV + i * 2:PP_MKV + i * 2 + 2] = _cols(mkv[i])
        pp[0:96, PP_MQK + i * 2] = mqk[i, 0]
        pp[0:96, PP_MQK + i * 2 + 1] = mqk[i, 1]
    shared = {
        "ppd": pp, "cstd": _const_mats(), "roped": _rope_tables(),
        "ada_w": np.ascontiguousarray(np.asarray(inp["ada_w"], f)),
        "mlp_w1": np.ascontiguousarray(np.asarray(inp["mlp_w1"], f)),
        "mlp_w2": np.ascontiguousarray(np.asarray(inp["mlp_w2"], f)),
        "w_in_e": w_in_e, "w_out_e": w_out_e, "w_in_m": w_in_m,
        "w_q_up": np.ascontiguousarray(np.asarray(inp["mla_w_q_up"], f)),
        "w_kv_up": np.ascontiguousarray(np.asarray(inp["mla_w_kv_up"], f)),
        "w_out_m": np.ascontiguousarray(np.asarray(inp["mla_w_out"], f)),
    }
    return shared


def _prep_core(inp, seqs):
    f = np.float32
    x = np.asarray(inp["x"], f)
    ctx = np.asarray(inp["ctx"], f)
    c = np.asarray(inp["c"], f)
    c_ctx = np.asarray(inp["c_ctx"], f)
    ns = len(seqs)
    xTd = np.empty((ns, 128, 8, NT), f)
    for j, sidx in enumerate(seqs):
        cat = np.concatenate([ctx[sidx], x[sidx]], axis=0)
        xTd[j] = cat.T.reshape(8, 128, NT).transpose(1, 0, 2)
    cT = np.zeros((128, 8, 5), f)
    for j, sidx in enumerate(seqs):
        cT[:, :, j] = _cols(c[sidx])
    cT[:, :, 4] = _cols(c_ctx)
    return {"xTd": xTd, "cTd": cT}


def _unpack_out(outT):
    ns = outT.shape[0]
    return np.ascontiguousarray(outT.transpose(0, 2, 1, 3).reshape(ns, 1024, NLAT).transpose(0, 2, 1))


_NC_CACHE = {}


def run(inputs, n_cores=N_CORES, nseq=SEQ_PER_CORE, nlayers=DEPTH, trace=False):
    key = (nseq, nlayers)
    if key not in _NC_CACHE:
        _NC_CACHE[key] = Builder(nseq, nlayers).build()
    nc = _NC_CACHE[key]
    shared = _prep_shared(inputs, nlayers)
    in_maps = []
    for cidx in range(n_cores):
        seqs = list(range(cidx * nseq, (cidx + 1) * nseq))
        m = dict(shared)
        m.update(_prep_core(inputs, seqs))
        in_maps.append(m)
    res = run_bass_kernel_spmd(nc, in_maps, core_ids=list(range(n_cores)), trace=trace)
    outs = [_unpack_out(np.asarray(r["outT"])) for r in res.results]
    return np.concatenate(outs, axis=0), res


def kernel(**inputs):
    out, _ = run(inputs)
    return out.astype(np.float32)
```

```python
import math
from contextlib import ExitStack
import numpy as np
import concourse.bass as bass
import concourse.mybir as mybir
from concourse.bass_utils import run_bass_kernel_spmd

F32 = mybir.dt.float32
BF16 = mybir.dt.bfloat16
ALU = mybir.AluOpType
AF = mybir.ActivationFunctionType

D = 1024
NT = 2304
NCTX = 256
NLAT = 2048
DEPTH = 4
EPS = 1e-6
TB = [(0, 256), (256, 512), (768, 512), (1280, 512), (1792, 512)]
PIECE = 4096
NRING = 3
N_CORES = 8
SEQ_PER_CORE = 4

PP_NMIX = 0
PP_NMLP = 32
PP_ADAB = 64
PP_GE = 256
PP_SUBLN = 264
PP_DLAM = 266
PP_MQ = 778
PP_MKV = 786
PP_MQK = 790
NPP = 794


def lambda_init(layer):
    return 0.8 - 0.6 * math.exp(-0.3 * layer)


class Op:
    __slots__ = ("eng", "fn", "deps", "odeps", "signal", "sem", "val", "isdma", "epoch", "idx", "dur", "ndep", "users", "rdy", "st", "fin", "done")


class Prog:
    ENGS = ("pe", "act", "dve", "pool", "sp")
    WIN = 48

    def __init__(self, nc, stack):
        self.nc = nc
        self.stack = stack
        self.ops = {e: [] for e in self.ENGS}
        self.res = {}
        self.epochs = []
        self.dsems = {}
        self.nsem = 0
        self.new_epoch()

    def _sem(self, name):
        self.nsem += 1
        return self.stack.enter_context(self.nc.semaphore(name))

    def new_epoch(self):
        i = len(self.epochs)
        self.epochs.append({e: self._sem(f"e{i}_{e}") for e in self.ENGS})

    def add(self, eng, fn, r=(), w=(), dma=None):
        op = Op()
        op.eng = eng
        op.fn = fn
        op.deps = []
        op.signal = False
        op.isdma = dma is not None
        op.epoch = len(self.epochs) - 1
        op.sem = None
        op.val = 0
        cand = []
        for n in r:
            st = self.res.setdefault(n, [None, []])
            if st[0] is not None:
                cand.append((st[0], "raw"))
        for n in w:
            st = self.res.setdefault(n, [None, []])
            if st[0] is not None:
                cand.append((st[0], "waw"))
            for rd in st[1]:
                cand.append((rd, "war"))
        seen = set()
        op.odeps = []
        op.idx = len(self.ops[eng])
        op.dur = est_dur(eng, fn, op.isdma)
        for d, t in cand:
            if d is op or id(d) in seen:
                continue
            seen.add(id(d))
            if (not d.isdma) and (not op.isdma) and d.eng == eng:
                if eng == "pe":
                    op.odeps.append(d)
                    continue
            d.signal = True
            op.deps.append(d)
        for n in w:
            st = self.res[n]
            st[0] = op
            st[1] = []
        wset = set(w)
        lim = op.idx - self.WIN - 2
        for n in r:
            if n in wset:
                continue
            st = self.res[n]
            if not op.isdma:
                st[1] = [x for x in st[1] if x.isdma or x.eng != eng or x.idx > lim]
            st[1].append(op)
        if dma is not None:
            ent = self.dsems.get(dma)
            if ent is None:
                ent = [self._sem("d_" + dma), 0]
                self.dsems[dma] = ent
            ent[1] += 16
            op.sem = ent[0]
            op.val = ent[1]
        self.ops[eng].append(op)
        return op

    def fence(self, old, new):
        evs = []
        for n in old:
            st = self.res.get(n)
            if st is None:
                continue
            if st[0] is not None:
                evs.append(st[0])
            evs.extend(st[1])
        uniq = []
        seen = set()
        for e in evs:
            if id(e) not in seen:
                seen.add(id(e))
                uniq.append(e)
        for n in new:
            st = self.res.setdefault(n, [None, []])
            st[1] = list(st[1]) + uniq

    def schedule(self):
        import heapq
        W = self.WIN
        ENGS = self.ENGS
        for e in ENGS:
            for op in self.ops[e]:
                op.users = []
                op.ndep = 0
                op.rdy = 0.0
                op.done = False
        for e in ENGS:
            for op in self.ops[e]:
                for d in op.deps:
                    d.users.append(op)
                    op.ndep += 1
                for d in op.odeps:
                    d.users.append(op)
                    op.ndep += 1
        lists = {e: list(self.ops[e]) for e in ENGS}
        head = {e: 0 for e in ENGS}
        free = {e: 0.0 for e in ENGS}
        dma_free = [0.0]
        order = {e: [] for e in ENGS}
        remaining = sum(len(v) for v in lists.values())

        def best_of(e):
            lst = lists[e]
            n = len(lst)
            i = head[e]
            while i < n and lst[i].done:
                i += 1
            head[e] = i
            w = 1 if e == "sp" else W
            best = None
            bst = 0.0
            fr = free[e]
            iend = min(n, i + w)
            while i < iend:
                op = lst[i]
                if not op.done:
                    if op.ndep == 0:
                        st = op.rdy if op.rdy > fr else fr
                        if best is None or st < bst:
                            best = op
                            bst = st
                            if st <= fr:
                                break
                i += 1
            return best, bst

        cache = {e: best_of(e) for e in ENGS}
        while remaining > 0:
            be = None
            bop = None
            bst = 0.0
            for e in ENGS:
                op, st = cache[e]
                if op is None:
                    continue
                if bop is None or st < bst:
                    be, bop, bst = e, op, st
            if bop is None:
                raise RuntimeError("scheduler deadlock")
            bop.done = True
            bop.st = bst
            if bop.isdma:
                free[be] = bst + 60.0
                t0 = max(bst, dma_free[0])
                dma_free[0] = t0 + bop.dur
                fin = t0 + bop.dur + 2000.0
            else:
                fin = bst + bop.dur
                free[be] = fin
            bop.fin = fin
            order[be].append(bop)
            remaining -= 1
            dirty = {be}
            for u in bop.users:
                u.ndep -= 1
                lat = fin + (0.0 if (u.eng == be and not bop.isdma) else 120.0)
                if lat > u.rdy:
                    u.rdy = lat
                if u.ndep == 0:
                    dirty.add(u.eng)
            for e in dirty:
                cache[e] = best_of(e)
        self.ops = order
        self.sim_time = max(free.values())

    def emit(self, final_waits):
        self.schedule()
        for e in self.ENGS:
            cnt = {}
            for op in self.ops[e]:
                if op.isdma:
                    continue
                if op.signal:
                    cnt[op.epoch] = cnt.get(op.epoch, 0) + 1
                    op.val = cnt[op.epoch]
                    op.sem = self.epochs[op.epoch][e]
        ops = self.ops

        def run(e, eng):
            waited = {}
            for op in ops[e]:
                for d in op.deps:
                    k = d.sem.num
                    if waited.get(k, 0) >= d.val:
                        continue
                    eng.wait_ge(d.sem, d.val)
                    waited[k] = d.val
                fns = op.fn if isinstance(op.fn, list) else [op.fn]
                ins = None
                for (mname, a, k) in fns:
                    ins = getattr(eng, mname)(*a, **k)
                if op.isdma:
                    ins.then_inc(op.sem, 16)
                elif op.signal:
                    ins.then_inc(op.sem, 1)
            if e == "sp":
                for d in final_waits:
                    eng.wait_ge(d.sem, d.val)

        with self.nc.Block() as block:
            @block.tensor
            def _(eng):
                run("pe", eng)

            @block.scalar
            def _(eng):
                run("act", eng)

            @block.vector
            def _(eng):
                run("dve", eng)

            @block.gpsimd
            def _(eng):
                run("pool", eng)

            @block.sync
            def _(eng):
                run("sp", eng)


def _free_elems(ap):
    n = 1
    for d in ap.shape[1:]:
        n *= int(d)
    return n


def est_dur(eng, fn, isdma):
    fns = fn if isinstance(fn, list) else [fn]
    tot = 0.0
    for (mname, a, k) in fns:
        out = k.get("out", a[0] if a else None)
        fe = _free_elems(out) if out is not None else 512
        if isdma:
            bpe = 2 if out.dtype == BF16 else 4
            tot += fe * int(out.shape[0]) * bpe / 180.0
        elif eng == "pe":
            tot += 45.0 + 0.41 * max(fe, 64)
        elif eng == "act":
            tot += 220.0 + 0.83 * fe
        elif eng == "dve":
            tot += (300.0 + 5.4 * fe) if mname == "reciprocal" else (250.0 + 1.04 * fe)
        else:
            tot += 300.0 + 1.45 * fe
    return tot


def I(mname, *a, **k):
    return (mname, a, k)


class Rot:
    def __init__(self, items):
        self.items = list(items)
        self.i = 0

    def next(self):
        it = self.items[self.i % len(self.items)]
        self.i += 1
        return it


import os
class Builder:
    def __init__(self, nseq, nlayers):
        self.nseq = nseq
        self.nlayers = nlayers
        self.nc = bass.Bass("TRN2", target_bir_lowering=False)
        self.pieces = []

    def dram_in(self, name, shape, dt=F32):
        return self.nc.dram_tensor(name, list(shape), dt, kind="ExternalInput").ap()

    def add_piece(self, src, length):
        self.pieces.append((src, length))
        return len(self.pieces) - 1

    def build(self):
        nc = self.nc
        nseq, nlayers = self.nseq, self.nlayers
        with ExitStack() as stack:
            self.stack = stack
            P = self.P = Prog(nc, stack)
            xTd = self.dram_in("xTd", [nseq, 128, 8, NT])
            cTd = self.dram_in("cTd", [128, 8, 5])
            ppd = self.dram_in("ppd", [128, NPP])
            cstd = self.dram_in("cstd", [128, 640])
            roped = self.dram_in("roped", [2, 128, 4096])
            ada_w = self.dram_in("ada_w", [4, 1024, 6144])
            mlp_w1 = self.dram_in("mlp_w1", [4, 1024, 4096])
            mlp_w2 = self.dram_in("mlp_w2", [4, 4096, 1024])
            w_in_e = self.dram_in("w_in_e", [2, 1024, 2304])
            w_out_e = self.dram_in("w_out_e", [2, 1024, 1024])
            w_in_m = self.dram_in("w_in_m", [2, 1024, 864])
            w_q_up = self.dram_in("w_q_up", [2, 512, 1536])
            w_kv_up = self.dram_in("w_kv_up", [2, 256, 2048])
            w_out_m = self.dram_in("w_out_m", [2, 1024, 1024])
            outT = nc.dram_tensor("outT", [nseq, 128, 8, NLAT], F32, kind="ExternalOutput").ap()

            def kmaj(w2d, c0, W):
                return w2d[:, c0:c0 + W].rearrange("(k p) w -> p k w", p=128)

            pc = {}
            pc["cst"] = self.add_piece(cstd[:, :], 640)
            pc["rope_e"] = self.add_piece(roped[0], 4096)
            pc["rope_m"] = self.add_piece(roped[1], 4096)
            for l in range(nlayers):
                for g in range(12):
                    pc[("ada", l, g)] = self.add_piece(kmaj(ada_w[l], g * 512, 512), 4096)
            for l in range(nlayers):
                i = l // 2
                if l % 2 == 0:
                    pc[("E0", l)] = self.add_piece(kmaj(w_in_e[i], 0, 512), 4096)
                    pc[("E1", l)] = self.add_piece(kmaj(w_in_e[i], 512, 256), 2048)
                    pc[("E2", l)] = self.add_piece(kmaj(w_in_e[i], 768, 512), 4096)
                    pc[("E3", l)] = self.add_piece(kmaj(w_in_e[i], 1280, 512), 4096)
                    pc[("E4", l)] = self.add_piece(kmaj(w_in_e[i], 1792, 512), 4096)
                    pc[("E5", l)] = self.add_piece(kmaj(w_out_e[i], 0, 512), 4096)
                    pc[("E6", l)] = self.add_piece(kmaj(w_out_e[i], 512, 512), 4096)
                else:
                    pc[("M0", l)] = self.add_piece(kmaj(w_in_m[i], 0, 512), 4096)
                    pc[("M1", l)] = self.add_piece(kmaj(w_in_m[i], 512, 352), 2816)
                    for hg in range(8):
                        pc[("M2", l, hg)] = self.add_piece(kmaj(w_kv_up[i], hg * 256, 256), 512)
                        pc[("M3", l, hg)] = self.add_piece(kmaj(w_q_up[i], hg * 192, 192), 768)
                    for hp in range(4):
                        pc[("M5", l, hp)] = self.add_piece(kmaj(w_out_m[i, hp * 256:(hp + 1) * 256, :], 0, 1024), 2048)
                for g in range(8):
                    pc[("W1", l, g)] = self.add_piece(kmaj(mlp_w1[l], g * 512, 512), 4096)
                for o in range(8):
                    pc[("W2", l, o)] = self.add_piece(kmaj(mlp_w2[l], o * 128, 128), 4096)
            self.pc = pc
            npieces = len(self.pieces)
            wbf = nc.dram_tensor("wbf", [npieces, 128, PIECE], BF16).ap()
            self.wbf = wbf

            def sb(name, shape, dt):
                return stack.enter_context(nc.sbuf_tensor(name, list(shape), dt))

            def ps(name, shape):
                return stack.enter_context(nc.psum_tensor(name, list(shape), F32))

            xT = self.xT = sb("xT", [128, 8, NT], F32)
            ARENA = 24192
            self.arena = sb("arena", [128, ARENA], BF16)
            self.ring = [sb(f"ring{i}", [128, PIECE], BF16) for i in range(NRING)]
            self.ropeT = sb("ropeT", [128, 2, 2048], BF16)
            self.hT = sb("hT", [128, 8, 512], BF16)
            self.QT = sb("QT", [128, 8, 512], BF16)
            self.OT = sb("OT", [128, 8, 512], BF16)
            self.PT = sb("PT", [128, 3, 1024], BF16)
            self.NTF = 6
            self.tmpf = sb("tmpf", [128, self.NTF, 512], F32)
            self.tmpb = sb("tmpb", [128, 4, 512], BF16)
            pp = self.pp = sb("pp", [128, NPP], F32)
            self.modT = sb("modT", [128, 4, 48, 5], F32)
            cmat = self.cmat = sb("cmat", [128, 640], BF16)
            cT = sb("cT", [128, 8, 5], F32)
            scT = sb("scT", [128, 8, 5], BF16)
            self.cols = sb("cols", [128, 32], F32)
            self.lamc = sb("lamc", [128, 16], F32)

            self.ones128 = cmat[:, 0:128]
            self.blk64 = cmat[:, 128:256]
            self.ones96 = cmat[:, 256:384]
            self.perm_e = cmat[:, 384:512]
            self.perm_m = cmat[:, 512:640]

            g0 = ps("g0", [128, 512]); g1 = ps("g1", [128, 512])
            s0 = ps("s0", [128, 1024]); s1 = ps("s1", [128, 1024])
            o0 = ps("o0", [128, 512]); o1 = ps("o1", [128, 512])
            self.bank = {"g0": g0[:, :], "g1": g1[:, :], "o0": o0[:, :], "o1": o1[:, :],
                         "s0a": s0[:, 0:512], "s0b": s0[:, 512:1024],
                         "s1a": s1[:, 0:512], "s1b": s1[:, 512:1024]}
            self.s_tiles = [(s0, ["s0a", "s0b"]), (s1, ["s1a", "s1b"])]
            self.gen_small = Rot(["g0", "g1"])
            self.gen_big = Rot(["g0", "g1", "o0", "o1", "s0a", "s0b", "s1a", "s1b"])
            self.gen = self.gen_big
            self.tf = Rot(list(range(self.NTF)))
            self.tb = Rot(list(range(4)))
            self.ptr = Rot(list(range(3)))
            self.ring_i = 0
            self.arena_names = []

            def dma_simple(out_ap, in_ap, r, w, key):
                return P.add("sp", I("dma_start", out=out_ap, in_=in_ap), r=r, w=w, dma=key)

            dma_simple(pp[:, :], ppd[:, :], [], ["pp"], "c_pp")
            dma_simple(cT[:, :, :], cTd[:, :, :], [], ["cT"], "c_cT")

            self.prepass()

            dma_simple(cmat[:, :], wbf[pc["cst"], :, 0:640], [f"wbf{pc['cst']}"], ["cmat"], "c_cm")

            self.preamble(cT, scT)

            store_ops = []
            for s in range(nseq):
                if s > 0:
                    P.new_epoch()
                for b, (t0, n) in enumerate(TB):
                    dma_simple(xT[:, :, t0:t0 + n], xTd[s, :, :, t0:t0 + n], [], [f"x{b}"], f"xl{b}")
                for l in range(nlayers):
                    if l == 2:
                        P.new_epoch()
                    cc = l < DEPTH - 1
                    self.layer_cols(l, s)
                    if l % 2 == 0:
                        self.even_layer(l, s, cc)
                    else:
                        self.mla_layer(l, s, cc)
                    self.ffn(l, s, cc)
                for b in range(1, 5):
                    t0, n = TB[b]
                    op = dma_simple(outT[s, :, :, t0 - NCTX:t0 - NCTX + n], xT[:, :, t0:t0 + n],
                                    [f"x{b}"], [], f"xs{b}")
                    store_ops.append(op)
            P.emit(store_ops)
        return nc

    def set_arena(self, names):
        self.P.fence(self.arena_names, names)
        self.arena_names = list(names)

    def aview(self, off, shape):
        n = int(np.prod(shape))
        ap = self.arena[:, off:off + n]
        if len(shape) == 2:
            return ap.rearrange("p (a b) -> p a b", a=shape[0])
        return ap

    def tmpF(self):
        i = self.tf.next()
        return self.tmpf[:, i, :], f"tf{i}"

    def tmpB(self):
        i = self.tb.next()
        return self.tmpb[:, i, :], f"tb{i}"

    def load_piece(self, key):
        pid = self.pc[key]
        ln = self.pieces[pid][1]
        slot = self.ring_i % NRING
        self.ring_i += 1
        t = self.ring[slot]
        self.P.add("sp", I("dma_start", out=t[:, 0:ln], in_=self.wbf[pid, :, 0:ln]),
                   r=[f"wbf{pid}"], w=[f"ring{slot}"], dma=f"ring{slot}")
        return t, f"ring{slot}"

    def prepass(self):
        P = self.P
        xflat = self.xT[:, :, :].rearrange("p a b -> p (a b)")
        import os
        NST = int(os.environ.get('NST', '4'))
        cast_engs = Rot(["dve", "act", "pool"])
        stg_names = [f"stg{i}" for i in range(NST)]
        cb_names = [f"cb{i}" for i in range(NST)]
        for pid, (src, ln) in enumerate(self.pieces):
            sl = pid % NST
            stg = xflat[:, sl * PIECE: sl * PIECE + ln]
            cb = self.arena[:, sl * PIECE: sl * PIECE + ln]
            if len(src.shape) == 3:
                stg_v = stg.rearrange("p (k w) -> p k w", k=src.shape[1])
            else:
                stg_v = stg
            P.add("sp", I("dma_start", out=stg_v, in_=src), r=[], w=[stg_names[sl]], dma=f"stl{sl}")
            ce = cast_engs.next()
            if ce == "act":
                P.add("act", I("activation", cb, stg, AF.Copy), r=[stg_names[sl]], w=[cb_names[sl]])
            else:
                P.add(ce, I("tensor_copy", cb, stg), r=[stg_names[sl]], w=[cb_names[sl]])
            P.add("sp", I("dma_start", out=self.wbf[pid, :, 0:ln], in_=cb),
                  r=[cb_names[sl]], w=[f"wbf{pid}"], dma=f"sts{sl}")
        P.fence(stg_names, [f"x{b}" for b in range(5)])
        self.arena_names = cb_names

    def preamble(self, cT, scT):
        P = self.P
        pp, lamc, modT = self.pp, self.lamc, self.modT
        nlayers = self.nlayers
        X = mybir.AxisListType.X
        for i in range(2):
            l = 2 * i
            if l >= nlayers:
                continue
            li = lambda_init(l)
            base = PP_DLAM + i * 256
            tA, nA = self.tmpF()
            P.add("dve", [I("tensor_tensor", tA[:, 0:64], pp[:, base:base + 64], pp[:, base + 64:base + 128], ALU.mult),
                          I("tensor_tensor", tA[:, 64:128], pp[:, base + 128:base + 192], pp[:, base + 192:base + 256], ALU.mult)],
                  r=["pp"], w=[nA])
            P.add("dve", [I("reduce_sum", lamc[:, 8 + 2 * i:9 + 2 * i], tA[:, 0:64], X),
                          I("reduce_sum", lamc[:, 9 + 2 * i:10 + 2 * i], tA[:, 64:128], X)],
                  r=[nA], w=["lamc_s"])
            P.add("act", I("activation", lamc[:, 12 + 2 * i:14 + 2 * i], lamc[:, 8 + 2 * i:10 + 2 * i], AF.Exp),
                  r=["lamc_s"], w=["lamc_e"])
            P.add("dve", I("tensor_tensor", lamc[:, 4 + i:5 + i], lamc[:, 13 + 2 * i:14 + 2 * i],
                           lamc[:, 12 + 2 * i:13 + 2 * i], ALU.subtract), r=["lamc_e"], w=["lamc_d"])
            P.add("dve", I("tensor_scalar", lamc[:, i:i + 1], lamc[:, 4 + i:5 + i], -li, None, ALU.add),
                  r=["lamc_d"], w=["lamc"])
            P.add("dve", I("tensor_scalar", lamc[:, 2 + i:3 + i], pp[:, PP_SUBLN + i:PP_SUBLN + i + 1], 1.0 - li, None, ALU.mult),
                  r=["pp"], w=["lamc"])
        P.add("act", I("activation", scT[:, :, :], cT[:, :, :], AF.Silu), r=["cT"], w=["scT"])
        for l in range(nlayers):
            bank = self.gen.next()
            bk = self.bank[bank]
            for g in range(12):
                t, rn = self.load_piece(("ada", l, g))
                tv = t[:, 0:4096].rearrange("p (k w) -> p k w", k=8)
                ins = []
                for c in range(4):
                    j = g * 4 + c
                    for k in range(8):
                        ins.append(I("matmul", bk[:, j * 5:j * 5 + 5], tv[:, k, c * 128:(c + 1) * 128],
                                     scT[:, k, :], start=(k == 0), stop=(k == 7)))
                P.add("pe", ins, r=[rn, "scT"], w=[bank])
            ins = []
            for j in range(48):
                ins.append(I("tensor_scalar", modT[:, l, j, :], bk[:, j * 5:j * 5 + 5],
                             pp[:, PP_ADAB + l * 48 + j:PP_ADAB + l * 48 + j + 1], None, ALU.add))
            P.add("dve", ins, r=[bank, "pp"], w=["modT"])

    def layer_cols(self, l, s):
        pp, modT, cols = self.pp, self.modT, self.cols
        specs = [(0, 8, s, PP_NMIX), (8, 8, 4, PP_NMIX), (16, 32, s, PP_NMLP), (24, 32, 4, PP_NMLP)]
        ins = []
        for (co, j0, col, nb) in specs:
            ins.append(I("scalar_tensor_tensor", cols[:, co:co + 8], modT[:, l, j0:j0 + 8, col], 1.0,
                         pp[:, nb + l * 8: nb + l * 8 + 8], ALU.add, ALU.mult))
        self.P.add("dve", ins, r=["modT", "pp"], w=["cols"])

    def modcol(self, l, m, k, col):
        return self.modT[:, l, m * 8 + k, col:col + 1]

    def hbuf(self, alt):
        if alt:
            return self.QT, [f"QT{c}" for c in range(8)]
        return self.hT, ["hT"]

    def norm_mod(self, l, s, b, which, alt=False, light_act=False):
        P = self.P
        t0, n = TB[b]
        isctx = (b == 0)
        col = 4 if isctx else s
        a_off = (0 if which == 1 else 16) + (8 if isctx else 0)
        m_shift = 0 if which == 1 else 3
        xT = self.xT
        hT, hres = self.hbuf(alt)
        xr = f"x{b}"
        P.add("act", I("activation", hT[:, :, 0:n], xT[:, :, t0:t0 + n], AF.Square), r=[xr], w=hres)
        bank = self.gen.next()
        bk = self.bank[bank]
        P.add("pe", [I("matmul", bk[:, 0:n], self.ones128, hT[:, k, 0:n], start=(k == 0), stop=(k == 7))
                     for k in range(8)], r=hres + ["cmat"], w=[bank])
        rt, rtn = self.tmpF()
        P.add("act", I("activation", rt[:, 0:n], bk[:, 0:n], AF.Ln, bias=EPS, scale=1.0 / D), r=[bank], w=[rtn])
        P.add("act", I("activation", rt[:, 0:n], rt[:, 0:n], AF.Exp, scale=-0.5), r=[rtn], w=[rtn])
        tks = [self.tmpF(), self.tmpF()]
        for k in range(8):
            tk, tkn = tks[k % 2]
            P.add("dve", I("scalar_tensor_tensor", tk[:, 0:n], xT[:, k, t0:t0 + n],
                           self.cols[:, a_off + k:a_off + k + 1], rt[:, 0:n], ALU.mult, ALU.mult),
                  r=[xr, "cols", rtn], w=[tkn])
            if light_act:
                P.add("pool", I("tensor_scalar", hT[:, k, 0:n], tk[:, 0:n], self.modcol(l, m_shift, k, col), None, ALU.add),
                      r=[tkn, "modT"], w=hres)
            else:
                P.add("act", I("activation", hT[:, k, 0:n], tk[:, 0:n], AF.Identity,
                               bias=self.modcol(l, m_shift, k, col), scale=1.0), r=[tkn, "modT"], w=hres)

    def proj(self, wt, wr, nk, W, c0, M, rhs_list, rhs_res, n):
        bank = self.gen.next()
        bk = self.bank[bank]
        tv = wt[:, 0:nk * W].rearrange("p (k w) -> p k w", k=nk)
        self.P.add("pe", [I("matmul", bk[0:M, 0:n], tv[:, k, c0:c0 + M], rhs_list[k], start=(k == 0), stop=(k == nk - 1))
                          for k in range(nk)], r=[wr] + list(rhs_res), w=[bank])
        return bank

    def qknorm_rope(self, bank, R, n, gain, ones_mat, dim, rope, out_ap, out_res, extra_rows=None, mode="P"):
        P = self.P
        bk = self.bank[bank]
        qg, qgn = self.tmpF()
        Rp = R if extra_rows is None else extra_rows[0]
        if mode == "A":
            P.add("act", I("activation", qg[0:Rp, 0:n], bk[0:Rp, 0:n], AF.Identity, scale=gain[0:Rp, :]),
                  r=[bank, "pp"], w=[qgn])
        else:
            P.add("dve", I("tensor_scalar", qg[0:Rp, 0:n], bk[0:Rp, 0:n], gain[0:Rp, :], None, ALU.mult),
                  r=[bank, "pp"], w=[qgn])
        if extra_rows is not None:
            r0, r1, src, srcres = extra_rows
            P.add("pool", I("tensor_copy", qg[r0:r1, 0:n], src), r=[srcres], w=[qgn])
        sq, sqn = self.tmpB()
        if mode == "A":
            P.add("act", I("activation", sq[0:R, 0:n], qg[0:R, 0:n], AF.Square), r=[qgn], w=[sqn])
        else:
            P.add(os.environ.get("E_SQ", "dve"), I("tensor_tensor", sq[0:R, 0:n], qg[0:R, 0:n], qg[0:R, 0:n], ALU.mult), r=[qgn], w=[sqn])
        b2 = self.gen.next()
        bk2 = self.bank[b2]
        P.add("pe", I("matmul", bk2[0:R, 0:n], ones_mat[0:R, 0:R], sq[0:R, 0:n], start=True, stop=True),
              r=[sqn, "cmat"], w=[b2])
        rt, rtn = self.tmpF()
        P.add("act", I("activation", rt[0:R, 0:n], bk2[0:R, 0:n], AF.Ln, bias=EPS, scale=1.0 / dim), r=[b2], w=[rtn])
        P.add("act", I("activation", rt[0:R, 0:n], rt[0:R, 0:n], AF.Exp, scale=-0.5), r=[rtn], w=[rtn])
        if rope is not None:
            perm, r0, r1, tok0 = rope
            qb, qbn = self.tmpB()
            P.add("dve" if mode == "A" else os.environ.get("E_QB", "dve"), I("tensor_copy", qb[0:R, 0:n], qg[0:R, 0:n]), r=[qgn], w=[qbn])
            b3 = self.gen.next()
            bk3 = self.bank[b3]
            P.add("pe", I("matmul", bk3[0:R, 0:n], perm[0:R, 0:R], qb[0:R, 0:n], start=True, stop=True),
                  r=[qbn, "cmat"], w=[b3])
            cosT = self.ropeT[r0:r1, 0, tok0:tok0 + n]
            sinT = self.ropeT[r0:r1, 1, tok0:tok0 + n]
            t2, t2n = self.tmpF()
            P.add("dve", I("tensor_tensor", t2[r0:r1, 0:n], bk3[r0:r1, 0:n], sinT, ALU.mult), r=[b3, "rope"], w=[t2n])
            P.add(os.environ.get("E_COS", "pool"), I("tensor_tensor", qg[r0:r1, 0:n], qg[r0:r1, 0:n], cosT, ALU.mult), r=[qgn, "rope"], w=[qgn])
            P.add(os.environ.get("E_ADD", "pool"), I("tensor_tensor", qg[r0:r1, 0:n], qg[r0:r1, 0:n], t2[r0:r1, 0:n], ALU.add),
                  r=[qgn, t2n], w=[qgn])
        P.add("dve", I("tensor_tensor", out_ap, qg[0:R, 0:n], rt[0:R, 0:n], ALU.mult), r=[qgn, rtn], w=[out_res])

    def load_rope(self, which):
        pid = self.pc[which]
        self.P.add("sp", I("dma_start", out=self.ropeT[:, :, :].rearrange("p a b -> p (a b)"), in_=self.wbf[pid, :, :]),
                   r=[f"wbf{pid}"], w=["rope"], dma="rope")

    def attention(self, maps, n, nchunks, scale):
        P = self.P
        npairs = nchunks // 2
        units = []
        for mi, m in enumerate(maps):
            for p in range(npairs):
                units.append((mi, p))
        opool = Rot(["o0", "o1"])
        state = {}

        def rec_S(u):
            mi, p = units[u]
            m = maps[mi]
            st, snames = self.s_tiles[u % 2]
            ins = []
            for h in range(2):
                ch = 2 * p + h
                ins.append(I("matmul", st[:, h * 512:h * 512 + n], m["kt"](ch), m["q"], start=True, stop=True))
            P.add("pe", ins, r=[m["kres"](2 * p), m["kres"](2 * p + 1), m["qres"]], w=snames)

        def rec_exp(u):
            st, snames = self.s_tiles[u % 2]
            pi = self.ptr.next()
            state[u] = pi
            src = st[:, :].rearrange("p (h c) -> p h c", h=2)[:, :, 0:n]
            dst = self.PT[:, pi, :].rearrange("p (h c) -> p h c", h=2)[:, :, 0:n]
            P.add("act", I("activation", dst, src, AF.Exp, scale=scale), r=snames, w=[f"pt{pi}"])

        def rec_PV(u):
            mi, p = units[u]
            m = maps[mi]
            pi = state[u]
            if p == 0:
                m["obanks"] = [opool.next() for _ in m["units"]]
            obanks = m["obanks"]
            ins = []
            for h in range(2):
                ch = 2 * p + h
                for ui, (lf, _) in enumerate(m["units"]):
                    ins.append(I("matmul", self.bank[obanks[ui]][:, 0:n], lf(ch), self.PT[:, pi, h * 512:h * 512 + n],
                                 start=(ch == 0), stop=(ch == nchunks - 1)))
            rr = [f"pt{pi}"]
            for (_, vf) in m["units"]:
                rr += [vf(2 * p), vf(2 * p + 1)]
            P.add("pe", ins, r=rr, w=list(obanks))
            if p == npairs - 1:
                m["done"](obanks)
                if m.get("after") is not None:
                    m["after"]()

        nu = len(units)
        PVD = int(os.environ.get("PVD", "1"))
        rec_S(0)
        for u in range(nu):
            if u + 1 < nu:
                rec_S(u + 1)
            rec_exp(u)
            if u - PVD >= 0:
                rec_PV(u - PVD)
        for u in range(max(0, nu - PVD), nu):
            rec_PV(u)

    def o_norm(self, obank, vrows, srows, n, out_ap, out_res, scratch=None):
        P = self.P
        bk = self.bank[obank]
        v0, v1 = vrows
        if scratch is None:
            t, tn = self.tmpF()
            tap = t[v0:v1, 0:n]
        else:
            tap, tn = scratch
        P.add("dve", I("reciprocal", tap, bk[srows[0]:srows[1], 0:n]), r=[obank], w=[tn])
        P.add("dve", I("tensor_tensor", out_ap, bk[v0:v1, 0:n], tap, ALU.mult), r=[obank, tn], w=[out_res])

    @staticmethod
    def blk_of_chunk(ch):
        t = ch * 128
        for b, (t0, n) in enumerate(TB):
            if t0 <= t < t0 + n:
                return b

    def unit_AP(self, first_off, second_off):
        ARW = self.arena.shape[1]
        return bass.AP(self.arena, first_off, [[ARW, 128], [second_off - first_off, 2], [1, 64]])

    def even_layer(self, l, s, cc):
        P = self.P
        i = l // 2
        pp = self.pp
        KT = self.aview(0, [5, NT])
        VOFF = 5 * NT
        VCH = 704
        Vt = self.arena
        V3 = self.arena[:, VOFF:VOFF + 18 * VCH].rearrange("p (c w) -> p c w", c=18)
        ktn = [f"KT{b}" for b in range(5)]
        vn = [f"V{b}" for b in range(5)]
        self.set_arena(ktn + vn)
        self.load_rope("rope_e")
        P.add("pool", I("memset", V3[:, :, 576:640], 1.0), r=[], w=vn)
        g_aq = pp[:, PP_GE + i * 4 + 0:PP_GE + i * 4 + 1]
        g_ak = pp[:, PP_GE + i * 4 + 1:PP_GE + i * 4 + 2]
        g_bq = pp[:, PP_GE + i * 4 + 2:PP_GE + i * 4 + 3]
        g_bk = pp[:, PP_GE + i * 4 + 3:PP_GE + i * 4 + 4]
        boc = self.blk_of_chunk

        self.gen = self.gen_big
        self.norm_mod(l, s, 0, 1, False)
        for b, (t0, n) in enumerate(TB):
            alt = (b % 2 == 1)
            if b + 1 < 5:
                self.norm_mod(l, s, b + 1, 1, not alt)
            hTb, hres = self.hbuf(alt)
            rope = None if b == 0 else (self.perm_e, 0, 128, t0 - NCTX)
            hrhs = [hTb[:, k, 0:n] for k in range(8)]
            wt, wr = self.load_piece(("E0", l))
            for c in range(4):
                bank = self.proj(wt, wr, 8, 512, c * 128, 128, hrhs, hres, n)
                self.qknorm_rope(bank, 128, n, g_ak, self.blk64, 64, rope, KT[:, c, t0:t0 + n], ktn[b], mode="A")
            wt1, wr1 = self.load_piece(("E1", l))
            bank = self.proj(wt1, wr1, 8, 256, 0, 128, hrhs, hres, n)
            self.qknorm_rope(bank, 128, n, g_bk, self.blk64, 64, rope, KT[:, 4, t0:t0 + n], ktn[b], mode="A")
            wt2, wr2 = self.load_piece(("E2", l))
            tv1 = wt1[:, 0:2048].rearrange("p (k w) -> p k w", k=8)
            tv2 = wt2[:, 0:4096].rearrange("p (k w) -> p k w", k=8)
            for u in range(n // 128):
                ch = (t0 + u * 128) // 128
                vbase = VOFF + ch * VCH
                bA = self.gen.next()
                bB = self.gen.next()
                ins = []
                for k in range(8):
                    ins.append(I("matmul", self.bank[bA][:, 0:512], hTb[:, k, u * 128:(u + 1) * 128], tv2[:, k, :],
                                 start=(k == 0), stop=(k == 7)))
                for k in range(8):
                    ins.append(I("matmul", self.bank[bB][:, 0:128], hTb[:, k, u * 128:(u + 1) * 128], tv1[:, k, 128:256],
                                 start=(k == 0), stop=(k == 7)))
                P.add("pe", ins, r=[wr1, wr2] + hres, w=[bA, bB])
                P.add("dve", I("tensor_copy", Vt[:, vbase:vbase + 512], self.bank[bA][:, 0:512]), r=[bA], w=[vn[b]])
                P.add("act", I("activation", self.unit_AP(vbase + 512, vbase + 640),
                               self.bank[bB][:, 0:128].rearrange("p (h c) -> p h c", h=2), AF.Copy),
                      r=[bB], w=[vn[b]])

        self.gen = self.gen_small
        neglam = self.lamc[:, i:i + 1]
        sublnp = self.lamc[:, 2 + i:3 + i]
        blocks = [b for b in range(5) if not (b == 0 and not cc)]
        LIGHT = bool(int(os.environ.get("LIGHT", "0")))
        qstate = {}

        def qbuild_chunk(b, cc_):
            t0, n = TB[b]
            rope = None if b == 0 else (self.perm_e, 0, 128, t0 - NCTX)
            hrhs = [self.hT[:, k, 0:n] for k in range(8)]
            if cc_ % 4 == 0:
                qstate["w"] = self.load_piece(("E3" if cc_ < 4 else "E4", l))
            wt, wr = qstate["w"]
            gq = g_aq if cc_ < 4 else g_bq
            bank = self.proj(wt, wr, 8, 512, (cc_ % 4) * 128, 128, hrhs, ["hT"], n)
            self.qknorm_rope(bank, 128, n, gq, self.blk64, 64, rope, self.QT[:, cc_, 0:n], f"QT{cc_}")

        self.norm_mod(l, s, blocks[0], 1)
        for c_ in range(8):
            qbuild_chunk(blocks[0], c_)
        for bi, b in enumerate(blocks):
            t0, n = TB[b]
            nchunks = 2 if b == 0 else 18
            nxt = blocks[bi + 1] if bi + 1 < len(blocks) else None
            if nxt is not None:
                self.norm_mod(l, s, nxt, 1, light_act=LIGHT)
            maps = []
            dtmp = {}
            for h in range(4):
                for m in range(2):
                    rows = (64 * m, 64 * m + 64)

                    def kt(ch, h=h, rows=rows):
                        return KT[rows[0]:rows[1], h, ch * 128:(ch + 1) * 128]

                    def uA(ch, h=h):
                        return V3[:, ch, h * 128:(h + 1) * 128]

                    def uB(ch):
                        return self.ones128

                    def done(obanks, h=h, m=m, n=n):
                        d, dn = self.tmpF()
                        dtmp[(h, m)] = (d, dn)
                        P.add("act", I("activation", d[:, 0:n], self.bank[obanks[1]][:, 0:n], AF.Ln), r=[obanks[1]], w=[dn])
                        P.add("act", I("activation", d[:, 0:n], d[:, 0:n], AF.Exp, scale=-1.0), r=[dn], w=[dn])
                        P.add("dve", I("tensor_tensor", d[:, 0:n], self.bank[obanks[0]][:, 0:n], d[:, 0:n], ALU.mult),
                              r=[obanks[0], dn], w=[dn])
                        if m == 1:
                            d0, d0n = dtmp[(h, 0)]
                            P.add("dve", I("scalar_tensor_tensor", d0[:, 0:n], d[:, 0:n], neglam, d0[:, 0:n], ALU.mult, ALU.add),
                                  r=[dn, d0n, "lamc"], w=[d0n])
                            sq, sqn = self.tmpB()
                            P.add("act", I("activation", sq[:, 0:n], d0[:, 0:n], AF.Square), r=[d0n], w=[sqn])
                            b2 = self.gen.next()
                            P.add("pe", I("matmul", self.bank[b2][:, 0:n], self.ones128, sq[:, 0:n], start=True, stop=True),
                                  r=[sqn, "cmat"], w=[b2])
                            rt, rtn = self.tmpF()
                            P.add("act", I("activation", rt[:, 0:n], self.bank[b2][:, 0:n], AF.Ln, bias=EPS, scale=1.0 / 128),
                                  r=[b2], w=[rtn])
                            P.add("act", I("activation", rt[:, 0:n], rt[:, 0:n], AF.Exp, scale=-0.5), r=[rtn], w=[rtn])
                            P.add("dve", I("scalar_tensor_tensor", self.OT[:, h, 0:n], d0[:, 0:n], sublnp, rt[:, 0:n],
                                           ALU.mult, ALU.mult), r=[d0n, rtn, "lamc"], w=[f"OT{h}"])
                    after = None
                    if m == 1 and nxt is not None:
                        after = (lambda nxt=nxt, h=h: qbuild_chunk(nxt, h))
                    maps.append(dict(kt=kt, kres=lambda ch: ktn[boc(ch)],
                                     q=self.QT[rows[0]:rows[1], h, 0:n], qres=f"QT{h}",
                                     units=[(uA, lambda ch: vn[boc(ch)]), (uB, lambda ch: "cmat")],
                                     done=done, after=after))
            self.attention(maps, n, nchunks, 64 ** -0.5)
            maps = []
            for g in range(4):
                for kv in range(2):
                    rows = (64 * kv, 64 * kv + 64)

                    def kt(ch, rows=rows):
                        return KT[rows[0]:rows[1], 4, ch * 128:(ch + 1) * 128]
                    if kv == 0:
                        def uf(ch):
                            return V3[:, ch, 512:640]
                        vrows, srows = (0, 64), (64, 128)
                    else:
                        def uf(ch):
                            return V3[:, ch, 576:704]
                        vrows, srows = (64, 128), (0, 64)

                    def done(obanks, g=g, vrows=vrows, srows=srows, n=n):
                        self.o_norm(obanks[0], vrows, srows, n, self.OT[vrows[0]:vrows[1], 4 + g, 0:n], f"OT{4 + g}")
                    after = None
                    if kv == 1 and nxt is not None:
                        after = (lambda nxt=nxt, g=g: qbuild_chunk(nxt, 4 + g))
                    maps.append(dict(kt=kt, kres=lambda ch: ktn[boc(ch)],
                                     q=self.QT[rows[0]:rows[1], 4 + g, 0:n], qres=f"QT{4 + g}",
                                     units=[(uf, lambda ch: vn[boc(ch)])], done=done, after=after))
            self.attention(maps, n, nchunks, 64 ** -0.5)
            col = 4 if b == 0 else s
            orhs = [self.OT[:, k, 0:n] for k in range(8)]
            ores = [f"OT{k}" for k in range(8)]
            for half, pk in enumerate(["E5", "E6"]):
                wt, wr = self.load_piece((pk, l))
                for c in range(4):
                    o = half * 4 + c
                    bank = self.proj(wt, wr, 8, 512, c * 128, 128, orhs, ores, n)
                    P.add("dve", I("scalar_tensor_tensor", self.xT[:, o, t0:t0 + n], self.bank[bank][:, 0:n],
                                   self.modcol(l, 2, o, col), self.xT[:, o, t0:t0 + n], ALU.mult, ALU.add),
                          r=[bank, "modT", f"x{b}"], w=[f"x{b}"])

    def mla_layer(self, l, s, cc):
        import os
        if int(os.environ.get("MLA_STOP", "99")) <= 0:
            return
        P = self.P
        i = l // 2
        pp = self.pp
        QCN = self.aview(0, [4, NT])
        KVCN = self.aview(4 * NT, [2, NT])
        KRG = self.arena[:, 6 * NT:7 * NT]
        KTm = self.aview(7 * NT, [2, NT])
        VOFF = 9 * NT
        VCH = 192
        V3 = self.arena[:, VOFF:VOFF + 18 * VCH].rearrange("p (c w) -> p c w", c=18)
        qn_ = [f"QCN{b}" for b in range(5)]
        kvn_ = [f"KVCN{b}" for b in range(5)]
        krn_ = [f"KRG{b}" for b in range(5)]
        ktn = [f"KTm{b}" for b in range(5)]
        vn = [f"Vm{b}" for b in range(5)]
        self.set_arena(qn_ + kvn_ + krn_ + ktn + vn)
        self.load_rope("rope_m")
        Vt = self.arena
        P.add("pool", I("memset", V3[:, :, 64:128], 1.0), r=[], w=vn)
        g_q = pp[:, PP_MQK + i * 2:PP_MQK + i * 2 + 1]
        g_k = pp[:, PP_MQK + i * 2 + 1:PP_MQK + i * 2 + 2]
        boc = self.blk_of_chunk

        self.gen = self.gen_big
        self.norm_mod(l, s, 0, 1, False)
        for b, (t0, n) in enumerate(TB):
            alt = (b % 2 == 1)
            if b + 1 < 5:
                self.norm_mod(l, s, b + 1, 1, not alt)
            hTb, hres = self.hbuf(alt)
            hrhs = [hTb[:, k, 0:n] for k in range(8)]
            for (pk, W, nch, dst, dres, gbase, dim) in [("M0", 512, 4, QCN, qn_[b], PP_MQ + i * 4, 512),
                                                        ("M1", 352, 2, KVCN, kvn_[b], PP_MKV + i * 2, 256)]:
                wt, wr = self.load_piece((pk, l))
                b2 = self.gen.next()
                for c in range(nch):
                    bank = self.proj(wt, wr, 8, W, c * 128, 128, hrhs, hres, n)
                    bk = self.bank[bank]
                    P.add("act", I("activation", dst[:, c, t0:t0 + n], bk[:, 0:n], AF.Copy), r=[bank], w=[dres])
                    sq, sqn = self.tmpB()
                    P.add("act", I("activation", sq[:, 0:n], bk[:, 0:n], AF.Square), r=[bank], w=[sqn])
                    P.add("pe", I("matmul", self.bank[b2][:, 0:n], self.ones128, sq[:, 0:n],
                                  start=(c == 0), stop=(c == nch - 1)), r=[sqn, "cmat"], w=[b2])
                rt, rtn = self.tmpF()
                P.add("act", I("activation", rt[:, 0:n], self.bank[b2][:, 0:n], AF.Ln, bias=EPS, scale=1.0 / dim),
                      r=[b2], w=[rtn])
                P.add("act", I("activation", rt[:, 0:n], rt[:, 0:n], AF.Exp, scale=-0.5), r=[rtn], w=[rtn])
                for c in range(nch):
                    P.add("dve", I("scalar_tensor_tensor", dst[:, c, t0:t0 + n], dst[:, c, t0:t0 + n],
                                   pp[:, gbase + c:gbase + c + 1], rt[:, 0:n], ALU.mult, ALU.mult),
                          r=[dres, rtn, "pp"], w=[dres])
                if pk == "M1":
                    bank = self.proj(wt, wr, 8, W, 256, 96, hrhs, hres, n)
                    bk = self.bank[bank]
                    qg, qgn = self.tmpF()
                    P.add("act", I("activation", qg[0:96, 0:n], bk[0:96, 0:n], AF.Identity, scale=g_k[0:96, :]),
                          r=[bank, "pp"], w=[qgn])
                    if b > 0:
                        tok0 = t0 - NCTX
                        qb, qbn = self.tmpB()
                        P.add("act", I("activation", qb[0:96, 0:n], qg[0:96, 0:n], AF.Copy), r=[qgn], w=[qbn])
                        b3 = self.gen.next()
                        P.add("pe", I("matmul", self.bank[b3][0:96, 0:n], self.perm_m[0:96, 0:96], qb[0:96, 0:n],
                                      start=True, stop=True), r=[qbn, "cmat"], w=[b3])
                        t2, t2n = self.tmpF()
                        P.add("dve", I("tensor_tensor", t2[64:96, 0:n], self.bank[b3][64:96, 0:n],
                                       self.ropeT[64:96, 1, tok0:tok0 + n], ALU.mult), r=[b3, "rope"], w=[t2n])
                        P.add("pool", I("tensor_tensor", qg[64:96, 0:n], qg[64:96, 0:n],
                                        self.ropeT[64:96, 0, tok0:tok0 + n], ALU.mult), r=[qgn, "rope"], w=[qgn])
                        P.add("pool", I("tensor_tensor", qg[64:96, 0:n], qg[64:96, 0:n], t2[64:96, 0:n], ALU.add),
                              r=[qgn, t2n], w=[qgn])
                    P.add("dve", I("tensor_copy", KRG[64:96, t0:t0 + n], qg[64:96, 0:n]), r=[qgn], w=[krn_[b]])

        import os
        STOP = int(os.environ.get("MLA_STOP", "99"))
        if STOP <= 1:
            return
        KVMODE = os.environ.get("KVMODE", "P")
        for hg in range(8):
            if STOP <= 5 and hg > 0:
                return
            self.gen = self.gen_big
            for b, (t0, n) in enumerate(TB):
                wt, wr = self.load_piece(("M2", l, hg))
                tv = wt[:, 0:512].rearrange("p (k w) -> p k w", k=2)
                kvrhs = [KVCN[:, k, t0:t0 + n] for k in range(2)]
                for hh in range(2):
                    bank = self.proj(wt, wr, 2, 256, hh * 128, 64, kvrhs, [kvn_[b]], n)
                    self.qknorm_rope(bank, 96, n, g_k, self.ones96, 96, None, KTm[0:96, hh, t0:t0 + n], ktn[b],
                                     extra_rows=(64, 96, KRG[64:96, t0:t0 + n], krn_[b]), mode=KVMODE)
                for u in range(n // 128):
                    ch = (t0 + u * 128) // 128
                    vbase = VOFF + ch * VCH
                    bA = self.gen.next()
                    ins = []
                    for k in range(2):
                        rv = tv[:, k, :].rearrange("p (h c) -> p h c", h=2)[:, :, 64:128]
                        ins.append(I("matmul", self.bank[bA][:, 0:128].rearrange("p (h c) -> p h c", h=2),
                                     KVCN[:, k, t0 + u * 128:t0 + (u + 1) * 128], rv, start=(k == 0), stop=(k == 1)))
                    P.add("pe", ins, r=[wr, kvn_[b]], w=[bA])
                    P.add("dve", I("tensor_copy", self.unit_AP(vbase, vbase + 128),
                                   self.bank[bA][:, 0:128].rearrange("p (h c) -> p h c", h=2)),
                          r=[bA], w=[vn[b]])
            self.gen = self.gen_small
            blocks = [b for b in range(5) if not (b == 0 and not cc)]

            def qbuild(b, slot):
                t0, n = TB[b]
                rope = None if b == 0 else (self.perm_m, 64, 96, t0 - NCTX)
                wt, wr = self.load_piece(("M3", l, hg))
                qrhs = [QCN[:, k, t0:t0 + n] for k in range(4)]
                for hh in range(2):
                    bank = self.proj(wt, wr, 4, 192, hh * 96, 96, qrhs, [qn_[b]], n)
                    self.qknorm_rope(bank, 96, n, g_q, self.ones96, 96, rope,
                                     self.QT[0:96, slot * 2 + hh, 0:n], f"QT{slot * 2 + hh}")

            qbuild(blocks[0], 0)
            for bi, b in enumerate(blocks):
                t0, n = TB[b]
                nchunks = 2 if b == 0 else 18
                slot = bi % 2
                if bi + 1 < len(blocks):
                    qbuild(blocks[bi + 1], (bi + 1) % 2)
                odd = (hg % 2 == 1)
                oslot = 0 if odd else 1 + b
                maps = []
                for hh in range(2):
                    def kt(ch, hh=hh):
                        return KTm[0:96, hh, ch * 128:(ch + 1) * 128]
                    if hh == 0:
                        def uf(ch):
                            return V3[:, ch, 0:128]
                        vrows, srows = (0, 64), (64, 128)
                    else:
                        def uf(ch):
                            return V3[:, ch, 64:192]
                        vrows, srows = (64, 128), (0, 64)

                    def done(obanks, vrows=vrows, srows=srows, n=n, oslot=oslot):
                        self.o_norm(obanks[0], vrows, srows, n, self.OT[vrows[0]:vrows[1], oslot, 0:n], f"OT{oslot}")
                    maps.append(dict(kt=kt, kres=lambda ch: ktn[boc(ch)],
                                     q=self.QT[0:96, slot * 2 + hh, 0:n], qres=f"QT{slot * 2 + hh}",
                                     units=[(uf, lambda ch: vn[boc(ch)])], done=done))
                self.attention(maps, n, nchunks, 96 ** -0.5)
                if not odd:
                    continue
                col = 4 if b == 0 else s
                wt, wr = self.load_piece(("M5", l, hg // 2))
                tv5 = wt[:, 0:2048].rearrange("p (k w) -> p k w", k=2)
                for o in range(8):
                    bank = self.gen.next()
                    P.add("pe", [I("matmul", self.bank[bank][:, 0:n], tv5[:, 0, o * 128:(o + 1) * 128], self.OT[:, 1 + b, 0:n],
                                   start=True, stop=False),
                                 I("matmul", self.bank[bank][:, 0:n], tv5[:, 1, o * 128:(o + 1) * 128], self.OT[:, 0, 0:n],
                                   start=False, stop=True)],
                          r=[wr, "OT0", f"OT{1 + b}"], w=[bank])
                    P.add("dve", I("scalar_tensor_tensor", self.xT[:, o, t0:t0 + n], self.bank[bank][:, 0:n],
                                   self.modcol(l, 2, o, col), self.xT[:, o, t0:t0 + n], ALU.mult, ALU.add),
                          r=[bank, "modT", f"x{b}"], w=[f"x{b}"])

    def ffn(self, l, s, cc):
        P = self.P
        HID = self.aview(0, [32, 512])
        hn = [f"hid{j}" for j in range(32)]
        self.set_arena(hn)
        self.gen = self.gen_big
        fblocks = [b for b in range(5) if not (b == 0 and not cc)]
        self.norm_mod(l, s, fblocks[0], 2, fblocks[0] % 2 == 1)
        for bi, b in enumerate(fblocks):
            t0, n = TB[b]
            col = 4 if b == 0 else s
            alt = (b % 2 == 1)
            if bi + 1 < len(fblocks):
                self.norm_mod(l, s, fblocks[bi + 1], 2, not alt)
            hTb, hres = self.hbuf(alt)
            hrhs = [hTb[:, k, 0:n] for k in range(8)]
            for g in range(8):
                wt, wr = self.load_piece(("W1", l, g))
                for c in range(4):
                    j = g * 4 + c
                    bank = self.proj(wt, wr, 8, 512, c * 128, 128, hrhs, hres, n)
                    t, tn = self.tmpF()
                    P.add("act", I("activation", t[:, 0:n], self.bank[bank][:, 0:n], AF.Relu), r=[bank], w=[tn])
                    P.add(os.environ.get("E_FFN", "pool"), I("tensor_tensor", HID[:, j, 0:n], t[:, 0:n], t[:, 0:n], ALU.mult), r=[tn], w=[hn[j]])
            hidrhs = [HID[:, k, 0:n] for k in range(32)]
            for o in range(8):
                wt, wr = self.load_piece(("W2", l, o))
                bank = self.proj(wt, wr, 32, 128, 0, 128, hidrhs, hn, n)
                P.add("dve", I("scalar_tensor_tensor", self.xT[:, o, t0:t0 + n], self.bank[bank][:, 0:n],
                               self.modcol(l, 5, o, col), self.xT[:, o, t0:t0 + n], ALU.mult, ALU.add),
                      r=[bank, "modT", f"x{b}"], w=[f"x{b}"])


def _rope_tables():
    tabs = np.zeros((2, 128, 2, NLAT), np.float32)
    rows = (np.arange(NLAT) // 64).astype(np.float32)
    colsp = (np.arange(NLAT) % 64).astype(np.float32)
    for which, (rot_dim, p0, p1, per) in enumerate([(64, 0, 128, 64), (32, 64, 96, 32)]):
        n_freq = rot_dim // 4
        inv = (10000.0 ** (-np.arange(n_freq, dtype=np.float32) / n_freq)).astype(np.float32)
        ang = np.concatenate([rows[:, None] * inv[None, :], colsp[:, None] * inv[None, :]], axis=1)
        half = rot_dim // 2
        for p in range(p0, p1):
            d = (p - p0) % per
            j = d % half
            sgn = -1.0 if d < half else 1.0
            tabs[which, p, 0, :] = np.cos(ang[:, j])
            tabs[which, p, 1, :] = sgn * np.sin(ang[:, j])
    return tabs.reshape(2, 128, 2 * NLAT)


def _const_mats():
    cm = np.zeros((128, 640), np.float32)
    cm[:, 0:128] = 1.0
    cm[0:64, 128:192] = 1.0
    cm[64:128, 192:256] = 1.0
    cm[0:96, 256:352] = 1.0
    for m in range(128):
        d = m % 64
        k = m + 32 if d < 32 else m - 32
        cm[k, 384 + m] = 1.0
    for m in range(64, 96):
        k = m + 16 if m < 80 else m - 16
        cm[k, 512 + m] = 1.0
    return cm


def _cols(v):
    return np.ascontiguousarray(v.reshape(-1, 128).T)


def _prep_shared(inp, nlayers):
    f = np.float32
    ab_w_in = np.asarray(inp["ab_w_in"], f)
    ab_w_out = np.asarray(inp["ab_w_out"], f)
    bq_perm = np.concatenate([np.arange(1536 + (kv * 4 + g) * 64, 1536 + (kv * 4 + g) * 64 + 64)
                              for g in range(4) for kv in range(2)])
    perm = np.concatenate([np.arange(512, 1024), np.arange(2048, 2176), np.arange(2176, 2304),
                           np.arange(1024, 1536), np.arange(0, 512), bq_perm])
    w_in_e = np.ascontiguousarray(ab_w_in[:, :, perm])
    rperm = np.concatenate([np.arange(0, 512)] +
                           [np.arange(512 + (kv * 4 + g) * 64, 512 + (kv * 4 + g) * 64 + 64)
                            for g in range(4) for kv in range(2)])
    w_out_e = np.ascontiguousarray(ab_w_out[:, rperm, :])
    mla_w_in = np.asarray(inp["mla_w_in"], f)
    w_in_m = np.zeros((2, 1024, 864), f)
    w_in_m[:, :, 0:768] = mla_w_in[:, :, 0:768]
    w_in_m[:, :, 832:864] = mla_w_in[:, :, 768:800]
    pp = np.zeros((128, NPP), f)
    nmix = np.asarray(inp["norm_mix"], f)
    nmlp = np.asarray(inp["norm_mlp"], f)
    adab = np.asarray(inp["ada_b"], f)
    for l in range(4):
        pp[:, PP_NMIX + l * 8:PP_NMIX + l * 8 + 8] = _cols(nmix[l])
        pp[:, PP_NMLP + l * 8:PP_NMLP + l * 8 + 8] = _cols(nmlp[l])
        pp[:, PP_ADAB + l * 48:PP_ADAB + l * 48 + 48] = _cols(adab[l])
    dqk = np.asarray(inp["diff_qk_norm"], f)
    gqk = np.asarray(inp["gqa_qk_norm"], f)
    subln = np.asarray(inp["diff_subln"], f)
    dlam = np.asarray(inp["diff_lambda"], f)
    mq = np.asarray(inp["mla_q_norm"], f)
    mkv = np.asarray(inp["mla_kv_norm"], f)
    mqk = np.asarray(inp["mla_qk_norm"], f)
    for i in range(2):
        pp[:, PP_GE + i * 4 + 0] = np.tile(dqk[i, 0], 2)
        pp[:, PP_GE + i * 4 + 1] = np.tile(dqk[i, 1], 2)
        pp[:, PP_GE + i * 4 + 2] = np.tile(gqk[i, 0], 2)
        pp[:, PP_GE + i * 4 + 3] = np.tile(gqk[i, 1], 2)
        pp[:, PP_SUBLN + i] = subln[i]
        pp[:, PP_DLAM + i * 256:PP_DLAM + (i + 1) * 256] = np.broadcast_to(dlam[i].reshape(1, 256), (128, 256))
        pp[:, PP_MQ + i * 4:PP_MQ + i * 4 + 4] = _cols(mq[i])
        pp[:, PP_MKV + i * 2:PP_MKV + i * 2 + 2] = _cols(mkv[i])
        pp[0:96, PP_MQK + i * 2] = mqk[i, 0]
        pp[0:96, PP_MQK + i * 2 + 1] = mqk[i, 1]
    shared = {
        "ppd": pp, "cstd": _const_mats(), "roped": _rope_tables(),
        "ada_w": np.ascontiguousarray(np.asarray(inp["ada_w"], f)),
        "mlp_w1": np.ascontiguousarray(np.asarray(inp["mlp_w1"], f)),
        "mlp_w2": np.ascontiguousarray(np.asarray(inp["mlp_w2"], f)),
        "w_in_e": w_in_e, "w_out_e": w_out_e, "w_in_m": w_in_m,
        "w_q_up": np.ascontiguousarray(np.asarray(inp["mla_w_q_up"], f)),
        "w_kv_up": np.ascontiguousarray(np.asarray(inp["mla_w_kv_up"], f)),
        "w_out_m": np.ascontiguousarray(np.asarray(inp["mla_w_out"], f)),
    }
    return shared


def _prep_core(inp, seqs):
    f = np.float32
    x = np.asarray(inp["x"], f)
    ctx = np.asarray(inp["ctx"], f)
    c = np.asarray(inp["c"], f)
    c_ctx = np.asarray(inp["c_ctx"], f)
    ns = len(seqs)
    xTd = np.empty((ns, 128, 8, NT), f)
    for j, sidx in enumerate(seqs):
        cat = np.concatenate([ctx[sidx], x[sidx]], axis=0)
        xTd[j] = cat.T.reshape(8, 128, NT).transpose(1, 0, 2)
    cT = np.zeros((128, 8, 5), f)
    for j, sidx in enumerate(seqs):
        cT[:, :, j] = _cols(c[sidx])
    cT[:, :, 4] = _cols(c_ctx)
    return {"xTd": xTd, "cTd": cT}


def _unpack_out(outT):
    ns = outT.shape[0]
    return np.ascontiguousarray(outT.transpose(0, 2, 1, 3).reshape(ns, 1024, NLAT).transpose(0, 2, 1))


_NC_CACHE = {}


def run(inputs, n_cores=N_CORES, nseq=SEQ_PER_CORE, nlayers=DEPTH, trace=False):
    key = (nseq, nlayers)
    if key not in _NC_CACHE:
        _NC_CACHE[key] = Builder(nseq, nlayers).build()
    nc = _NC_CACHE[key]
    shared = _prep_shared(inputs, nlayers)
    in_maps = []
    for cidx in range(n_cores):
        seqs = list(range(cidx * nseq, (cidx + 1) * nseq))
        m = dict(shared)
        m.update(_prep_core(inputs, seqs))
        in_maps.append(m)
    res = run_bass_kernel_spmd(nc, in_maps, core_ids=list(range(n_cores)), trace=trace)
    outs = [_unpack_out(np.asarray(r["outT"])) for r in res.results]
    return np.concatenate(outs, axis=0), res


def kernel(**inputs):
    out, _ = run(inputs)
    return out.astype(np.float32)
```

```python
import math
from contextlib import ExitStack
import numpy as np
import concourse.bass as bass
import concourse.mybir as mybir
from concourse.bass_utils import run_bass_kernel_spmd

F32 = mybir.dt.float32
BF16 = mybir.dt.bfloat16
ALU = mybir.AluOpType
AF = mybir.ActivationFunctionType

D = 1024
NT = 2304
NCTX = 256
NLAT = 2048
DEPTH = 4
EPS = 1e-6
TB = [(0, 256), (256, 512), (768, 512), (1280, 512), (1792, 512)]
PIECE = 4096
NRING = 3
N_CORES = 8
SEQ_PER_CORE = 4

PP_NMIX = 0
PP_NMLP = 32
PP_ADAB = 64
PP_GE = 256
PP_SUBLN = 264
PP_DLAM = 266
PP_MQ = 778
PP_MKV = 786
PP_MQK = 790
NPP = 794


def lambda_init(layer):
    return 0.8 - 0.6 * math.exp(-0.3 * layer)


class Op:
    __slots__ = ("eng", "fn", "deps", "odeps", "signal", "sem", "val", "isdma", "epoch", "idx", "dur", "ndep", "users", "rdy", "st", "fin", "done")


class Prog:
    ENGS = ("pe", "act", "dve", "pool", "sp")
    WIN = 48

    def __init__(self, nc, stack):
        self.nc = nc
        self.stack = stack
        self.ops = {e: [] for e in self.ENGS}
        self.res = {}
        self.epochs = []
        self.dsems = {}
        self.nsem = 0
        self.new_epoch()

    def _sem(self, name):
        self.nsem += 1
        return self.stack.enter_context(self.nc.semaphore(name))

    def new_epoch(self):
        i = len(self.epochs)
        self.epochs.append({e: self._sem(f"e{i}_{e}") for e in self.ENGS})

    def add(self, eng, fn, r=(), w=(), dma=None):
        op = Op()
        op.eng = eng
        op.fn = fn
        op.deps = []
        op.signal = False
        op.isdma = dma is not None
        op.epoch = len(self.epochs) - 1
        op.sem = None
        op.val = 0
        cand = []
        for n in r:
            st = self.res.setdefault(n, [None, []])
            if st[0] is not None:
                cand.append((st[0], "raw"))
        for n in w:
            st = self.res.setdefault(n, [None, []])
            if st[0] is not None:
                cand.append((st[0], "waw"))
            for rd in st[1]:
                cand.append((rd, "war"))
        seen = set()
        op.odeps = []
        op.idx = len(self.ops[eng])
        op.dur = est_dur(eng, fn, op.isdma)
        for d, t in cand:
            if d is op or id(d) in seen:
                continue
            seen.add(id(d))
            if (not d.isdma) and (not op.isdma) and d.eng == eng:
                if eng == "pe":
                    op.odeps.append(d)
                    continue
            d.signal = True
            op.deps.append(d)
        for n in w:
            st = self.res[n]
            st[0] = op
            st[1] = []
        wset = set(w)
        lim = op.idx - self.WIN - 2
        for n in r:
            if n in wset:
                continue
            st = self.res[n]
            if not op.isdma:
                st[1] = [x for x in st[1] if x.isdma or x.eng != eng or x.idx > lim]
            st[1].append(op)
        if dma is not None:
            ent = self.dsems.get(dma)
            if ent is None:
                ent = [self._sem("d_" + dma), 0]
                self.dsems[dma] = ent
            ent[1] += 16
            op.sem = ent[0]
            op.val = ent[1]
        self.ops[eng].append(op)
        return op

    def fence(self, old, new):
        evs = []
        for n in old:
            st = self.res.get(n)
            if st is None:
                continue
            if st[0] is not None:
                evs.append(st[0])
            evs.extend(st[1])
        uniq = []
        seen = set()
        for e in evs:
            if id(e) not in seen:
                seen.add(id(e))
                uniq.append(e)
        for n in new:
            st = self.res.setdefault(n, [None, []])
            st[1] = list(st[1]) + uniq

    def schedule(self):
        import heapq
        W = self.WIN
        ENGS = self.ENGS
        for e in ENGS:
            for op in self.ops[e]:
                op.users = []
                op.ndep = 0
                op.rdy = 0.0
                op.done = False
        for e in ENGS:
            for op in self.ops[e]:
                for d in op.deps:
                    d.users.append(op)
                    op.ndep += 1
                for d in op.odeps:
                    d.users.append(op)
                    op.ndep += 1
        lists = {e: list(self.ops[e]) for e in ENGS}
        head = {e: 0 for e in ENGS}
        free = {e: 0.0 for e in ENGS}
        dma_free = [0.0]
        order = {e: [] for e in ENGS}
        remaining = sum(len(v) for v in lists.values())

        def best_of(e):
            lst = lists[e]
            n = len(lst)
            i = head[e]
            while i < n and lst[i].done:
                i += 1
            head[e] = i
            w = 1 if e == "sp" else W
            best = None
            bst = 0.0
            fr = free[e]
            iend = min(n, i + w)
            while i < iend:
                op = lst[i]
                if not op.done:
                    if op.ndep == 0:
                        st = op.rdy if op.rdy > fr else fr
                        if best is None or st < bst:
                            best = op
                            bst = st
                            if st <= fr:
                                break
                i += 1
            return best, bst

        cache = {e: best_of(e) for e in ENGS}
        while remaining > 0:
            be = None
            bop = None
            bst = 0.0
            for e in ENGS:
                op, st = cache[e]
                if op is None:
                    continue
                if bop is None or st < bst:
                    be, bop, bst = e, op, st
            if bop is None:
                raise RuntimeError("scheduler deadlock")
            bop.done = True
            bop.st = bst
            if bop.isdma:
                free[be] = bst + 60.0
                t0 = max(bst, dma_free[0])
                dma_free[0] = t0 + bop.dur
                fin = t0 + bop.dur + 2000.0
            else:
                fin = bst + bop.dur
                free[be] = fin
            bop.fin = fin
            order[be].append(bop)
            remaining -= 1
            dirty = {be}
            for u in bop.users:
                u.ndep -= 1
                lat = fin + (0.0 if (u.eng == be and not bop.isdma) else 120.0)
                if lat > u.rdy:
                    u.rdy = lat
                if u.ndep == 0:
                    dirty.add(u.eng)
            for e in dirty:
                cache[e] = best_of(e)
        self.ops = order
        self.sim_time = max(free.values())

    def emit(self, final_waits):
        self.schedule()
        for e in self.ENGS:
            cnt = {}
            for op in self.ops[e]:
                if op.isdma:
                    continue
                if op.signal:
                    cnt[op.epoch] = cnt.get(op.epoch, 0) + 1
                    op.val = cnt[op.epoch]
                    op.sem = self.epochs[op.epoch][e]
        ops = self.ops

        def run(e, eng):
            waited = {}
            for op in ops[e]:
                for d in op.deps:
                    k = d.sem.num
                    if waited.get(k, 0) >= d.val:
                        continue
                    eng.wait_ge(d.sem, d.val)
                    waited[k] = d.val
                fns = op.fn if isinstance(op.fn, list) else [op.fn]
                ins = None
                for (mname, a, k) in fns:
                    ins = getattr(eng, mname)(*a, **k)
                if op.isdma:
                    ins.then_inc(op.sem, 16)
                elif op.signal:
                    ins.then_inc(op.sem, 1)
            if e == "sp":
                for d in final_waits:
                    eng.wait_ge(d.sem, d.val)

        with self.nc.Block() as block:
            @block.tensor
            def _(eng):
                run("pe", eng)

            @block.scalar
            def _(eng):
                run("act", eng)

            @block.vector
            def _(eng):
                run("dve", eng)

            @block.gpsimd
            def _(eng):
                run("pool", eng)

            @block.sync
            def _(eng):
                run("sp", eng)


def _free_elems(ap):
    n = 1
    for d in ap.shape[1:]:
        n *= int(d)
    return n


def est_dur(eng, fn, isdma):
    fns = fn if isinstance(fn, list) else [fn]
    tot = 0.0
    for (mname, a, k) in fns:
        out = k.get("out", a[0] if a else None)
        fe = _free_elems(out) if out is not None else 512
        if isdma:
            bpe = 2 if out.dtype == BF16 else 4
            tot += fe * int(out.shape[0]) * bpe / 180.0
        elif eng == "pe":
            tot += 45.0 + 0.41 * max(fe, 64)
        elif eng == "act":
            tot += 220.0 + 0.83 * fe
        elif eng == "dve":
            tot += (300.0 + 5.4 * fe) if mname == "reciprocal" else (250.0 + 1.04 * fe)
        else:
            tot += 300.0 + 1.45 * fe
    return tot


def I(mname, *a, **k):
    return (mname, a, k)


class Rot:
    def __init__(self, items):
        self.items = list(items)
        self.i = 0

    def next(self):
        it = self.items[self.i % len(self.items)]
        self.i += 1
        return it


import os
class Builder:
    def __init__(self, nseq, nlayers):
        self.nseq = nseq
        self.nlayers = nlayers
        self.nc = bass.Bass("TRN2", target_bir_lowering=False)
        self.pieces = []

    def dram_in(self, name, shape, dt=F32):
        return self.nc.dram_tensor(name, list(shape), dt, kind="ExternalInput").ap()

    def add_piece(self, src, length):
        self.pieces.append((src, length))
        return len(self.pieces) - 1

    def build(self):
        nc = self.nc
        nseq, nlayers = self.nseq, self.nlayers
        with ExitStack() as stack:
            self.stack = stack
            P = self.P = Prog(nc, stack)
            xTd = self.dram_in("xTd", [nseq, 128, 8, NT])
            cTd = self.dram_in("cTd", [128, 8, 5])
            ppd = self.dram_in("ppd", [128, NPP])
            cstd = self.dram_in("cstd", [128, 640])
            roped = self.dram_in("roped", [2, 128, 4096])
            ada_w = self.dram_in("ada_w", [4, 1024, 6144])
            mlp_w1 = self.dram_in("mlp_w1", [4, 1024, 4096])
            mlp_w2 = self.dram_in("mlp_w2", [4, 4096, 1024])
            w_in_e = self.dram_in("w_in_e", [2, 1024, 2304])
            w_out_e = self.dram_in("w_out_e", [2, 1024, 1024])
            w_in_m = self.dram_in("w_in_m", [2, 1024, 864])
            w_q_up = self.dram_in("w_q_up", [2, 512, 1536])
            w_kv_up = self.dram_in("w_kv_up", [2, 256, 2048])
            w_out_m = self.dram_in("w_out_m", [2, 1024, 1024])
            outT = nc.dram_tensor("outT", [nseq, 128, 8, NLAT], F32, kind="ExternalOutput").ap()

            def kmaj(w2d, c0, W):
                return w2d[:, c0:c0 + W].rearrange("(k p) w -> p k w", p=128)

            pc = {}
            self.ada_pids = {}
            pc["cst"] = self.add_piece(cstd[:, :], 640)
            pc["rope_e"] = self.add_piece(roped[0], 4096)
            pc["rope_m"] = self.add_piece(roped[1], 4096)
            for l in range(nlayers):
                for g in range(12):
                    pc[("ada", l, g)] = self.add_piece(kmaj(ada_w[l], g * 512, 512), 4096)
                    self.ada_pids[pc[("ada", l, g)]] = (l, g)
            for l in range(nlayers):
                i = l // 2
                if l % 2 == 0:
                    pc[("E0", l)] = self.add_piece(kmaj(w_in_e[i], 0, 512), 4096)
                    pc[("E1", l)] = self.add_piece(kmaj(w_in_e[i], 512, 256), 2048)
                    pc[("E2", l)] = self.add_piece(kmaj(w_in_e[i], 768, 512), 4096)
                    pc[("E3", l)] = self.add_piece(kmaj(w_in_e[i], 1280, 512), 4096)
                    pc[("E4", l)] = self.add_piece(kmaj(w_in_e[i], 1792, 512), 4096)
                    pc[("E5", l)] = self.add_piece(kmaj(w_out_e[i], 0, 512), 4096)
                    pc[("E6", l)] = self.add_piece(kmaj(w_out_e[i], 512, 512), 4096)
                else:
                    pc[("M0", l)] = self.add_piece(kmaj(w_in_m[i], 0, 512), 4096)
                    pc[("M1", l)] = self.add_piece(kmaj(w_in_m[i], 512, 352), 2816)
                    for hg in range(8):
                        pc[("M2", l, hg)] = self.add_piece(kmaj(w_kv_up[i], hg * 256, 256), 512)
                        pc[("M3", l, hg)] = self.add_piece(kmaj(w_q_up[i], hg * 192, 192), 768)
                    for hp in range(4):
                        pc[("M5", l, hp)] = self.add_piece(kmaj(w_out_m[i, hp * 256:(hp + 1) * 256, :], 0, 1024), 2048)
                for g in range(8):
                    pc[("W1", l, g)] = self.add_piece(kmaj(mlp_w1[l], g * 512, 512), 4096)
                for o in range(8):
                    pc[("W2", l, o)] = self.add_piece(kmaj(mlp_w2[l], o * 128, 128), 4096)
            self.pc = pc
            npieces = len(self.pieces)
            wbf = nc.dram_tensor("wbf", [npieces, 128, PIECE], BF16).ap()
            self.wbf = wbf

            def sb(name, shape, dt):
                return stack.enter_context(nc.sbuf_tensor(name, list(shape), dt))

            def ps(name, shape):
                return stack.enter_context(nc.psum_tensor(name, list(shape), F32))

            xT = self.xT = sb("xT", [128, 8, NT], F32)
            ARENA = 24192
            self.arena = sb("arena", [128, ARENA], BF16)
            self.ring = [sb(f"ring{i}", [128, PIECE], BF16) for i in range(NRING)]
            self.ropeT = sb("ropeT", [128, 2, 2048], BF16)
            self.hT = sb("hT", [128, 8, 512], BF16)
            self.QT = sb("QT", [128, 8, 512], BF16)
            self.OT = sb("OT", [128, 8, 512], BF16)
            self.PT = sb("PT", [128, 3, 1024], BF16)
            self.NTF = 6
            self.tmpf = sb("tmpf", [128, self.NTF, 512], F32)
            self.tmpb = sb("tmpb", [128, 4, 512], BF16)
            pp = self.pp = sb("pp", [128, NPP], F32)
            self.modT = sb("modT", [128, 4, 48, 5], F32)
            cmat = self.cmat = sb("cmat", [128, 640], BF16)
            cT = sb("cT", [128, 8, 5], F32)
            scT = self.scT = sb("scT", [128, 8, 5], BF16)
            self.cols = sb("cols", [128, 32], F32)
            self.lamc = sb("lamc", [128, 16], F32)

            self.ones128 = cmat[:, 0:128]
            self.blk64 = cmat[:, 128:256]
            self.ones96 = cmat[:, 256:384]
            self.perm_e = cmat[:, 384:512]
            self.perm_m = cmat[:, 512:640]

            g0 = ps("g0", [128, 512]); g1 = ps("g1", [128, 512])
            s0 = ps("s0", [128, 1024]); s1 = ps("s1", [128, 1024])
            o0 = ps("o0", [128, 512]); o1 = ps("o1", [128, 512])
            self.bank = {"g0": g0[:, :], "g1": g1[:, :], "o0": o0[:, :], "o1": o1[:, :],
                         "s0a": s0[:, 0:512], "s0b": s0[:, 512:1024],
                         "s1a": s1[:, 0:512], "s1b": s1[:, 512:1024]}
            self.s_tiles = [(s0, ["s0a", "s0b"]), (s1, ["s1a", "s1b"])]
            self.gen_small = Rot(["g0", "g1"])
            self.gen_big = Rot(["g0", "g1", "o0", "o1", "s0a", "s0b", "s1a", "s1b"])
            self.gen = self.gen_big
            self.tf = Rot(list(range(self.NTF)))
            self.tb = Rot(list(range(4)))
            self.ptr = Rot(list(range(3)))
            self.ring_i = 0
            self.arena_names = []

            def dma_simple(out_ap, in_ap, r, w, key):
                return P.add("sp", I("dma_start", out=out_ap, in_=in_ap), r=r, w=w, dma=key)

            dma_simple(pp[:, :], ppd[:, :], [], ["pp"], "c_pp")
            dma_simple(cT[:, :, :], cTd[:, :, :], [], ["cT"], "c_cT")

            P.add("act", I("activation", scT[:, :, :], cT[:, :, :], AF.Silu), r=["cT"], w=["scT"])
            self.prepass()

            dma_simple(cmat[:, :], wbf[pc["cst"], :, 0:640], [f"wbf{pc['cst']}"], ["cmat"], "c_cm")

            self.preamble(cT, scT)

            store_ops = []
            for s in range(nseq):
                if s > 0:
                    P.new_epoch()
                for b, (t0, n) in enumerate(TB):
                    dma_simple(xT[:, :, t0:t0 + n], xTd[s, :, :, t0:t0 + n], [], [f"x{b}"], f"xl{b}")
                for l in range(nlayers):
                    if l == 2:
                        P.new_epoch()
                    cc = l < DEPTH - 1
                    self.layer_cols(l, s)
                    if l % 2 == 0:
                        self.even_layer(l, s, cc)
                    else:
                        self.mla_layer(l, s, cc)
                    self.ffn(l, s, cc)
                for b in range(1, 5):
                    t0, n = TB[b]
                    op = dma_simple(outT[s, :, :, t0 - NCTX:t0 - NCTX + n], xT[:, :, t0:t0 + n],
                                    [f"x{b}"], [], f"xs{b}")
                    store_ops.append(op)
            P.emit(store_ops)
        return nc

    def set_arena(self, names):
        self.P.fence(self.arena_names, names)
        self.arena_names = list(names)

    def aview(self, off, shape):
        n = int(np.prod(shape))
        ap = self.arena[:, off:off + n]
        if len(shape) == 2:
            return ap.rearrange("p (a b) -> p a b", a=shape[0])
        return ap

    def tmpF(self):
        i = self.tf.next()
        return self.tmpf[:, i, :], f"tf{i}"

    def tmpB(self):
        i = self.tb.next()
        return self.tmpb[:, i, :], f"tb{i}"

    def load_piece(self, key):
        pid = self.pc[key]
        ln = self.pieces[pid][1]
        slot = self.ring_i % NRING
        self.ring_i += 1
        t = self.ring[slot]
        self.P.add("sp", I("dma_start", out=t[:, 0:ln], in_=self.wbf[pid, :, 0:ln]),
                   r=[f"wbf{pid}"], w=[f"ring{slot}"], dma=f"ring{slot}")
        return t, f"ring{slot}"

    def prepass(self):
        P = self.P
        xflat = self.xT[:, :, :].rearrange("p a b -> p (a b)")
        import os
        NST = int(os.environ.get('NST', '4'))
        cast_engs = Rot(["dve", "act", "pool"])
        stg_names = [f"stg{i}" for i in range(NST)]
        cb_names = [f"cb{i}" for i in range(NST)]
        for pid, (src, ln) in enumerate(self.pieces):
            sl = pid % NST
            stg = xflat[:, sl * PIECE: sl * PIECE + ln]
            cb = self.arena[:, sl * PIECE: sl * PIECE + ln]
            if len(src.shape) == 3:
                stg_v = stg.rearrange("p (k w) -> p k w", k=src.shape[1])
            else:
                stg_v = stg
            P.add("sp", I("dma_start", out=stg_v, in_=src), r=[], w=[stg_names[sl]], dma=f"stl{sl}")
            ce = cast_engs.next()
            if ce == "act":
                P.add("act", I("activation", cb, stg, AF.Copy), r=[stg_names[sl]], w=[cb_names[sl]])
            else:
                P.add(ce, I("tensor_copy", cb, stg), r=[stg_names[sl]], w=[cb_names[sl]])
            if pid in self.ada_pids:
                l_, g_ = self.ada_pids[pid]
                bank = ["g0", "g1", "o0", "o1"][l_]
                bk = self.bank[bank]
                cb_v = cb.rearrange("p (k w) -> p k w", k=8)
                ins = []
                for c in range(4):
                    j = g_ * 4 + c
                    for k in range(8):
                        ins.append(I("matmul", bk[:, j * 5:j * 5 + 5], cb_v[:, k, c * 128:(c + 1) * 128],
                                     self.scT[:, k, :], start=(k == 0), stop=(k == 7)))
                P.add("pe", ins, r=[cb_names[sl], "scT"], w=[bank])
                if g_ == 11:
                    ins = []
                    for j in range(48):
                        ins.append(I("tensor_scalar", self.modT[:, l_, j, :], bk[:, j * 5:j * 5 + 5],
                                     self.pp[:, PP_ADAB + l_ * 48 + j:PP_ADAB + l_ * 48 + j + 1], None, ALU.add))
                    P.add("dve", ins, r=[bank, "pp"], w=["modT"])
                continue
            P.add("sp", I("dma_start", out=self.wbf[pid, :, 0:ln], in_=cb),
                  r=[cb_names[sl]], w=[f"wbf{pid}"], dma=f"sts{sl}")
        P.fence(stg_names, [f"x{b}" for b in range(5)])
        self.arena_names = cb_names

    def preamble(self, cT, scT):
        P = self.P
        pp, lamc, modT = self.pp, self.lamc, self.modT
        nlayers = self.nlayers
        X = mybir.AxisListType.X
        for i in range(2):
            l = 2 * i
            if l >= nlayers:
                continue
            li = lambda_init(l)
            base = PP_DLAM + i * 256
            tA, nA = self.tmpF()
            P.add("dve", [I("tensor_tensor", tA[:, 0:64], pp[:, base:base + 64], pp[:, base + 64:base + 128], ALU.mult),
                          I("tensor_tensor", tA[:, 64:128], pp[:, base + 128:base + 192], pp[:, base + 192:base + 256], ALU.mult)],
                  r=["pp"], w=[nA])
            P.add("dve", [I("reduce_sum", lamc[:, 8 + 2 * i:9 + 2 * i], tA[:, 0:64], X),
                          I("reduce_sum", lamc[:, 9 + 2 * i:10 + 2 * i], tA[:, 64:128], X)],
                  r=[nA], w=["lamc_s"])
            P.add("act", I("activation", lamc[:, 12 + 2 * i:14 + 2 * i], lamc[:, 8 + 2 * i:10 + 2 * i], AF.Exp),
                  r=["lamc_s"], w=["lamc_e"])
            P.add("dve", I("tensor_tensor", lamc[:, 4 + i:5 + i], lamc[:, 13 + 2 * i:14 + 2 * i],
                           lamc[:, 12 + 2 * i:13 + 2 * i], ALU.subtract), r=["lamc_e"], w=["lamc_d"])
            P.add("dve", I("tensor_scalar", lamc[:, i:i + 1], lamc[:, 4 + i:5 + i], -li, None, ALU.add),
                  r=["lamc_d"], w=["lamc"])
            P.add("dve", I("tensor_scalar", lamc[:, 2 + i:3 + i], pp[:, PP_SUBLN + i:PP_SUBLN + i + 1], 1.0 - li, None, ALU.mult),
                  r=["pp"], w=["lamc"])

    def layer_cols(self, l, s):
        pp, modT, cols = self.pp, self.modT, self.cols
        specs = [(0, 8, s, PP_NMIX), (8, 8, 4, PP_NMIX), (16, 32, s, PP_NMLP), (24, 32, 4, PP_NMLP)]
        ins = []
        for (co, j0, col, nb) in specs:
            ins.append(I("scalar_tensor_tensor", cols[:, co:co + 8], modT[:, l, j0:j0 + 8, col], 1.0,
                         pp[:, nb + l * 8: nb + l * 8 + 8], ALU.add, ALU.mult))
        self.P.add("dve", ins, r=["modT", "pp"], w=["cols"])

    def modcol(self, l, m, k, col):
        return self.modT[:, l, m * 8 + k, col:col + 1]

    def hbuf(self, alt):
        if alt:
            return self.QT, [f"QT{c}" for c in range(8)]
        return self.hT, ["hT"]

    def norm_mod(self, l, s, b, which, alt=False):
        P = self.P
        t0, n = TB[b]
        isctx = (b == 0)
        col = 4 if isctx else s
        a_off = (0 if which == 1 else 16) + (8 if isctx else 0)
        m_shift = 0 if which == 1 else 3
        xT = self.xT
        hT, hres = self.hbuf(alt)
        xr = f"x{b}"
        P.add("act", I("activation", hT[:, :, 0:n], xT[:, :, t0:t0 + n], AF.Square), r=[xr], w=hres)
        bank = self.gen.next()
        bk = self.bank[bank]
        P.add("pe", [I("matmul", bk[:, 0:n], self.ones128, hT[:, k, 0:n], start=(k == 0), stop=(k == 7))
                     for k in range(8)], r=hres + ["cmat"], w=[bank])
        rt, rtn = self.tmpF()
        P.add("act", I("activation", rt[:, 0:n], bk[:, 0:n], AF.Ln, bias=EPS, scale=1.0 / D), r=[bank], w=[rtn])
        P.add("act", I("activation", rt[:, 0:n], rt[:, 0:n], AF.Exp, scale=-0.5), r=[rtn], w=[rtn])
        tks = [self.tmpF(), self.tmpF()]
        for k in range(8):
            tk, tkn = tks[k % 2]
            P.add("dve", I("scalar_tensor_tensor", tk[:, 0:n], xT[:, k, t0:t0 + n],
                           self.cols[:, a_off + k:a_off + k + 1], rt[:, 0:n], ALU.mult, ALU.mult),
                  r=[xr, "cols", rtn], w=[tkn])
            P.add("act", I("activation", hT[:, k, 0:n], tk[:, 0:n], AF.Identity,
                           bias=self.modcol(l, m_shift, k, col), scale=1.0), r=[tkn, "modT"], w=hres)

    def proj(self, wt, wr, nk, W, c0, M, rhs_list, rhs_res, n):
        bank = self.gen.next()
        bk = self.bank[bank]
        tv = wt[:, 0:nk * W].rearrange("p (k w) -> p k w", k=nk)
        self.P.add("pe", [I("matmul", bk[0:M, 0:n], tv[:, k, c0:c0 + M], rhs_list[k], start=(k == 0), stop=(k == nk - 1))
                          for k in range(nk)], r=[wr] + list(rhs_res), w=[bank])
        return bank

    def qknorm_rope(self, bank, R, n, gain, ones_mat, dim, rope, out_ap, out_res, extra_rows=None, mode="P"):
        P = self.P
        bk = self.bank[bank]
        qg, qgn = self.tmpF()
        Rp = R if extra_rows is None else extra_rows[0]
        if mode == "A":
            P.add("act", I("activation", qg[0:Rp, 0:n], bk[0:Rp, 0:n], AF.Identity, scale=gain[0:Rp, :]),
                  r=[bank, "pp"], w=[qgn])
        else:
            P.add("dve", I("tensor_scalar", qg[0:Rp, 0:n], bk[0:Rp, 0:n], gain[0:Rp, :], None, ALU.mult),
                  r=[bank, "pp"], w=[qgn])
        if extra_rows is not None:
            r0, r1, src, srcres = extra_rows
            P.add("pool", I("tensor_copy", qg[r0:r1, 0:n], src), r=[srcres], w=[qgn])
        sq, sqn = self.tmpB()
        if mode == "A":
            P.add("act", I("activation", sq[0:R, 0:n], qg[0:R, 0:n], AF.Square), r=[qgn], w=[sqn])
        else:
            P.add("pool", I("tensor_tensor", sq[0:R, 0:n], qg[0:R, 0:n], qg[0:R, 0:n], ALU.mult), r=[qgn], w=[sqn])
        b2 = self.gen.next()
        bk2 = self.bank[b2]
        P.add("pe", I("matmul", bk2[0:R, 0:n], ones_mat[0:R, 0:R], sq[0:R, 0:n], start=True, stop=True),
              r=[sqn, "cmat"], w=[b2])
        rt, rtn = self.tmpF()
        P.add("act", I("activation", rt[0:R, 0:n], bk2[0:R, 0:n], AF.Ln, bias=EPS, scale=1.0 / dim), r=[b2], w=[rtn])
        P.add("act", I("activation", rt[0:R, 0:n], rt[0:R, 0:n], AF.Exp, scale=-0.5), r=[rtn], w=[rtn])
        if rope is not None:
            perm, r0, r1, tok0 = rope
            qb, qbn = self.tmpB()
            P.add("dve" if mode == "A" else "pool", I("tensor_copy", qb[0:R, 0:n], qg[0:R, 0:n]), r=[qgn], w=[qbn])
            b3 = self.gen.next()
            bk3 = self.bank[b3]
            P.add("pe", I("matmul", bk3[0:R, 0:n], perm[0:R, 0:R], qb[0:R, 0:n], start=True, stop=True),
                  r=[qbn, "cmat"], w=[b3])
            cosT = self.ropeT[r0:r1, 0, tok0:tok0 + n]
            sinT = self.ropeT[r0:r1, 1, tok0:tok0 + n]
            t2, t2n = self.tmpF()
            P.add("dve", I("tensor_tensor", t2[r0:r1, 0:n], bk3[r0:r1, 0:n], sinT, ALU.mult), r=[b3, "rope"], w=[t2n])
            P.add("pool", I("tensor_tensor", qg[r0:r1, 0:n], qg[r0:r1, 0:n], cosT, ALU.mult), r=[qgn, "rope"], w=[qgn])
            P.add("pool", I("tensor_tensor", qg[r0:r1, 0:n], qg[r0:r1, 0:n], t2[r0:r1, 0:n], ALU.add),
                  r=[qgn, t2n], w=[qgn])
        P.add("dve", I("tensor_tensor", out_ap, qg[0:R, 0:n], rt[0:R, 0:n], ALU.mult), r=[qgn, rtn], w=[out_res])

    def load_rope(self, which):
        pid = self.pc[which]
        self.P.add("sp", I("dma_start", out=self.ropeT[:, :, :].rearrange("p a b -> p (a b)"), in_=self.wbf[pid, :, :]),
                   r=[f"wbf{pid}"], w=["rope"], dma="rope")

    def attention(self, maps, n, nchunks, scale):
        P = self.P
        npairs = nchunks // 2
        units = []
        for mi, m in enumerate(maps):
            for p in range(npairs):
                units.append((mi, p))
        opool = Rot(["o0", "o1"])
        state = {}

        def rec_S(u):
            mi, p = units[u]
            m = maps[mi]
            st, snames = self.s_tiles[u % 2]
            ins = []
            for h in range(2):
                ch = 2 * p + h
                ins.append(I("matmul", st[:, h * 512:h * 512 + n], m["kt"](ch), m["q"], start=True, stop=True))
            P.add("pe", ins, r=[m["kres"](2 * p), m["kres"](2 * p + 1), m["qres"]], w=snames)

        def rec_exp(u):
            st, snames = self.s_tiles[u % 2]
            pi = self.ptr.next()
            state[u] = pi
            src = st[:, :].rearrange("p (h c) -> p h c", h=2)[:, :, 0:n]
            dst = self.PT[:, pi, :].rearrange("p (h c) -> p h c", h=2)[:, :, 0:n]
            P.add("act", I("activation", dst, src, AF.Exp, scale=scale), r=snames, w=[f"pt{pi}"])

        def rec_PV(u):
            mi, p = units[u]
            m = maps[mi]
            pi = state[u]
            if p == 0:
                m["obanks"] = [opool.next() for _ in m["units"]]
            obanks = m["obanks"]
            ins = []
            for h in range(2):
                ch = 2 * p + h
                for ui, (lf, _) in enumerate(m["units"]):
                    ins.append(I("matmul", self.bank[obanks[ui]][:, 0:n], lf(ch), self.PT[:, pi, h * 512:h * 512 + n],
                                 start=(ch == 0), stop=(ch == nchunks - 1)))
            rr = [f"pt{pi}"]
            for (_, vf) in m["units"]:
                rr += [vf(2 * p), vf(2 * p + 1)]
            P.add("pe", ins, r=rr, w=list(obanks))
            if p == npairs - 1:
                m["done"](obanks)
                if m.get("after") is not None:
                    m["after"]()

        nu = len(units)
        PVD = int(os.environ.get("PVD", "1"))
        rec_S(0)
        for u in range(nu):
            if u + 1 < nu:
                rec_S(u + 1)
            rec_exp(u)
            if u - PVD >= 0:
                rec_PV(u - PVD)
        for u in range(max(0, nu - PVD), nu):
            rec_PV(u)

    def o_norm(self, obank, vrows, srows, n, out_ap, out_res, scratch=None):
        P = self.P
        bk = self.bank[obank]
        v0, v1 = vrows
        if scratch is None:
            t, tn = self.tmpF()
            tap = t[v0:v1, 0:n]
        else:
            tap, tn = scratch
        P.add("dve", I("reciprocal", tap, bk[srows[0]:srows[1], 0:n]), r=[obank], w=[tn])
        P.add("dve", I("tensor_tensor", out_ap, bk[v0:v1, 0:n], tap, ALU.mult), r=[obank, tn], w=[out_res])

    @staticmethod
    def blk_of_chunk(ch):
        t = ch * 128
        for b, (t0, n) in enumerate(TB):
            if t0 <= t < t0 + n:
                return b

    def unit_AP(self, first_off, second_off):
        ARW = self.arena.shape[1]
        return bass.AP(self.arena, first_off, [[ARW, 128], [second_off - first_off, 2], [1, 64]])

    def even_layer(self, l, s, cc):
        P = self.P
        i = l // 2
        pp = self.pp
        KT = self.aview(0, [5, NT])
        VOFF = 5 * NT
        VCH = 704
        Vt = self.arena
        V3 = self.arena[:, VOFF:VOFF + 18 * VCH].rearrange("p (c w) -> p c w", c=18)
        ktn = [f"KT{b}" for b in range(5)]
        vn = [f"V{b}" for b in range(5)]
        self.set_arena(ktn + vn)
        self.load_rope("rope_e")
        P.add("pool", I("memset", V3[:, :, 576:640], 1.0), r=[], w=vn)
        g_aq = pp[:, PP_GE + i * 4 + 0:PP_GE + i * 4 + 1]
        g_ak = pp[:, PP_GE + i * 4 + 1:PP_GE + i * 4 + 2]
        g_bq = pp[:, PP_GE + i * 4 + 2:PP_GE + i * 4 + 3]
        g_bk = pp[:, PP_GE + i * 4 + 3:PP_GE + i * 4 + 4]
        boc = self.blk_of_chunk

        self.gen = self.gen_big
        self.norm_mod(l, s, 0, 1, False)
        for b, (t0, n) in enumerate(TB):
            alt = (b % 2 == 1)
            if b + 1 < 5:
                self.norm_mod(l, s, b + 1, 1, not alt)
            hTb, hres = self.hbuf(alt)
            rope = None if b == 0 else (self.perm_e, 0, 128, t0 - NCTX)
            hrhs = [hTb[:, k, 0:n] for k in range(8)]
            wt, wr = self.load_piece(("E0", l))
            for c in range(4):
                bank = self.proj(wt, wr, 8, 512, c * 128, 128, hrhs, hres, n)
                self.qknorm_rope(bank, 128, n, g_ak, self.blk64, 64, rope, KT[:, c, t0:t0 + n], ktn[b], mode="A")
            wt1, wr1 = self.load_piece(("E1", l))
            bank = self.proj(wt1, wr1, 8, 256, 0, 128, hrhs, hres, n)
            self.qknorm_rope(bank, 128, n, g_bk, self.blk64, 64, rope, KT[:, 4, t0:t0 + n], ktn[b], mode="A")
            wt2, wr2 = self.load_piece(("E2", l))
            tv1 = wt1[:, 0:2048].rearrange("p (k w) -> p k w", k=8)
            tv2 = wt2[:, 0:4096].rearrange("p (k w) -> p k w", k=8)
            for u in range(n // 128):
                ch = (t0 + u * 128) // 128
                vbase = VOFF + ch * VCH
                bA = self.gen.next()
                bB = self.gen.next()
                ins = []
                for k in range(8):
                    ins.append(I("matmul", self.bank[bA][:, 0:512], hTb[:, k, u * 128:(u + 1) * 128], tv2[:, k, :],
                                 start=(k == 0), stop=(k == 7)))
                for k in range(8):
                    ins.append(I("matmul", self.bank[bB][:, 0:128], hTb[:, k, u * 128:(u + 1) * 128], tv1[:, k, 128:256],
                                 start=(k == 0), stop=(k == 7)))
                P.add("pe", ins, r=[wr1, wr2] + hres, w=[bA, bB])
                P.add("dve", I("tensor_copy", Vt[:, vbase:vbase + 512], self.bank[bA][:, 0:512]), r=[bA], w=[vn[b]])
                P.add("act", I("activation", self.unit_AP(vbase + 512, vbase + 640),
                               self.bank[bB][:, 0:128].rearrange("p (h c) -> p h c", h=2), AF.Copy),
                      r=[bB], w=[vn[b]])

        self.gen = self.gen_small
        neglam = self.lamc[:, i:i + 1]
        sublnp = self.lamc[:, 2 + i:3 + i]
        blocks = [b for b in range(5) if not (b == 0 and not cc)]
        qstate = {}

        def qbuild_chunk(b, cc_):
            t0, n = TB[b]
            rope = None if b == 0 else (self.perm_e, 0, 128, t0 - NCTX)
            hrhs = [self.hT[:, k, 0:n] for k in range(8)]
            if cc_ % 4 == 0:
                qstate["w"] = self.load_piece(("E3" if cc_ < 4 else "E4", l))
            wt, wr = qstate["w"]
            gq = g_aq if cc_ < 4 else g_bq
            bank = self.proj(wt, wr, 8, 512, (cc_ % 4) * 128, 128, hrhs, ["hT"], n)
            self.qknorm_rope(bank, 128, n, gq, self.blk64, 64, rope, self.QT[:, cc_, 0:n], f"QT{cc_}")

        self.norm_mod(l, s, blocks[0], 1)
        for c_ in range(8):
            qbuild_chunk(blocks[0], c_)
        for bi, b in enumerate(blocks):
            t0, n = TB[b]
            nchunks = 2 if b == 0 else 18
            nxt = blocks[bi + 1] if bi + 1 < len(blocks) else None
            if nxt is not None:
                self.norm_mod(l, s, nxt, 1)
            maps = []
            dtmp = {}
            for h in range(4):
                for m in range(2):
                    rows = (64 * m, 64 * m + 64)

                    def kt(ch, h=h, rows=rows):
                        return KT[rows[0]:rows[1], h, ch * 128:(ch + 1) * 128]

                    def uA(ch, h=h):
                        return V3[:, ch, h * 128:(h + 1) * 128]

                    def uB(ch):
                        return self.ones128

                    def done(obanks, h=h, m=m, n=n):
                        d, dn = self.tmpF()
                        dtmp[(h, m)] = (d, dn)
                        P.add("act", I("activation", d[:, 0:n], self.bank[obanks[1]][:, 0:n], AF.Ln), r=[obanks[1]], w=[dn])
                        P.add("act", I("activation", d[:, 0:n], d[:, 0:n], AF.Exp, scale=-1.0), r=[dn], w=[dn])
                        P.add("dve", I("tensor_tensor", d[:, 0:n], self.bank[obanks[0]][:, 0:n], d[:, 0:n], ALU.mult),
                              r=[obanks[0], dn], w=[dn])
                        if m == 1:
                            d0, d0n = dtmp[(h, 0)]
                            P.add("dve", I("scalar_tensor_tensor", d0[:, 0:n], d[:, 0:n], neglam, d0[:, 0:n], ALU.mult, ALU.add),
                                  r=[dn, d0n, "lamc"], w=[d0n])
                            sq, sqn = self.tmpB()
                            P.add("act", I("activation", sq[:, 0:n], d0[:, 0:n], AF.Square), r=[d0n], w=[sqn])
                            b2 = self.gen.next()
                            P.add("pe", I("matmul", self.bank[b2][:, 0:n], self.ones128, sq[:, 0:n], start=True, stop=True),
                                  r=[sqn, "cmat"], w=[b2])
                            rt, rtn = self.tmpF()
                            P.add("act", I("activation", rt[:, 0:n], self.bank[b2][:, 0:n], AF.Ln, bias=EPS, scale=1.0 / 128),
                                  r=[b2], w=[rtn])
                            P.add("act", I("activation", rt[:, 0:n], rt[:, 0:n], AF.Exp, scale=-0.5), r=[rtn], w=[rtn])
                            P.add("dve", I("scalar_tensor_tensor", self.OT[:, h, 0:n], d0[:, 0:n], sublnp, rt[:, 0:n],
                                           ALU.mult, ALU.mult), r=[d0n, rtn, "lamc"], w=[f"OT{h}"])
                    after = None
                    if m == 1 and nxt is not None:
                        after = (lambda nxt=nxt, h=h: qbuild_chunk(nxt, h))
                    maps.append(dict(kt=kt, kres=lambda ch: ktn[boc(ch)],
                                     q=self.QT[rows[0]:rows[1], h, 0:n], qres=f"QT{h}",
                                     units=[(uA, lambda ch: vn[boc(ch)]), (uB, lambda ch: "cmat")],
                                     done=done, after=after))
            self.attention(maps, n, nchunks, 64 ** -0.5)
            maps = []
            for g in range(4):
                for kv in range(2):
                    rows = (64 * kv, 64 * kv + 64)

                    def kt(ch, rows=rows):
                        return KT[rows[0]:rows[1], 4, ch * 128:(ch + 1) * 128]
                    if kv == 0:
                        def uf(ch):
                            return V3[:, ch, 512:640]
                        vrows, srows = (0, 64), (64, 128)
                    else:
                        def uf(ch):
                            return V3[:, ch, 576:704]
                        vrows, srows = (64, 128), (0, 64)

                    def done(obanks, g=g, vrows=vrows, srows=srows, n=n):
                        self.o_norm(obanks[0], vrows, srows, n, self.OT[vrows[0]:vrows[1], 4 + g, 0:n], f"OT{4 + g}")
                    after = None
                    if kv == 1 and nxt is not None:
                        after = (lambda nxt=nxt, g=g: qbuild_chunk(nxt, 4 + g))
                    maps.append(dict(kt=kt, kres=lambda ch: ktn[boc(ch)],
                                     q=self.QT[rows[0]:rows[1], 4 + g, 0:n], qres=f"QT{4 + g}",
                                     units=[(uf, lambda ch: vn[boc(ch)])], done=done, after=after))
            self.attention(maps, n, nchunks, 64 ** -0.5)
            col = 4 if b == 0 else s
            orhs = [self.OT[:, k, 0:n] for k in range(8)]
            ores = [f"OT{k}" for k in range(8)]
            for half, pk in enumerate(["E5", "E6"]):
                wt, wr = self.load_piece((pk, l))
                for c in range(4):
                    o = half * 4 + c
                    bank = self.proj(wt, wr, 8, 512, c * 128, 128, orhs, ores, n)
                    P.add("dve", I("scalar_tensor_tensor", self.xT[:, o, t0:t0 + n], self.bank[bank][:, 0:n],
                                   self.modcol(l, 2, o, col), self.xT[:, o, t0:t0 + n], ALU.mult, ALU.add),
                          r=[bank, "modT", f"x{b}"], w=[f"x{b}"])

    def mla_layer(self, l, s, cc):
        import os
        if int(os.environ.get("MLA_STOP", "99")) <= 0:
            return
        P = self.P
        i = l // 2
        pp = self.pp
        QCN = self.aview(0, [4, NT])
        KVCN = self.aview(4 * NT, [2, NT])
        KRG = self.arena[:, 6 * NT:7 * NT]
        KTm = self.aview(7 * NT, [2, NT])
        VOFF = 9 * NT
        VCH = 192
        V3 = self.arena[:, VOFF:VOFF + 18 * VCH].rearrange("p (c w) -> p c w", c=18)
        qn_ = [f"QCN{b}" for b in range(5)]
        kvn_ = [f"KVCN{b}" for b in range(5)]
        krn_ = [f"KRG{b}" for b in range(5)]
        ktn = [f"KTm{b}" for b in range(5)]
        vn = [f"Vm{b}" for b in range(5)]
        self.set_arena(qn_ + kvn_ + krn_ + ktn + vn)
        self.load_rope("rope_m")
        Vt = self.arena
        P.add("pool", I("memset", V3[:, :, 64:128], 1.0), r=[], w=vn)
        g_q = pp[:, PP_MQK + i * 2:PP_MQK + i * 2 + 1]
        g_k = pp[:, PP_MQK + i * 2 + 1:PP_MQK + i * 2 + 2]
        boc = self.blk_of_chunk

        self.gen = self.gen_big
        self.norm_mod(l, s, 0, 1, False)
        for b, (t0, n) in enumerate(TB):
            alt = (b % 2 == 1)
            if b + 1 < 5:
                self.norm_mod(l, s, b + 1, 1, not alt)
            hTb, hres = self.hbuf(alt)
            hrhs = [hTb[:, k, 0:n] for k in range(8)]
            for (pk, W, nch, dst, dres, gbase, dim) in [("M0", 512, 4, QCN, qn_[b], PP_MQ + i * 4, 512),
                                                        ("M1", 352, 2, KVCN, kvn_[b], PP_MKV + i * 2, 256)]:
                wt, wr = self.load_piece((pk, l))
                b2 = self.gen.next()
                for c in range(nch):
                    bank = self.proj(wt, wr, 8, W, c * 128, 128, hrhs, hres, n)
                    bk = self.bank[bank]
                    P.add("act", I("activation", dst[:, c, t0:t0 + n], bk[:, 0:n], AF.Copy), r=[bank], w=[dres])
                    sq, sqn = self.tmpB()
                    P.add("act", I("activation", sq[:, 0:n], bk[:, 0:n], AF.Square), r=[bank], w=[sqn])
                    P.add("pe", I("matmul", self.bank[b2][:, 0:n], self.ones128, sq[:, 0:n],
                                  start=(c == 0), stop=(c == nch - 1)), r=[sqn, "cmat"], w=[b2])
                rt, rtn = self.tmpF()
                P.add("act", I("activation", rt[:, 0:n], self.bank[b2][:, 0:n], AF.Ln, bias=EPS, scale=1.0 / dim),
                      r=[b2], w=[rtn])
                P.add("act", I("activation", rt[:, 0:n], rt[:, 0:n], AF.Exp, scale=-0.5), r=[rtn], w=[rtn])
                for c in range(nch):
                    P.add("dve", I("scalar_tensor_tensor", dst[:, c, t0:t0 + n], dst[:, c, t0:t0 + n],
                                   pp[:, gbase + c:gbase + c + 1], rt[:, 0:n], ALU.mult, ALU.mult),
                          r=[dres, rtn, "pp"], w=[dres])
                if pk == "M1":
                    bank = self.proj(wt, wr, 8, W, 256, 96, hrhs, hres, n)
                    bk = self.bank[bank]
                    qg, qgn = self.tmpF()
                    P.add("act", I("activation", qg[0:96, 0:n], bk[0:96, 0:n], AF.Identity, scale=g_k[0:96, :]),
                          r=[bank, "pp"], w=[qgn])
                    if b > 0:
                        tok0 = t0 - NCTX
                        qb, qbn = self.tmpB()
                        P.add("act", I("activation", qb[0:96, 0:n], qg[0:96, 0:n], AF.Copy), r=[qgn], w=[qbn])
                        b3 = self.gen.next()
                        P.add("pe", I("matmul", self.bank[b3][0:96, 0:n], self.perm_m[0:96, 0:96], qb[0:96, 0:n],
                                      start=True, stop=True), r=[qbn, "cmat"], w=[b3])
                        t2, t2n = self.tmpF()
                        P.add("dve", I("tensor_tensor", t2[64:96, 0:n], self.bank[b3][64:96, 0:n],
                                       self.ropeT[64:96, 1, tok0:tok0 + n], ALU.mult), r=[b3, "rope"], w=[t2n])
                        P.add("pool", I("tensor_tensor", qg[64:96, 0:n], qg[64:96, 0:n],
                                        self.ropeT[64:96, 0, tok0:tok0 + n], ALU.mult), r=[qgn, "rope"], w=[qgn])
                        P.add("pool", I("tensor_tensor", qg[64:96, 0:n], qg[64:96, 0:n], t2[64:96, 0:n], ALU.add),
                              r=[qgn, t2n], w=[qgn])
                    P.add("dve", I("tensor_copy", KRG[64:96, t0:t0 + n], qg[64:96, 0:n]), r=[qgn], w=[krn_[b]])

        import os
        STOP = int(os.environ.get("MLA_STOP", "99"))
        if STOP <= 1:
            return
        KVMODE = os.environ.get("KVMODE", "P")
        for hg in range(8):
            if STOP <= 5 and hg > 0:
                return
            self.gen = self.gen_big
            for b, (t0, n) in enumerate(TB):
                wt, wr = self.load_piece(("M2", l, hg))
                tv = wt[:, 0:512].rearrange("p (k w) -> p k w", k=2)
                kvrhs = [KVCN[:, k, t0:t0 + n] for k in range(2)]
                for hh in range(2):
                    bank = self.proj(wt, wr, 2, 256, hh * 128, 64, kvrhs, [kvn_[b]], n)
                    self.qknorm_rope(bank, 96, n, g_k, self.ones96, 96, None, KTm[0:96, hh, t0:t0 + n], ktn[b],
                                     extra_rows=(64, 96, KRG[64:96, t0:t0 + n], krn_[b]), mode=KVMODE)
                for u in range(n // 128):
                    ch = (t0 + u * 128) // 128
                    vbase = VOFF + ch * VCH
                    bA = self.gen.next()
                    ins = []
                    for k in range(2):
                        rv = tv[:, k, :].rearrange("p (h c) -> p h c", h=2)[:, :, 64:128]
                        ins.append(I("matmul", self.bank[bA][:, 0:128].rearrange("p (h c) -> p h c", h=2),
                                     KVCN[:, k, t0 + u * 128:t0 + (u + 1) * 128], rv, start=(k == 0), stop=(k == 1)))
                    P.add("pe", ins, r=[wr, kvn_[b]], w=[bA])
                    P.add("dve", I("tensor_copy", self.unit_AP(vbase, vbase + 128),
                                   self.bank[bA][:, 0:128].rearrange("p (h c) -> p h c", h=2)),
                          r=[bA], w=[vn[b]])
            self.gen = self.gen_small
            blocks = [b for b in range(5) if not (b == 0 and not cc)]

            def qbuild(b, slot):
                t0, n = TB[b]
                rope = None if b == 0 else (self.perm_m, 64, 96, t0 - NCTX)
                wt, wr = self.load_piece(("M3", l, hg))
                qrhs = [QCN[:, k, t0:t0 + n] for k in range(4)]
                for hh in range(2):
                    bank = self.proj(wt, wr, 4, 192, hh * 96, 96, qrhs, [qn_[b]], n)
                    self.qknorm_rope(bank, 96, n, g_q, self.ones96, 96, rope,
                                     self.QT[0:96, slot * 2 + hh, 0:n], f"QT{slot * 2 + hh}")

            qbuild(blocks[0], 0)
            for bi, b in enumerate(blocks):
                t0, n = TB[b]
                nchunks = 2 if b == 0 else 18
                slot = bi % 2
                if bi + 1 < len(blocks):
                    qbuild(blocks[bi + 1], (bi + 1) % 2)
                odd = (hg % 2 == 1)
                oslot = 0 if odd else 1 + b
                maps = []
                for hh in range(2):
                    def kt(ch, hh=hh):
                        return KTm[0:96, hh, ch * 128:(ch + 1) * 128]
                    if hh == 0:
                        def uf(ch):
                            return V3[:, ch, 0:128]
                        vrows, srows = (0, 64), (64, 128)
                    else:
                        def uf(ch):
                            return V3[:, ch, 64:192]
                        vrows, srows = (64, 128), (0, 64)

                    def done(obanks, vrows=vrows, srows=srows, n=n, oslot=oslot):
                        self.o_norm(obanks[0], vrows, srows, n, self.OT[vrows[0]:vrows[1], oslot, 0:n], f"OT{oslot}")
                    maps.append(dict(kt=kt, kres=lambda ch: ktn[boc(ch)],
                                     q=self.QT[0:96, slot * 2 + hh, 0:n], qres=f"QT{slot * 2 + hh}",
                                     units=[(uf, lambda ch: vn[boc(ch)])], done=done))
                self.attention(maps, n, nchunks, 96 ** -0.5)
                if not odd:
                    continue
                col = 4 if b == 0 else s
                wt, wr = self.load_piece(("M5", l, hg // 2))
                tv5 = wt[:, 0:2048].rearrange("p (k w) -> p k w", k=2)
                for o in range(8):
                    bank = self.gen.next()
                    P.add("pe", [I("matmul", self.bank[bank][:, 0:n], tv5[:, 0, o * 128:(o + 1) * 128], self.OT[:, 1 + b, 0:n],
                                   start=True, stop=False),
                                 I("matmul", self.bank[bank][:, 0:n], tv5[:, 1, o * 128:(o + 1) * 128], self.OT[:, 0, 0:n],
                                   start=False, stop=True)],
                          r=[wr, "OT0", f"OT{1 + b}"], w=[bank])
                    P.add("dve", I("scalar_tensor_tensor", self.xT[:, o, t0:t0 + n], self.bank[bank][:, 0:n],
                                   self.modcol(l, 2, o, col), self.xT[:, o, t0:t0 + n], ALU.mult, ALU.add),
                          r=[bank, "modT", f"x{b}"], w=[f"x{b}"])

    def ffn(self, l, s, cc):
        P = self.P
        HID = self.aview(0, [32, 512])
        hn = [f"hid{j}" for j in range(32)]
        self.set_arena(hn)
        self.gen = self.gen_big
        fblocks = [b for b in range(5) if not (b == 0 and not cc)]
        self.norm_mod(l, s, fblocks[0], 2, fblocks[0] % 2 == 1)
        for bi, b in enumerate(fblocks):
            t0, n = TB[b]
            col = 4 if b == 0 else s
            alt = (b % 2 == 1)
            if bi + 1 < len(fblocks):
                self.norm_mod(l, s, fblocks[bi + 1], 2, not alt)
            hTb, hres = self.hbuf(alt)
            hrhs = [hTb[:, k, 0:n] for k in range(8)]
            for g in range(8):
                wt, wr = self.load_piece(("W1", l, g))
                for c in range(4):
                    j = g * 4 + c
                    bank = self.proj(wt, wr, 8, 512, c * 128, 128, hrhs, hres, n)
                    t, tn = self.tmpF()
                    P.add("act", I("activation", t[:, 0:n], self.bank[bank][:, 0:n], AF.Relu), r=[bank], w=[tn])
                    P.add("pool", I("tensor_tensor", HID[:, j, 0:n], t[:, 0:n], t[:, 0:n], ALU.mult), r=[tn], w=[hn[j]])
            hidrhs = [HID[:, k, 0:n] for k in range(32)]
            for o in range(8):
                wt, wr = self.load_piece(("W2", l, o))
                bank = self.proj(wt, wr, 32, 128, 0, 128, hidrhs, hn, n)
                P.add("dve", I("scalar_tensor_tensor", self.xT[:, o, t0:t0 + n], self.bank[bank][:, 0:n],
                               self.modcol(l, 5, o, col), self.xT[:, o, t0:t0 + n], ALU.mult, ALU.add),
                      r=[bank, "modT", f"x{b}"], w=[f"x{b}"])


def _rope_tables():
    tabs = np.zeros((2, 128, 2, NLAT), np.float32)
    rows = (np.arange(NLAT) // 64).astype(np.float32)
    colsp = (np.arange(NLAT) % 64).astype(np.float32)
    for which, (rot_dim, p0, p1, per) in enumerate([(64, 0, 128, 64), (32, 64, 96, 32)]):
        n_freq = rot_dim // 4
        inv = (10000.0 ** (-np.arange(n_freq, dtype=np.float32) / n_freq)).astype(np.float32)
        ang = np.concatenate([rows[:, None] * inv[None, :], colsp[:, None] * inv[None, :]], axis=1)
        half = rot_dim // 2
        for p in range(p0, p1):
            d = (p - p0) % per
            j = d % half
            sgn = -1.0 if d < half else 1.0
            tabs[which, p, 0, :] = np.cos(ang[:, j])
            tabs[which, p, 1, :] = sgn * np.sin(ang[:, j])
    return tabs.reshape(2, 128, 2 * NLAT)


def _const_mats():
    cm = np.zeros((128, 640), np.float32)
    cm[:, 0:128] = 1.0
    cm[0:64, 128:192] = 1.0
    cm[64:128, 192:256] = 1.0
    cm[0:96, 256:352] = 1.0
    for m in range(128):
        d = m % 64
        k = m + 32 if d < 32 else m - 32
        cm[k, 384 + m] = 1.0
    for m in range(64, 96):
        k = m + 16 if m < 80 else m - 16
        cm[k, 512 + m] = 1.0
    return cm


def _cols(v):
    return np.ascontiguousarray(v.reshape(-1, 128).T)


def _prep_shared(inp, nlayers):
    f = np.float32
    ab_w_in = np.asarray(inp["ab_w_in"], f)
    ab_w_out = np.asarray(inp["ab_w_out"], f)
    bq_perm = np.concatenate([np.arange(1536 + (kv * 4 + g) * 64, 1536 + (kv * 4 + g) * 64 + 64)
                              for g in range(4) for kv in range(2)])
    perm = np.concatenate([np.arange(512, 1024), np.arange(2048, 2176), np.arange(2176, 2304),
                           np.arange(1024, 1536), np.arange(0, 512), bq_perm])
    w_in_e = np.ascontiguousarray(ab_w_in[:, :, perm])
    rperm = np.concatenate([np.arange(0, 512)] +
                           [np.arange(512 + (kv * 4 + g) * 64, 512 + (kv * 4 + g) * 64 + 64)
                            for g in range(4) for kv in range(2)])
    w_out_e = np.ascontiguousarray(ab_w_out[:, rperm, :])
    mla_w_in = np.asarray(inp["mla_w_in"], f)
    w_in_m = np.zeros((2, 1024, 864), f)
    w_in_m[:, :, 0:768] = mla_w_in[:, :, 0:768]
    w_in_m[:, :, 832:864] = mla_w_in[:, :, 768:800]
    pp = np.zeros((128, NPP), f)
    nmix = np.asarray(inp["norm_mix"], f)
    nmlp = np.asarray(inp["norm_mlp"], f)
    adab = np.asarray(inp["ada_b"], f)
    for l in range(4):
        pp[:, PP_NMIX + l * 8:PP_NMIX + l * 8 + 8] = _cols(nmix[l])
        pp[:, PP_NMLP + l * 8:PP_NMLP + l * 8 + 8] = _cols(nmlp[l])
        pp[:, PP_ADAB + l * 48:PP_ADAB + l * 48 + 48] = _cols(adab[l])
    dqk = np.asarray(inp["diff_qk_norm"], f)
    gqk = np.asarray(inp["gqa_qk_norm"], f)
    subln = np.asarray(inp["diff_subln"], f)
    dlam = np.asarray(inp["diff_lambda"], f)
    mq = np.asarray(inp["mla_q_norm"], f)
    mkv = np.asarray(inp["mla_kv_norm"], f)
    mqk = np.asarray(inp["mla_qk_norm"], f)
    for i in range(2):
        pp[:, PP_GE + i * 4 + 0] = np.tile(dqk[i, 0], 2)
        pp[:, PP_GE + i * 4 + 1] = np.tile(dqk[i, 1], 2)
        pp[:, PP_GE + i * 4 + 2] = np.tile(gqk[i, 0], 2)
        pp[:, PP_GE + i * 4 + 3] = np.tile(gqk[i, 1], 2)
        pp[:, PP_SUBLN + i] = subln[i]
        pp[:, PP_DLAM + i * 256:PP_DLAM + (i + 1) * 256] = np.broadcast_to(dlam[i].reshape(1, 256), (128, 256))
        pp[:, PP_MQ + i * 4:PP_MQ + i * 4 + 4] = _cols(mq[i])
        pp[:, PP_MKV + i * 2:PP_MKV + i * 2 + 2] = _cols(mkv[i])
        pp[0:96, PP_MQK + i * 2] = mqk[i, 0]
        pp[0:96, PP_MQK + i * 2 + 1] = mqk[i, 1]
    shared = {
        "ppd": pp, "cstd": _const_mats(), "roped": _rope_tables(),
        "ada_w": np.ascontiguousarray(np.asarray(inp["ada_w"], f)),
        "mlp_w1": np.ascontiguousarray(np.asarray(inp["mlp_w1"], f)),
        "mlp_w2": np.ascontiguousarray(np.asarray(inp["mlp_w2"], f)),
        "w_in_e": w_in_e, "w_out_e": w_out_e, "w_in_m": w_in_m,
        "w_q_up": np.ascontiguousarray(np.asarray(inp["mla_w_q_up"], f)),
        "w_kv_up": np.ascontiguousarray(np.asarray(inp["mla_w_kv_up"], f)),
        "w_out_m": np.ascontiguousarray(np.asarray(inp["mla_w_out"], f)),
    }
    return shared


def _prep_core(inp, seqs):
    f = np.float32
    x = np.asarray(inp["x"], f)
    ctx = np.asarray(inp["ctx"], f)
    c = np.asarray(inp["c"], f)
    c_ctx = np.asarray(inp["c_ctx"], f)
    ns = len(seqs)
    xTd = np.empty((ns, 128, 8, NT), f)
    for j, sidx in enumerate(seqs):
        cat = np.concatenate([ctx[sidx], x[sidx]], axis=0)
        xTd[j] = cat.T.reshape(8, 128, NT).transpose(1, 0, 2)
    cT = np.zeros((128, 8, 5), f)
    for j, sidx in enumerate(seqs):
        cT[:, :, j] = _cols(c[sidx])
    cT[:, :, 4] = _cols(c_ctx)
    return {"xTd": xTd, "cTd": cT}


def _unpack_out(outT):
    ns = outT.shape[0]
    return np.ascontiguousarray(outT.transpose(0, 2, 1, 3).reshape(ns, 1024, NLAT).transpose(0, 2, 1))


_NC_CACHE = {}


def run(inputs, n_cores=N_CORES, nseq=SEQ_PER_CORE, nlayers=DEPTH, trace=False):
    key = (nseq, nlayers)
    if key not in _NC_CACHE:
        _NC_CACHE[key] = Builder(nseq, nlayers).build()
    nc = _NC_CACHE[key]
    shared = _prep_shared(inputs, nlayers)
    in_maps = []
    for cidx in range(n_cores):
        seqs = list(range(cidx * nseq, (cidx + 1) * nseq))
        m = dict(shared)
        m.update(_prep_core(inputs, seqs))
        in_maps.append(m)
    res = run_bass_kernel_spmd(nc, in_maps, core_ids=list(range(n_cores)), trace=trace)
    outs = [_unpack_out(np.asarray(r["outT"])) for r in res.results]
    return np.concatenate(outs, axis=0), res


def kernel(**inputs):
    out, _ = run(inputs)
    return out.astype(np.float32)
```

```python
import math
from contextlib import ExitStack
import numpy as np
import concourse.bass as bass
import concourse.mybir as mybir
from concourse.bass_utils import run_bass_kernel_spmd

F32 = mybir.dt.float32
BF16 = mybir.dt.bfloat16
ALU = mybir.AluOpType
AF = mybir.ActivationFunctionType

D = 1024
NT = 2304
NCTX = 256
NLAT = 2048
DEPTH = 4
EPS = 1e-6
TB = [(0, 256), (256, 512), (768, 512), (1280, 512), (1792, 512)]
PIECE = 4096
NRING = 3
N_CORES = 8
SEQ_PER_CORE = 4

PP_NMIX = 0
PP_NMLP = 32
PP_ADAB = 64
PP_GE = 256
PP_SUBLN = 264
PP_DLAM = 266
PP_MQ = 778
PP_MKV = 786
PP_MQK = 790
NPP = 794


def lambda_init(layer):
    return 0.8 - 0.6 * math.exp(-0.3 * layer)


class Op:
    __slots__ = ("eng", "fn", "deps", "odeps", "signal", "sem", "val", "isdma", "epoch", "idx", "dur", "ndep", "users", "rdy", "st", "fin", "done")


class Prog:
    ENGS = ("pe", "act", "dve", "pool", "sp")
    WIN = 48

    def __init__(self, nc, stack):
        self.nc = nc
        self.stack = stack
        self.ops = {e: [] for e in self.ENGS}
        self.res = {}
        self.epochs = []
        self.dsems = {}
        self.nsem = 0
        self.new_epoch()

    def _sem(self, name):
        self.nsem += 1
        return self.stack.enter_context(self.nc.semaphore(name))

    def new_epoch(self):
        i = len(self.epochs)
        self.epochs.append({e: self._sem(f"e{i}_{e}") for e in self.ENGS})

    def add(self, eng, fn, r=(), w=(), dma=None):
        op = Op()
        op.eng = eng
        op.fn = fn
        op.deps = []
        op.signal = False
        op.isdma = dma is not None
        op.epoch = len(self.epochs) - 1
        op.sem = None
        op.val = 0
        cand = []
        for n in r:
            st = self.res.setdefault(n, [None, []])
            if st[0] is not None:
                cand.append((st[0], "raw"))
        for n in w:
            st = self.res.setdefault(n, [None, []])
            if st[0] is not None:
                cand.append((st[0], "waw"))
            for rd in st[1]:
                cand.append((rd, "war"))
        seen = set()
        op.odeps = []
        op.idx = len(self.ops[eng])
        op.dur = est_dur(eng, fn, op.isdma)
        for d, t in cand:
            if d is op or id(d) in seen:
                continue
            seen.add(id(d))
            if (not d.isdma) and (not op.isdma) and d.eng == eng:
                if eng == "pe":
                    op.odeps.append(d)
                    continue
            d.signal = True
            op.deps.append(d)
        for n in w:
            st = self.res[n]
            st[0] = op
            st[1] = []
        wset = set(w)
        lim = op.idx - self.WIN - 2
        for n in r:
            if n in wset:
                continue
            st = self.res[n]
            if not op.isdma:
                st[1] = [x for x in st[1] if x.isdma or x.eng != eng or x.idx > lim]
            st[1].append(op)
        if dma is not None:
            ent = self.dsems.get(dma)
            if ent is None:
                ent = [self._sem("d_" + dma), 0]
                self.dsems[dma] = ent
            ent[1] += 16
            op.sem = ent[0]
            op.val = ent[1]
        self.ops[eng].append(op)
        return op

    def fence(self, old, new):
        evs = []
        for n in old:
            st = self.res.get(n)
            if st is None:
                continue
            if st[0] is not None:
                evs.append(st[0])
            evs.extend(st[1])
        uniq = []
        seen = set()
        for e in evs:
            if id(e) not in seen:
                seen.add(id(e))
                uniq.append(e)
        for n in new:
            st = self.res.setdefault(n, [None, []])
            st[1] = list(st[1]) + uniq

    def schedule(self):
        import heapq
        W = self.WIN
        ENGS = self.ENGS
        for e in ENGS:
            for op in self.ops[e]:
                op.users = []
                op.ndep = 0
                op.rdy = 0.0
                op.done = False
        for e in ENGS:
            for op in self.ops[e]:
                for d in op.deps:
                    d.users.append(op)
                    op.ndep += 1
                for d in op.odeps:
                    d.users.append(op)
                    op.ndep += 1
        lists = {e: list(self.ops[e]) for e in ENGS}
        head = {e: 0 for e in ENGS}
        free = {e: 0.0 for e in ENGS}
        dma_free = [0.0]
        order = {e: [] for e in ENGS}
        remaining = sum(len(v) for v in lists.values())

        def best_of(e):
            lst = lists[e]
            n = len(lst)
            i = head[e]
            while i < n and lst[i].done:
                i += 1
            head[e] = i
            w = 1 if e == "sp" else W
            best = None
            bst = 0.0
            fr = free[e]
            iend = min(n, i + w)
            while i < iend:
                op = lst[i]
                if not op.done:
                    if op.ndep == 0:
                        st = op.rdy if op.rdy > fr else fr
                        if best is None or st < bst:
                            best = op
                            bst = st
                            if st <= fr:
                                break
                i += 1
            return best, bst

        cache = {e: best_of(e) for e in ENGS}
        while remaining > 0:
            be = None
            bop = None
            bst = 0.0
            for e in ENGS:
                op, st = cache[e]
                if op is None:
                    continue
                if bop is None or st < bst:
                    be, bop, bst = e, op, st
            if bop is None:
                raise RuntimeError("scheduler deadlock")
            bop.done = True
            bop.st = bst
            if bop.isdma:
                free[be] = bst + 60.0
                t0 = max(bst, dma_free[0])
                dma_free[0] = t0 + bop.dur
                fin = t0 + bop.dur + 2000.0
            else:
                fin = bst + bop.dur
                free[be] = fin
            bop.fin = fin
            order[be].append(bop)
            remaining -= 1
            dirty = {be}
            for u in bop.users:
                u.ndep -= 1
                lat = fin + (0.0 if (u.eng == be and not bop.isdma) else 120.0)
                if lat > u.rdy:
                    u.rdy = lat
                if u.ndep == 0:
                    dirty.add(u.eng)
            for e in dirty:
                cache[e] = best_of(e)
        self.ops = order
        self.sim_time = max(free.values())

    def emit(self, final_waits):
        self.schedule()
        for e in self.ENGS:
            cnt = {}
            for op in self.ops[e]:
                if op.isdma:
                    continue
                if op.signal:
                    cnt[op.epoch] = cnt.get(op.epoch, 0) + 1
                    op.val = cnt[op.epoch]
                    op.sem = self.epochs[op.epoch][e]
        ops = self.ops

        def run(e, eng):
            waited = {}
            for op in ops[e]:
                for d in op.deps:
                    k = d.sem.num
                    if waited.get(k, 0) >= d.val:
                        continue
                    eng.wait_ge(d.sem, d.val)
                    waited[k] = d.val
                fns = op.fn if isinstance(op.fn, list) else [op.fn]
                ins = None
                for (mname, a, k) in fns:
                    ins = getattr(eng, mname)(*a, **k)
                if op.isdma:
                    ins.then_inc(op.sem, 16)
                elif op.signal:
                    ins.then_inc(op.sem, 1)
            if e == "sp":
                for d in final_waits:
                    eng.wait_ge(d.sem, d.val)

        with self.nc.Block() as block:
            @block.tensor
            def _(eng):
                run("pe", eng)

            @block.scalar
            def _(eng):
                run("act", eng)

            @block.vector
            def _(eng):
                run("dve", eng)

            @block.gpsimd
            def _(eng):
                run("pool", eng)

            @block.sync
            def _(eng):
                run("sp", eng)


def _free_elems(ap):
    n = 1
    for d in ap.shape[1:]:
        n *= int(d)
    return n


def est_dur(eng, fn, isdma):
    fns = fn if isinstance(fn, list) else [fn]
    tot = 0.0
    for (mname, a, k) in fns:
        out = k.get("out", a[0] if a else None)
        fe = _free_elems(out) if out is not None else 512
        if isdma:
            bpe = 2 if out.dtype == BF16 else 4
            tot += fe * int(out.shape[0]) * bpe / 180.0
        elif eng == "pe":
            tot += 45.0 + 0.41 * max(fe, 64)
        elif eng == "act":
            tot += 220.0 + 0.83 * fe
        elif eng == "dve":
            tot += (300.0 + 5.4 * fe) if mname == "reciprocal" else (250.0 + 1.04 * fe)
        else:
            tot += 300.0 + 1.45 * fe
    return tot


def I(mname, *a, **k):
    return (mname, a, k)


class Rot:
    def __init__(self, items):
        self.items = list(items)
        self.i = 0

    def next(self):
        it = self.items[self.i % len(self.items)]
        self.i += 1
        return it


import os
class Builder:
    def __init__(self, nseq, nlayers):
        self.nseq = nseq
        self.nlayers = nlayers
        self.nc = bass.Bass("TRN2", target_bir_lowering=False)
        self.pieces = []

    def dram_in(self, name, shape, dt=F32):
        return self.nc.dram_tensor(name, list(shape), dt, kind="ExternalInput").ap()

    def add_piece(self, src, length):
        self.pieces.append((src, length))
        return len(self.pieces) - 1

    def build(self):
        nc = self.nc
        nseq, nlayers = self.nseq, self.nlayers
        with ExitStack() as stack:
            self.stack = stack
            P = self.P = Prog(nc, stack)
            xTd = self.dram_in("xTd", [nseq, 128, 8, NT])
            cTd = self.dram_in("cTd", [128, 8, 5])
            ppd = self.dram_in("ppd", [128, NPP])
            cstd = self.dram_in("cstd", [128, 640])
            roped = self.dram_in("roped", [2, 128, 4096])
            ada_w = self.dram_in("ada_w", [4, 1024, 6144])
            mlp_w1 = self.dram_in("mlp_w1", [4, 1024, 4096])
            mlp_w2 = self.dram_in("mlp_w2", [4, 4096, 1024])
            w_in_e = self.dram_in("w_in_e", [2, 1024, 2304])
            w_out_e = self.dram_in("w_out_e", [2, 1024, 1024])
            w_in_m = self.dram_in("w_in_m", [2, 1024, 864])
            w_q_up = self.dram_in("w_q_up", [2, 512, 1536])
            w_kv_up = self.dram_in("w_kv_up", [2, 256, 2048])
            w_out_m = self.dram_in("w_out_m", [2, 1024, 1024])
            outT = nc.dram_tensor("outT", [nseq, 128, 8, NLAT], F32, kind="ExternalOutput").ap()

            def kmaj(w2d, c0, W):
                return w2d[:, c0:c0 + W].rearrange("(k p) w -> p k w", p=128)

            pc = {}
            self.ada_pids = {}
            pc["cst"] = self.add_piece(cstd[:, :], 640)
            pc["rope_e"] = self.add_piece(roped[0], 4096)
            pc["rope_m"] = self.add_piece(roped[1], 4096)
            for l in range(nlayers):
                for g in range(12):
                    pc[("ada", l, g)] = self.add_piece(kmaj(ada_w[l], g * 512, 512), 4096)
                    self.ada_pids[pc[("ada", l, g)]] = (l, g)
            for l in range(nlayers):
                i = l // 2
                if l % 2 == 0:
                    pc[("E0", l)] = self.add_piece(kmaj(w_in_e[i], 0, 512), 4096)
                    pc[("E1", l)] = self.add_piece(kmaj(w_in_e[i], 512, 256), 2048)
                    pc[("E2", l)] = self.add_piece(kmaj(w_in_e[i], 768, 512), 4096)
                    pc[("E3", l)] = self.add_piece(kmaj(w_in_e[i], 1280, 512), 4096)
                    pc[("E4", l)] = self.add_piece(kmaj(w_in_e[i], 1792, 512), 4096)
                    pc[("E5", l)] = self.add_piece(kmaj(w_out_e[i], 0, 512), 4096)
                    pc[("E6", l)] = self.add_piece(kmaj(w_out_e[i], 512, 512), 4096)
                else:
                    pc[("M0", l)] = self.add_piece(kmaj(w_in_m[i], 0, 512), 4096)
                    pc[("M1", l)] = self.add_piece(kmaj(w_in_m[i], 512, 352), 2816)
                    for hg in range(8):
                        pc[("M2", l, hg)] = self.add_piece(kmaj(w_kv_up[i], hg * 256, 256), 512)
                        pc[("M3", l, hg)] = self.add_piece(kmaj(w_q_up[i], hg * 192, 192), 768)
                    for hp in range(4):
                        pc[("M5", l, hp)] = self.add_piece(kmaj(w_out_m[i, hp * 256:(hp + 1) * 256, :], 0, 1024), 2048)
                for g in range(8):
                    pc[("W1", l, g)] = self.add_piece(kmaj(mlp_w1[l], g * 512, 512), 4096)
                for o in range(8):
                    pc[("W2", l, o)] = self.add_piece(kmaj(mlp_w2[l], o * 128, 128), 4096)
            self.pc = pc
            npieces = len(self.pieces)
            wbf = nc.dram_tensor("wbf", [npieces, 128, PIECE], BF16).ap()
            self.wbf = wbf

            def sb(name, shape, dt):
                return stack.enter_context(nc.sbuf_tensor(name, list(shape), dt))

            def ps(name, shape):
                return stack.enter_context(nc.psum_tensor(name, list(shape), F32))

            xT = self.xT = sb("xT", [128, 8, NT], F32)
            ARENA = 24192
            self.arena = sb("arena", [128, ARENA], BF16)
            self.ring = [sb(f"ring{i}", [128, PIECE], BF16) for i in range(NRING)]
            self.ropeT = sb("ropeT", [128, 2, 2048], BF16)
            self.hT = sb("hT", [128, 8, 512], BF16)
            self.QT = sb("QT", [128, 8, 512], BF16)
            self.OT = sb("OT", [128, 8, 512], BF16)
            self.PT = sb("PT", [128, 3, 1024], BF16)
            self.NTF = 6
            self.tmpf = sb("tmpf", [128, self.NTF, 512], F32)
            self.tmpb = sb("tmpb", [128, 4, 512], BF16)
            pp = self.pp = sb("pp", [128, NPP], F32)
            self.modT = sb("modT", [128, 4, 48, 5], F32)
            cmat = self.cmat = sb("cmat", [128, 640], BF16)
            cT = sb("cT", [128, 8, 5], F32)
            scT = self.scT = sb("scT", [128, 8, 5], BF16)
            self.cols = sb("cols", [128, 32], F32)
            self.lamc = sb("lamc", [128, 16], F32)

            self.ones128 = cmat[:, 0:128]
            self.blk64 = cmat[:, 128:256]
            self.ones96 = cmat[:, 256:384]
            self.perm_e = cmat[:, 384:512]
            self.perm_m = cmat[:, 512:640]

            g0 = ps("g0", [128, 512]); g1 = ps("g1", [128, 512])
            s0 = ps("s0", [128, 1024]); s1 = ps("s1", [128, 1024])
            o0 = ps("o0", [128, 512]); o1 = ps("o1", [128, 512])
            self.bank = {"g0": g0[:, :], "g1": g1[:, :], "o0": o0[:, :], "o1": o1[:, :],
                         "s0a": s0[:, 0:512], "s0b": s0[:, 512:1024],
                         "s1a": s1[:, 0:512], "s1b": s1[:, 512:1024]}
            self.s_tiles = [(s0, ["s0a", "s0b"]), (s1, ["s1a", "s1b"])]
            self.gen_small = Rot(["g0", "g1"])
            self.gen_big = Rot(["g0", "g1", "o0", "o1", "s0a", "s0b", "s1a", "s1b"])
            self.gen = self.gen_big
            self.tf = Rot(list(range(self.NTF)))
            self.tb = Rot(list(range(4)))
            self.ptr = Rot(list(range(3)))
            self.ring_i = 0
            self.arena_names = []

            def dma_simple(out_ap, in_ap, r, w, key):
                return P.add("sp", I("dma_start", out=out_ap, in_=in_ap), r=r, w=w, dma=key)

            dma_simple(pp[:, :], ppd[:, :], [], ["pp"], "c_pp")
            dma_simple(cT[:, :, :], cTd[:, :, :], [], ["cT"], "c_cT")

            P.add("act", I("activation", scT[:, :, :], cT[:, :, :], AF.Silu), r=["cT"], w=["scT"])
            self.prepass()

            dma_simple(cmat[:, :], wbf[pc["cst"], :, 0:640], [f"wbf{pc['cst']}"], ["cmat"], "c_cm")

            self.preamble(cT, scT)

            store_ops = []
            for s in range(nseq):
                if s > 0:
                    P.new_epoch()
                for b, (t0, n) in enumerate(TB):
                    dma_simple(xT[:, :, t0:t0 + n], xTd[s, :, :, t0:t0 + n], [], [f"x{b}"], f"xl{b}")
                for l in range(nlayers):
                    if l == 2:
                        P.new_epoch()
                    cc = l < DEPTH - 1
                    self.layer_cols(l, s)
                    if l % 2 == 0:
                        self.even_layer(l, s, cc)
                    else:
                        self.mla_layer(l, s, cc)
                    self.ffn(l, s, cc)
                for b in range(1, 5):
                    t0, n = TB[b]
                    op = dma_simple(outT[s, :, :, t0 - NCTX:t0 - NCTX + n], xT[:, :, t0:t0 + n],
                                    [f"x{b}"], [], f"xs{b}")
                    store_ops.append(op)
            P.emit(store_ops)
        return nc

    def set_arena(self, names):
        self.P.fence(self.arena_names, names)
        self.arena_names = list(names)

    def aview(self, off, shape):
        n = int(np.prod(shape))
        ap = self.arena[:, off:off + n]
        if len(shape) == 2:
            return ap.rearrange("p (a b) -> p a b", a=shape[0])
        return ap

    def tmpF(self):
        i = self.tf.next()
        return self.tmpf[:, i, :], f"tf{i}"

    def tmpB(self):
        i = self.tb.next()
        return self.tmpb[:, i, :], f"tb{i}"

    def load_piece(self, key):
        pid = self.pc[key]
        ln = self.pieces[pid][1]
        slot = self.ring_i % NRING
        self.ring_i += 1
        t = self.ring[slot]
        self.P.add("sp", I("dma_start", out=t[:, 0:ln], in_=self.wbf[pid, :, 0:ln]),
                   r=[f"wbf{pid}"], w=[f"ring{slot}"], dma=f"ring{slot}")
        return t, f"ring{slot}"

    def prepass(self):
        P = self.P
        xflat = self.xT[:, :, :].rearrange("p a b -> p (a b)")
        import os
        NST = int(os.environ.get('NST', '4'))
        cast_engs = Rot(["dve", "act", "pool"])
        stg_names = [f"stg{i}" for i in range(NST)]
        cb_names = [f"cb{i}" for i in range(NST)]
        def views(pid):
            src, ln = self.pieces[pid]
            sl = pid % NST
            stg = xflat[:, sl * PIECE: sl * PIECE + ln]
            cb = self.arena[:, sl * PIECE: sl * PIECE + ln]
            if len(src.shape) == 3:
                stg_v = stg.rearrange("p (k w) -> p k w", k=src.shape[1])
            else:
                stg_v = stg
            return src, ln, sl, stg, stg_v, cb

        def rec_load(pid):
            src, ln, sl, stg, stg_v, cb = views(pid)
            P.add("sp", I("dma_start", out=stg_v, in_=src), r=[], w=[stg_names[sl]], dma=f"stl{sl}")

        def rec_cast_store(pid):
            src, ln, sl, stg, stg_v, cb = views(pid)
            ce = cast_engs.next()
            if ce == "act":
                P.add("act", I("activation", cb, stg, AF.Copy), r=[stg_names[sl]], w=[cb_names[sl]])
            else:
                P.add(ce, I("tensor_copy", cb, stg), r=[stg_names[sl]], w=[cb_names[sl]])
            if pid in self.ada_pids:
                l_, g_ = self.ada_pids[pid]
                bank = ["g0", "g1", "o0", "o1"][l_]
                bk = self.bank[bank]
                cb_v = cb.rearrange("p (k w) -> p k w", k=8)
                ins = []
                for c in range(4):
                    j = g_ * 4 + c
                    for k in range(8):
                        ins.append(I("matmul", bk[:, j * 5:j * 5 + 5], cb_v[:, k, c * 128:(c + 1) * 128],
                                     self.scT[:, k, :], start=(k == 0), stop=(k == 7)))
                P.add("pe", ins, r=[cb_names[sl], "scT"], w=[bank])
                if g_ == 11:
                    ins = []
                    for j in range(48):
                        ins.append(I("tensor_scalar", self.modT[:, l_, j, :], bk[:, j * 5:j * 5 + 5],
                                     self.pp[:, PP_ADAB + l_ * 48 + j:PP_ADAB + l_ * 48 + j + 1], None, ALU.add))
                    P.add("dve", ins, r=[bank, "pp"], w=["modT"])
                return
            P.add("sp", I("dma_start", out=self.wbf[pid, :, 0:ln], in_=cb),
                  r=[cb_names[sl]], w=[f"wbf{pid}"], dma=f"sts{sl}")

        npc = len(self.pieces)
        LA = int(os.environ.get("PRE_LA", str(NST - 1)))
        for pid in range(min(LA, npc)):
            rec_load(pid)
        for pid in range(npc):
            if pid + LA < npc:
                rec_load(pid + LA)
            rec_cast_store(pid)
        P.fence(stg_names, [f"x{b}" for b in range(5)])
        self.arena_names = cb_names

    def preamble(self, cT, scT):
        P = self.P
        pp, lamc, modT = self.pp, self.lamc, self.modT
        nlayers = self.nlayers
        X = mybir.AxisListType.X
        for i in range(2):
            l = 2 * i
            if l >= nlayers:
                continue
            li = lambda_init(l)
            base = PP_DLAM + i * 256
            tA, nA = self.tmpF()
            P.add("dve", [I("tensor_tensor", tA[:, 0:64], pp[:, base:base + 64], pp[:, base + 64:base + 128], ALU.mult),
                          I("tensor_tensor", tA[:, 64:128], pp[:, base + 128:base + 192], pp[:, base + 192:base + 256], ALU.mult)],
                  r=["pp"], w=[nA])
            P.add("dve", [I("reduce_sum", lamc[:, 8 + 2 * i:9 + 2 * i], tA[:, 0:64], X),
                          I("reduce_sum", lamc[:, 9 + 2 * i:10 + 2 * i], tA[:, 64:128], X)],
                  r=[nA], w=["lamc_s"])
            P.add("act", I("activation", lamc[:, 12 + 2 * i:14 + 2 * i], lamc[:, 8 + 2 * i:10 + 2 * i], AF.Exp),
                  r=["lamc_s"], w=["lamc_e"])
            P.add("dve", I("tensor_tensor", lamc[:, 4 + i:5 + i], lamc[:, 13 + 2 * i:14 + 2 * i],
                           lamc[:, 12 + 2 * i:13 + 2 * i], ALU.subtract), r=["lamc_e"], w=["lamc_d"])
            P.add("dve", I("tensor_scalar", lamc[:, i:i + 1], lamc[:, 4 + i:5 + i], -li, None, ALU.add),
                  r=["lamc_d"], w=["lamc"])
            P.add("dve", I("tensor_scalar", lamc[:, 2 + i:3 + i], pp[:, PP_SUBLN + i:PP_SUBLN + i + 1], 1.0 - li, None, ALU.mult),
                  r=["pp"], w=["lamc"])

    def layer_cols(self, l, s):
        pp, modT, cols = self.pp, self.modT, self.cols
        specs = [(0, 8, s, PP_NMIX), (8, 8, 4, PP_NMIX), (16, 32, s, PP_NMLP), (24, 32, 4, PP_NMLP)]
        ins = []
        for (co, j0, col, nb) in specs:
            ins.append(I("scalar_tensor_tensor", cols[:, co:co + 8], modT[:, l, j0:j0 + 8, col], 1.0,
                         pp[:, nb + l * 8: nb + l * 8 + 8], ALU.add, ALU.mult))
        self.P.add("dve", ins, r=["modT", "pp"], w=["cols"])

    def modcol(self, l, m, k, col):
        return self.modT[:, l, m * 8 + k, col:col + 1]

    def hbuf(self, alt):
        if alt:
            return self.QT, [f"QT{c}" for c in range(8)]
        return self.hT, ["hT"]

    def norm_mod(self, l, s, b, which, alt=False):
        P = self.P
        t0, n = TB[b]
        isctx = (b == 0)
        col = 4 if isctx else s
        a_off = (0 if which == 1 else 16) + (8 if isctx else 0)
        m_shift = 0 if which == 1 else 3
        xT = self.xT
        hT, hres = self.hbuf(alt)
        xr = f"x{b}"
        P.add("act", I("activation", hT[:, :, 0:n], xT[:, :, t0:t0 + n], AF.Square), r=[xr], w=hres)
        bank = self.gen.next()
        bk = self.bank[bank]
        P.add("pe", [I("matmul", bk[:, 0:n], self.ones128, hT[:, k, 0:n], start=(k == 0), stop=(k == 7))
                     for k in range(8)], r=hres + ["cmat"], w=[bank])
        rt, rtn = self.tmpF()
        P.add("act", I("activation", rt[:, 0:n], bk[:, 0:n], AF.Ln, bias=EPS, scale=1.0 / D), r=[bank], w=[rtn])
        P.add("act", I("activation", rt[:, 0:n], rt[:, 0:n], AF.Exp, scale=-0.5), r=[rtn], w=[rtn])
        tks = [self.tmpF(), self.tmpF()]
        for k in range(8):
            tk, tkn = tks[k % 2]
            P.add("dve", I("scalar_tensor_tensor", tk[:, 0:n], xT[:, k, t0:t0 + n],
                           self.cols[:, a_off + k:a_off + k + 1], rt[:, 0:n], ALU.mult, ALU.mult),
                  r=[xr, "cols", rtn], w=[tkn])
            P.add("act", I("activation", hT[:, k, 0:n], tk[:, 0:n], AF.Identity,
                           bias=self.modcol(l, m_shift, k, col), scale=1.0), r=[tkn, "modT"], w=hres)

    def proj(self, wt, wr, nk, W, c0, M, rhs_list, rhs_res, n):
        bank = self.gen.next()
        bk = self.bank[bank]
        tv = wt[:, 0:nk * W].rearrange("p (k w) -> p k w", k=nk)
        self.P.add("pe", [I("matmul", bk[0:M, 0:n], tv[:, k, c0:c0 + M], rhs_list[k], start=(k == 0), stop=(k == nk - 1))
                          for k in range(nk)], r=[wr] + list(rhs_res), w=[bank])
        return bank

    def qknorm_rope(self, bank, R, n, gain, ones_mat, dim, rope, out_ap, out_res, extra_rows=None, mode="P"):
        P = self.P
        bk = self.bank[bank]
        qg, qgn = self.tmpF()
        Rp = R if extra_rows is None else extra_rows[0]
        if mode == "A":
            P.add("act", I("activation", qg[0:Rp, 0:n], bk[0:Rp, 0:n], AF.Identity, scale=gain[0:Rp, :]),
                  r=[bank, "pp"], w=[qgn])
        else:
            P.add("dve", I("tensor_scalar", qg[0:Rp, 0:n], bk[0:Rp, 0:n], gain[0:Rp, :], None, ALU.mult),
                  r=[bank, "pp"], w=[qgn])
        if extra_rows is not None:
            r0, r1, src, srcres = extra_rows
            P.add("pool", I("tensor_copy", qg[r0:r1, 0:n], src), r=[srcres], w=[qgn])
        sq, sqn = self.tmpB()
        if mode == "A":
            P.add("act", I("activation", sq[0:R, 0:n], qg[0:R, 0:n], AF.Square), r=[qgn], w=[sqn])
        else:
            P.add("pool", I("tensor_tensor", sq[0:R, 0:n], qg[0:R, 0:n], qg[0:R, 0:n], ALU.mult), r=[qgn], w=[sqn])
        b2 = self.gen.next()
        bk2 = self.bank[b2]
        P.add("pe", I("matmul", bk2[0:R, 0:n], ones_mat[0:R, 0:R], sq[0:R, 0:n], start=True, stop=True),
              r=[sqn, "cmat"], w=[b2])
        rt, rtn = self.tmpF()
        P.add("act", I("activation", rt[0:R, 0:n], bk2[0:R, 0:n], AF.Ln, bias=EPS, scale=1.0 / dim), r=[b2], w=[rtn])
        P.add("act", I("activation", rt[0:R, 0:n], rt[0:R, 0:n], AF.Exp, scale=-0.5), r=[rtn], w=[rtn])
        if rope is not None:
            perm, r0, r1, tok0 = rope
            qb, qbn = self.tmpB()
            P.add("dve" if mode == "A" else "pool", I("tensor_copy", qb[0:R, 0:n], qg[0:R, 0:n]), r=[qgn], w=[qbn])
            b3 = self.gen.next()
            bk3 = self.bank[b3]
            P.add("pe", I("matmul", bk3[0:R, 0:n], perm[0:R, 0:R], qb[0:R, 0:n], start=True, stop=True),
                  r=[qbn, "cmat"], w=[b3])
            cosT = self.ropeT[r0:r1, 0, tok0:tok0 + n]
            sinT = self.ropeT[r0:r1, 1, tok0:tok0 + n]
            t2, t2n = self.tmpF()
            P.add("dve", I("tensor_tensor", t2[r0:r1, 0:n], bk3[r0:r1, 0:n], sinT, ALU.mult), r=[b3, "rope"], w=[t2n])
            P.add("pool", I("tensor_tensor", qg[r0:r1, 0:n], qg[r0:r1, 0:n], cosT, ALU.mult), r=[qgn, "rope"], w=[qgn])
            P.add("pool", I("tensor_tensor", qg[r0:r1, 0:n], qg[r0:r1, 0:n], t2[r0:r1, 0:n], ALU.add),
                  r=[qgn, t2n], w=[qgn])
        P.add("dve", I("tensor_tensor", out_ap, qg[0:R, 0:n], rt[0:R, 0:n], ALU.mult), r=[qgn, rtn], w=[out_res])

    def load_rope(self, which):
        pid = self.pc[which]
        self.P.add("sp", I("dma_start", out=self.ropeT[:, :, :].rearrange("p a b -> p (a b)"), in_=self.wbf[pid, :, :]),
                   r=[f"wbf{pid}"], w=["rope"], dma="rope")

    def attention(self, maps, n, nchunks, scale):
        P = self.P
        npairs = nchunks // 2
        units = []
        for mi, m in enumerate(maps):
            for p in range(npairs):
                units.append((mi, p))
        opool = Rot(["o0", "o1"])
        state = {}

        def rec_S(u):
            mi, p = units[u]
            m = maps[mi]
            st, snames = self.s_tiles[u % 2]
            ins = []
            for h in range(2):
                ch = 2 * p + h
                ins.append(I("matmul", st[:, h * 512:h * 512 + n], m["kt"](ch), m["q"], start=True, stop=True))
            P.add("pe", ins, r=[m["kres"](2 * p), m["kres"](2 * p + 1), m["qres"]], w=snames)

        def rec_exp(u):
            st, snames = self.s_tiles[u % 2]
            pi = self.ptr.next()
            state[u] = pi
            src = st[:, :].rearrange("p (h c) -> p h c", h=2)[:, :, 0:n]
            dst = self.PT[:, pi, :].rearrange("p (h c) -> p h c", h=2)[:, :, 0:n]
            P.add("act", I("activation", dst, src, AF.Exp, scale=scale), r=snames, w=[f"pt{pi}"])

        def rec_PV(u):
            mi, p = units[u]
            m = maps[mi]
            pi = state[u]
            if p == 0:
                m["obanks"] = [opool.next() for _ in m["units"]]
            obanks = m["obanks"]
            ins = []
            for h in range(2):
                ch = 2 * p + h
                for ui, (lf, _) in enumerate(m["units"]):
                    ins.append(I("matmul", self.bank[obanks[ui]][:, 0:n], lf(ch), self.PT[:, pi, h * 512:h * 512 + n],
                                 start=(ch == 0), stop=(ch == nchunks - 1)))
            rr = [f"pt{pi}"]
            for (_, vf) in m["units"]:
                rr += [vf(2 * p), vf(2 * p + 1)]
            P.add("pe", ins, r=rr, w=list(obanks))
            if p == npairs - 1:
                m["done"](obanks)
                if m.get("after") is not None:
                    m["after"]()

        nu = len(units)
        PVD = int(os.environ.get("PVD", "1"))
        rec_S(0)
        for u in range(nu):
            if u + 1 < nu:
                rec_S(u + 1)
            rec_exp(u)
            if u - PVD >= 0:
                rec_PV(u - PVD)
        for u in range(max(0, nu - PVD), nu):
            rec_PV(u)

    def o_norm(self, obank, vrows, srows, n, out_ap, out_res, scratch=None):
        P = self.P
        bk = self.bank[obank]
        v0, v1 = vrows
        if scratch is None:
            t, tn = self.tmpF()
            tap = t[v0:v1, 0:n]
        else:
            tap, tn = scratch
        P.add("dve", I("reciprocal", tap, bk[srows[0]:srows[1], 0:n]), r=[obank], w=[tn])
        P.add("dve", I("tensor_tensor", out_ap, bk[v0:v1, 0:n], tap, ALU.mult), r=[obank, tn], w=[out_res])

    @staticmethod
    def blk_of_chunk(ch):
        t = ch * 128
        for b, (t0, n) in enumerate(TB):
            if t0 <= t < t0 + n:
                return b

    def unit_AP(self, first_off, second_off):
        ARW = self.arena.shape[1]
        return bass.AP(self.arena, first_off, [[ARW, 128], [second_off - first_off, 2], [1, 64]])

    def even_layer(self, l, s, cc):
        P = self.P
        i = l // 2
        pp = self.pp
        KT = self.aview(0, [5, NT])
        VOFF = 5 * NT
        VCH = 704
        Vt = self.arena
        V3 = self.arena[:, VOFF:VOFF + 18 * VCH].rearrange("p (c w) -> p c w", c=18)
        ktn = [f"KT{b}" for b in range(5)]
        vn = [f"V{b}" for b in range(5)]
        self.set_arena(ktn + vn)
        self.load_rope("rope_e")
        P.add("pool", I("memset", V3[:, :, 576:640], 1.0), r=[], w=vn)
        g_aq = pp[:, PP_GE + i * 4 + 0:PP_GE + i * 4 + 1]
        g_ak = pp[:, PP_GE + i * 4 + 1:PP_GE + i * 4 + 2]
        g_bq = pp[:, PP_GE + i * 4 + 2:PP_GE + i * 4 + 3]
        g_bk = pp[:, PP_GE + i * 4 + 3:PP_GE + i * 4 + 4]
        boc = self.blk_of_chunk

        self.gen = self.gen_big
        self.norm_mod(l, s, 0, 1, False)
        for b, (t0, n) in enumerate(TB):
            alt = (b % 2 == 1)
            if b + 1 < 5:
                self.norm_mod(l, s, b + 1, 1, not alt)
            hTb, hres = self.hbuf(alt)
            rope = None if b == 0 else (self.perm_e, 0, 128, t0 - NCTX)
            hrhs = [hTb[:, k, 0:n] for k in range(8)]
            wt, wr = self.load_piece(("E0", l))
            for c in range(4):
                bank = self.proj(wt, wr, 8, 512, c * 128, 128, hrhs, hres, n)
                self.qknorm_rope(bank, 128, n, g_ak, self.blk64, 64, rope, KT[:, c, t0:t0 + n], ktn[b], mode="A")
            wt1, wr1 = self.load_piece(("E1", l))
            bank = self.proj(wt1, wr1, 8, 256, 0, 128, hrhs, hres, n)
            self.qknorm_rope(bank, 128, n, g_bk, self.blk64, 64, rope, KT[:, 4, t0:t0 + n], ktn[b], mode="A")
            wt2, wr2 = self.load_piece(("E2", l))
            tv1 = wt1[:, 0:2048].rearrange("p (k w) -> p k w", k=8)
            tv2 = wt2[:, 0:4096].rearrange("p (k w) -> p k w", k=8)
            for u in range(n // 128):
                ch = (t0 + u * 128) // 128
                vbase = VOFF + ch * VCH
                bA = self.gen.next()
                bB = self.gen.next()
                ins = []
                for k in range(8):
                    ins.append(I("matmul", self.bank[bA][:, 0:512], hTb[:, k, u * 128:(u + 1) * 128], tv2[:, k, :],
                                 start=(k == 0), stop=(k == 7)))
                for k in range(8):
                    ins.append(I("matmul", self.bank[bB][:, 0:128], hTb[:, k, u * 128:(u + 1) * 128], tv1[:, k, 128:256],
                                 start=(k == 0), stop=(k == 7)))
                P.add("pe", ins, r=[wr1, wr2] + hres, w=[bA, bB])
                P.add("dve", I("tensor_copy", Vt[:, vbase:vbase + 512], self.bank[bA][:, 0:512]), r=[bA], w=[vn[b]])
                P.add("act", I("activation", self.unit_AP(vbase + 512, vbase + 640),
                               self.bank[bB][:, 0:128].rearrange("p (h c) -> p h c", h=2), AF.Copy),
                      r=[bB], w=[vn[b]])

        self.gen = self.gen_small
        neglam = self.lamc[:, i:i + 1]
        sublnp = self.lamc[:, 2 + i:3 + i]
        blocks = [b for b in range(5) if not (b == 0 and not cc)]
        qstate = {}

        def qbuild_chunk(b, cc_):
            t0, n = TB[b]
            rope = None if b == 0 else (self.perm_e, 0, 128, t0 - NCTX)
            hrhs = [self.hT[:, k, 0:n] for k in range(8)]
            if cc_ % 4 == 0:
                qstate["w"] = self.load_piece(("E3" if cc_ < 4 else "E4", l))
            wt, wr = qstate["w"]
            gq = g_aq if cc_ < 4 else g_bq
            bank = self.proj(wt, wr, 8, 512, (cc_ % 4) * 128, 128, hrhs, ["hT"], n)
            self.qknorm_rope(bank, 128, n, gq, self.blk64, 64, rope, self.QT[:, cc_, 0:n], f"QT{cc_}")

        self.norm_mod(l, s, blocks[0], 1)
        for c_ in range(8):
            qbuild_chunk(blocks[0], c_)
        for bi, b in enumerate(blocks):
            t0, n = TB[b]
            nchunks = 2 if b == 0 else 18
            nxt = blocks[bi + 1] if bi + 1 < len(blocks) else None
            if nxt is not None:
                self.norm_mod(l, s, nxt, 1)
            maps = []
            dtmp = {}
            for h in range(4):
                for m in range(2):
                    rows = (64 * m, 64 * m + 64)

                    def kt(ch, h=h, rows=rows):
                        return KT[rows[0]:rows[1], h, ch * 128:(ch + 1) * 128]

                    def uA(ch, h=h):
                        return V3[:, ch, h * 128:(h + 1) * 128]

                    def uB(ch):
                        return self.ones128

                    def done(obanks, h=h, m=m, n=n):
                        d, dn = self.tmpF()
                        dtmp[(h, m)] = (d, dn)
                        P.add("act", I("activation", d[:, 0:n], self.bank[obanks[1]][:, 0:n], AF.Ln), r=[obanks[1]], w=[dn])
                        P.add("act", I("activation", d[:, 0:n], d[:, 0:n], AF.Exp, scale=-1.0), r=[dn], w=[dn])
                        P.add("dve", I("tensor_tensor", d[:, 0:n], self.bank[obanks[0]][:, 0:n], d[:, 0:n], ALU.mult),
                              r=[obanks[0], dn], w=[dn])
                        if m == 1:
                            d0, d0n = dtmp[(h, 0)]
                            P.add("dve", I("scalar_tensor_tensor", d0[:, 0:n], d[:, 0:n], neglam, d0[:, 0:n], ALU.mult, ALU.add),
                                  r=[dn, d0n, "lamc"], w=[d0n])
                            sq, sqn = self.tmpB()
                            P.add("act", I("activation", sq[:, 0:n], d0[:, 0:n], AF.Square), r=[d0n], w=[sqn])
                            b2 = self.gen.next()
                            P.add("pe", I("matmul", self.bank[b2][:, 0:n], self.ones128, sq[:, 0:n], start=True, stop=True),
                                  r=[sqn, "cmat"], w=[b2])
                            rt, rtn = self.tmpF()
                            P.add("act", I("activation", rt[:, 0:n], self.bank[b2][:, 0:n], AF.Ln, bias=EPS, scale=1.0 / 128),
                                  r=[b2], w=[rtn])
                            P.add("act", I("activation", rt[:, 0:n], rt[:, 0:n], AF.Exp, scale=-0.5), r=[rtn], w=[rtn])
                            P.add("dve", I("scalar_tensor_tensor", self.OT[:, h, 0:n], d0[:, 0:n], sublnp, rt[:, 0:n],
                                           ALU.mult, ALU.mult), r=[d0n, rtn, "lamc"], w=[f"OT{h}"])
                    after = None
                    if m == 1 and nxt is not None:
                        after = (lambda nxt=nxt, h=h: qbuild_chunk(nxt, h))
                    maps.append(dict(kt=kt, kres=lambda ch: ktn[boc(ch)],
                                     q=self.QT[rows[0]:rows[1], h, 0:n], qres=f"QT{h}",
                                     units=[(uA, lambda ch: vn[boc(ch)]), (uB, lambda ch: "cmat")],
                                     done=done, after=after))
            self.attention(maps, n, nchunks, 64 ** -0.5)
            maps = []
            for g in range(4):
                for kv in range(2):
                    rows = (64 * kv, 64 * kv + 64)

                    def kt(ch, rows=rows):
                        return KT[rows[0]:rows[1], 4, ch * 128:(ch + 1) * 128]
                    if kv == 0:
                        def uf(ch):
                            return V3[:, ch, 512:640]
                        vrows, srows = (0, 64), (64, 128)
                    else:
                        def uf(ch):
                            return V3[:, ch, 576:704]
                        vrows, srows = (64, 128), (0, 64)

                    def done(obanks, g=g, vrows=vrows, srows=srows, n=n):
                        self.o_norm(obanks[0], vrows, srows, n, self.OT[vrows[0]:vrows[1], 4 + g, 0:n], f"OT{4 + g}")
                    after = None
                    if kv == 1 and nxt is not None:
                        after = (lambda nxt=nxt, g=g: qbuild_chunk(nxt, 4 + g))
                    maps.append(dict(kt=kt, kres=lambda ch: ktn[boc(ch)],
                                     q=self.QT[rows[0]:rows[1], 4 + g, 0:n], qres=f"QT{4 + g}",
                                     units=[(uf, lambda ch: vn[boc(ch)])], done=done, after=after))
            self.attention(maps, n, nchunks, 64 ** -0.5)
            col = 4 if b == 0 else s
            orhs = [self.OT[:, k, 0:n] for k in range(8)]
            ores = [f"OT{k}" for k in range(8)]
            for half, pk in enumerate(["E5", "E6"]):
                wt, wr = self.load_piece((pk, l))
                for c in range(4):
                    o = half * 4 + c
                    bank = self.proj(wt, wr, 8, 512, c * 128, 128, orhs, ores, n)
                    P.add("dve", I("scalar_tensor_tensor", self.xT[:, o, t0:t0 + n], self.bank[bank][:, 0:n],
                                   self.modcol(l, 2, o, col), self.xT[:, o, t0:t0 + n], ALU.mult, ALU.add),
                          r=[bank, "modT", f"x{b}"], w=[f"x{b}"])

    def mla_layer(self, l, s, cc):
        import os
        if int(os.environ.get("MLA_STOP", "99")) <= 0:
            return
        P = self.P
        i = l // 2
        pp = self.pp
        QCN = self.aview(0, [4, NT])
        KVCN = self.aview(4 * NT, [2, NT])
        KRG = self.arena[:, 6 * NT:7 * NT]
        KTm = self.aview(7 * NT, [2, NT])
        VOFF = 9 * NT
        VCH = 192
        V3 = self.arena[:, VOFF:VOFF + 18 * VCH].rearrange("p (c w) -> p c w", c=18)
        qn_ = [f"QCN{b}" for b in range(5)]
        kvn_ = [f"KVCN{b}" for b in range(5)]
        krn_ = [f"KRG{b}" for b in range(5)]
        ktn = [f"KTm{b}" for b in range(5)]
        vn = [f"Vm{b}" for b in range(5)]
        self.set_arena(qn_ + kvn_ + krn_ + ktn + vn)
        self.load_rope("rope_m")
        Vt = self.arena
        P.add("pool", I("memset", V3[:, :, 64:128], 1.0), r=[], w=vn)
        g_q = pp[:, PP_MQK + i * 2:PP_MQK + i * 2 + 1]
        g_k = pp[:, PP_MQK + i * 2 + 1:PP_MQK + i * 2 + 2]
        boc = self.blk_of_chunk

        self.gen = self.gen_big
        self.norm_mod(l, s, 0, 1, False)
        for b, (t0, n) in enumerate(TB):
            alt = (b % 2 == 1)
            if b + 1 < 5:
                self.norm_mod(l, s, b + 1, 1, not alt)
            hTb, hres = self.hbuf(alt)
            hrhs = [hTb[:, k, 0:n] for k in range(8)]
            for (pk, W, nch, dst, dres, gbase, dim) in [("M0", 512, 4, QCN, qn_[b], PP_MQ + i * 4, 512),
                                                        ("M1", 352, 2, KVCN, kvn_[b], PP_MKV + i * 2, 256)]:
                wt, wr = self.load_piece((pk, l))
                b2 = self.gen.next()
                for c in range(nch):
                    bank = self.proj(wt, wr, 8, W, c * 128, 128, hrhs, hres, n)
                    bk = self.bank[bank]
                    P.add("act", I("activation", dst[:, c, t0:t0 + n], bk[:, 0:n], AF.Copy), r=[bank], w=[dres])
                    sq, sqn = self.tmpB()
                    P.add("act", I("activation", sq[:, 0:n], bk[:, 0:n], AF.Square), r=[bank], w=[sqn])
                    P.add("pe", I("matmul", self.bank[b2][:, 0:n], self.ones128, sq[:, 0:n],
                                  start=(c == 0), stop=(c == nch - 1)), r=[sqn, "cmat"], w=[b2])
                rt, rtn = self.tmpF()
                P.add("act", I("activation", rt[:, 0:n], self.bank[b2][:, 0:n], AF.Ln, bias=EPS, scale=1.0 / dim),
                      r=[b2], w=[rtn])
                P.add("act", I("activation", rt[:, 0:n], rt[:, 0:n], AF.Exp, scale=-0.5), r=[rtn], w=[rtn])
                for c in range(nch):
                    P.add("dve", I("scalar_tensor_tensor", dst[:, c, t0:t0 + n], dst[:, c, t0:t0 + n],
                                   pp[:, gbase + c:gbase + c + 1], rt[:, 0:n], ALU.mult, ALU.mult),
                          r=[dres, rtn, "pp"], w=[dres])
                if pk == "M1":
                    bank = self.proj(wt, wr, 8, W, 256, 96, hrhs, hres, n)
                    bk = self.bank[bank]
                    qg, qgn = self.tmpF()
                    P.add("act", I("activation", qg[0:96, 0:n], bk[0:96, 0:n], AF.Identity, scale=g_k[0:96, :]),
                          r=[bank, "pp"], w=[qgn])
                    if b > 0:
                        tok0 = t0 - NCTX
                        qb, qbn = self.tmpB()
                        P.add("act", I("activation", qb[0:96, 0:n], qg[0:96, 0:n], AF.Copy), r=[qgn], w=[qbn])
                        b3 = self.gen.next()
                        P.add("pe", I("matmul", self.bank[b3][0:96, 0:n], self.perm_m[0:96, 0:96], qb[0:96, 0:n],
                                      start=True, stop=True), r=[qbn, "cmat"], w=[b3])
                        t2, t2n = self.tmpF()
                        P.add("dve", I("tensor_tensor", t2[64:96, 0:n], self.bank[b3][64:96, 0:n],
                                       self.ropeT[64:96, 1, tok0:tok0 + n], ALU.mult), r=[b3, "rope"], w=[t2n])
                        P.add("pool", I("tensor_tensor", qg[64:96, 0:n], qg[64:96, 0:n],
                                        self.ropeT[64:96, 0, tok0:tok0 + n], ALU.mult), r=[qgn, "rope"], w=[qgn])
                        P.add("pool", I("tensor_tensor", qg[64:96, 0:n], qg[64:96, 0:n], t2[64:96, 0:n], ALU.add),
                              r=[qgn, t2n], w=[qgn])
                    P.add("dve", I("tensor_copy", KRG[64:96, t0:t0 + n], qg[64:96, 0:n]), r=[qgn], w=[krn_[b]])

        import os
        STOP = int(os.environ.get("MLA_STOP", "99"))
        if STOP <= 1:
            return
        KVMODE = os.environ.get("KVMODE", "P")
        for hg in range(8):
            if STOP <= 5 and hg > 0:
                return
            self.gen = self.gen_big
            for b, (t0, n) in enumerate(TB):
                wt, wr = self.load_piece(("M2", l, hg))
                tv = wt[:, 0:512].rearrange("p (k w) -> p k w", k=2)
                kvrhs = [KVCN[:, k, t0:t0 + n] for k in range(2)]
                for hh in range(2):
                    bank = self.proj(wt, wr, 2, 256, hh * 128, 64, kvrhs, [kvn_[b]], n)
                    self.qknorm_rope(bank, 96, n, g_k, self.ones96, 96, None, KTm[0:96, hh, t0:t0 + n], ktn[b],
                                     extra_rows=(64, 96, KRG[64:96, t0:t0 + n], krn_[b]), mode=KVMODE)
                for u in range(n // 128):
                    ch = (t0 + u * 128) // 128
                    vbase = VOFF + ch * VCH
                    bA = self.gen.next()
                    ins = []
                    for k in range(2):
                        rv = tv[:, k, :].rearrange("p (h c) -> p h c", h=2)[:, :, 64:128]
                        ins.append(I("matmul", self.bank[bA][:, 0:128].rearrange("p (h c) -> p h c", h=2),
                                     KVCN[:, k, t0 + u * 128:t0 + (u + 1) * 128], rv, start=(k == 0), stop=(k == 1)))
                    P.add("pe", ins, r=[wr, kvn_[b]], w=[bA])
                    P.add("dve", I("tensor_copy", self.unit_AP(vbase, vbase + 128),
                                   self.bank[bA][:, 0:128].rearrange("p (h c) -> p h c", h=2)),
                          r=[bA], w=[vn[b]])
            self.gen = self.gen_small
            blocks = [b for b in range(5) if not (b == 0 and not cc)]

            def qbuild(b, slot):
                t0, n = TB[b]
                rope = None if b == 0 else (self.perm_m, 64, 96, t0 - NCTX)
                wt, wr = self.load_piece(("M3", l, hg))
                qrhs = [QCN[:, k, t0:t0 + n] for k in range(4)]
                for hh in range(2):
                    bank = self.proj(wt, wr, 4, 192, hh * 96, 96, qrhs, [qn_[b]], n)
                    self.qknorm_rope(bank, 96, n, g_q, self.ones96, 96, rope,
                                     self.QT[0:96, slot * 2 + hh, 0:n], f"QT{slot * 2 + hh}")

            qbuild(blocks[0], 0)
            for bi, b in enumerate(blocks):
                t0, n = TB[b]
                nchunks = 2 if b == 0 else 18
                slot = bi % 2
                if bi + 1 < len(blocks):
                    qbuild(blocks[bi + 1], (bi + 1) % 2)
                odd = (hg % 2 == 1)
                oslot = 0 if odd else 1 + b
                maps = []
                for hh in range(2):
                    def kt(ch, hh=hh):
                        return KTm[0:96, hh, ch * 128:(ch + 1) * 128]
                    if hh == 0:
                        def uf(ch):
                            return V3[:, ch, 0:128]
                        vrows, srows = (0, 64), (64, 128)
                    else:
                        def uf(ch):
                            return V3[:, ch, 64:192]
                        vrows, srows = (64, 128), (0, 64)

                    def done(obanks, vrows=vrows, srows=srows, n=n, oslot=oslot):
                        self.o_norm(obanks[0], vrows, srows, n, self.OT[vrows[0]:vrows[1], oslot, 0:n], f"OT{oslot}")
                    maps.append(dict(kt=kt, kres=lambda ch: ktn[boc(ch)],
                                     q=self.QT[0:96, slot * 2 + hh, 0:n], qres=f"QT{slot * 2 + hh}",
                                     units=[(uf, lambda ch: vn[boc(ch)])], done=done))
                self.attention(maps, n, nchunks, 96 ** -0.5)
                if not odd:
                    continue
                col = 4 if b == 0 else s
                wt, wr = self.load_piece(("M5", l, hg // 2))
                tv5 = wt[:, 0:2048].rearrange("p (k w) -> p k w", k=2)
                for o in range(8):
                    bank = self.gen.next()
                    P.add("pe", [I("matmul", self.bank[bank][:, 0:n], tv5[:, 0, o * 128:(o + 1) * 128], self.OT[:, 1 + b, 0:n],
                                   start=True, stop=False),
                                 I("matmul", self.bank[bank][:, 0:n], tv5[:, 1, o * 128:(o + 1) * 128], self.OT[:, 0, 0:n],
                                   start=False, stop=True)],
                          r=[wr, "OT0", f"OT{1 + b}"], w=[bank])
                    P.add("dve", I("scalar_tensor_tensor", self.xT[:, o, t0:t0 + n], self.bank[bank][:, 0:n],
                                   self.modcol(l, 2, o, col), self.xT[:, o, t0:t0 + n], ALU.mult, ALU.add),
                          r=[bank, "modT", f"x{b}"], w=[f"x{b}"])

    def ffn(self, l, s, cc):
        P = self.P
        HID = self.aview(0, [32, 512])
        hn = [f"hid{j}" for j in range(32)]
        self.set_arena(hn)
        self.gen = self.gen_big
        fblocks = [b for b in range(5) if not (b == 0 and not cc)]
        self.norm_mod(l, s, fblocks[0], 2, fblocks[0] % 2 == 1)
        for bi, b in enumerate(fblocks):
            t0, n = TB[b]
            col = 4 if b == 0 else s
            alt = (b % 2 == 1)
            if bi + 1 < len(fblocks):
                self.norm_mod(l, s, fblocks[bi + 1], 2, not alt)
            hTb, hres = self.hbuf(alt)
            hrhs = [hTb[:, k, 0:n] for k in range(8)]
            for g in range(8):
                wt, wr = self.load_piece(("W1", l, g))
                for c in range(4):
                    j = g * 4 + c
                    bank = self.proj(wt, wr, 8, 512, c * 128, 128, hrhs, hres, n)
                    t, tn = self.tmpF()
                    P.add("act", I("activation", t[:, 0:n], self.bank[bank][:, 0:n], AF.Relu), r=[bank], w=[tn])
                    P.add("pool", I("tensor_tensor", HID[:, j, 0:n], t[:, 0:n], t[:, 0:n], ALU.mult), r=[tn], w=[hn[j]])
            hidrhs = [HID[:, k, 0:n] for k in range(32)]
            for o in range(8):
                wt, wr = self.load_piece(("W2", l, o))
                bank = self.proj(wt, wr, 32, 128, 0, 128, hidrhs, hn, n)
                P.add("dve", I("scalar_tensor_tensor", self.xT[:, o, t0:t0 + n], self.bank[bank][:, 0:n],
                               self.modcol(l, 5, o, col), self.xT[:, o, t0:t0 + n], ALU.mult, ALU.add),
                      r=[bank, "modT", f"x{b}"], w=[f"x{b}"])


def _rope_tables():
    tabs = np.zeros((2, 128, 2, NLAT), np.float32)
    rows = (np.arange(NLAT) // 64).astype(np.float32)
    colsp = (np.arange(NLAT) % 64).astype(np.float32)
    for which, (rot_dim, p0, p1, per) in enumerate([(64, 0, 128, 64), (32, 64, 96, 32)]):
        n_freq = rot_dim // 4
        inv = (10000.0 ** (-np.arange(n_freq, dtype=np.float32) / n_freq)).astype(np.float32)
        ang = np.concatenate([rows[:, None] * inv[None, :], colsp[:, None] * inv[None, :]], axis=1)
        half = rot_dim // 2
        for p in range(p0, p1):
            d = (p - p0) % per
            j = d % half
            sgn = -1.0 if d < half else 1.0
            tabs[which, p, 0, :] = np.cos(ang[:, j])
            tabs[which, p, 1, :] = sgn * np.sin(ang[:, j])
    return tabs.reshape(2, 128, 2 * NLAT)


def _const_mats():
    cm = np.zeros((128, 640), np.float32)
    cm[:, 0:128] = 1.0
    cm[0:64, 128:192] = 1.0
    cm[64:128, 192:256] = 1.0
    cm[0:96, 256:352] = 1.0
    for m in range(128):
        d = m % 64
        k = m + 32 if d < 32 else m - 32
        cm[k, 384 + m] = 1.0
    for m in range(64, 96):
        k = m + 16 if m < 80 else m - 16
        cm[k, 512 + m] = 1.0
    return cm


def _cols(v):
    return np.ascontiguousarray(v.reshape(-1, 128).T)


def _prep_shared(inp, nlayers):
    f = np.float32
    ab_w_in = np.asarray(inp["ab_w_in"], f)
    ab_w_out = np.asarray(inp["ab_w_out"], f)
    bq_perm = np.concatenate([np.arange(1536 + (kv * 4 + g) * 64, 1536 + (kv * 4 + g) * 64 + 64)
                              for g in range(4) for kv in range(2)])
    perm = np.concatenate([np.arange(512, 1024), np.arange(2048, 2176), np.arange(2176, 2304),
                           np.arange(1024, 1536), np.arange(0, 512), bq_perm])
    w_in_e = np.ascontiguousarray(ab_w_in[:, :, perm])
    rperm = np.concatenate([np.arange(0, 512)] +
                           [np.arange(512 + (kv * 4 + g) * 64, 512 + (kv * 4 + g) * 64 + 64)
                            for g in range(4) for kv in range(2)])
    w_out_e = np.ascontiguousarray(ab_w_out[:, rperm, :])
    mla_w_in = np.asarray(inp["mla_w_in"], f)
    w_in_m = np.zeros((2, 1024, 864), f)
    w_in_m[:, :, 0:768] = mla_w_in[:, :, 0:768]
    w_in_m[:, :, 832:864] = mla_w_in[:, :, 768:800]
    pp = np.zeros((128, NPP), f)
    nmix = np.asarray(inp["norm_mix"], f)
    nmlp = np.asarray(inp["norm_mlp"], f)
    adab = np.asarray(inp["ada_b"], f)
    for l in range(4):
        pp[:, PP_NMIX + l * 8:PP_NMIX + l * 8 + 8] = _cols(nmix[l])
        pp[:, PP_NMLP + l * 8:PP_NMLP + l * 8 + 8] = _cols(nmlp[l])
        pp[:, PP_ADAB + l * 48:PP_ADAB + l * 48 + 48] = _cols(adab[l])
    dqk = np.asarray(inp["diff_qk_norm"], f)
    gqk = np.asarray(inp["gqa_qk_norm"], f)
    subln = np.asarray(inp["diff_subln"], f)
    dlam = np.asarray(inp["diff_lambda"], f)
    mq = np.asarray(inp["mla_q_norm"], f)
    mkv = np.asarray(inp["mla_kv_norm"], f)
    mqk = np.asarray(inp["mla_qk_norm"], f)
    for i in range(2):
        pp[:, PP_GE + i * 4 + 0] = np.tile(dqk[i, 0], 2)
        pp[:, PP_GE + i * 4 + 1] = np.tile(dqk[i, 1], 2)
        pp[:, PP_GE + i * 4 + 2] = np.tile(gqk[i, 0], 2)
        pp[:, PP_GE + i * 4 + 3] = np.tile(gqk[i, 1], 2)
        pp[:, PP_SUBLN + i] = subln[i]
        pp[:, PP_DLAM + i * 256:PP_DLAM + (i + 1) * 256] = np.broadcast_to(dlam[i].reshape(1, 256), (128, 256))
        pp[:, PP_MQ + i * 4:PP_MQ + i * 4 + 4] = _cols(mq[i])
        pp[:, PP_MKV + i * 2:PP_MKV + i * 2 + 2] = _cols(mkv[i])
        pp[0:96, PP_MQK + i * 2] = mqk[i, 0]
        pp[0:96, PP_MQK + i * 2 + 1] = mqk[i, 1]
    shared = {
        "ppd": pp, "cstd": _const_mats(), "roped": _rope_tables(),
        "ada_w": np.ascontiguousarray(np.asarray(inp["ada_w"], f)),
        "mlp_w1": np.ascontiguousarray(np.asarray(inp["mlp_w1"], f)),
        "mlp_w2": np.ascontiguousarray(np.asarray(inp["mlp_w2"], f)),
        "w_in_e": w_in_e, "w_out_e": w_out_e, "w_in_m": w_in_m,
        "w_q_up": np.ascontiguousarray(np.asarray(inp["mla_w_q_up"], f)),
        "w_kv_up": np.ascontiguousarray(np.asarray(inp["mla_w_kv_up"], f)),
        "w_out_m": np.ascontiguousarray(np.asarray(inp["mla_w_out"], f)),
    }
    return shared


def _prep_core(inp, seqs):
    f = np.float32
    x = np.asarray(inp["x"], f)
    ctx = np.asarray(inp["ctx"], f)
    c = np.asarray(inp["c"], f)
    c_ctx = np.asarray(inp["c_ctx"], f)
    ns = len(seqs)
    xTd = np.empty((ns, 128, 8, NT), f)
    for j, sidx in enumerate(seqs):
        cat = np.concatenate([ctx[sidx], x[sidx]], axis=0)
        xTd[j] = cat.T.reshape(8, 128, NT).transpose(1, 0, 2)
    cT = np.zeros((128, 8, 5), f)
    for j, sidx in enumerate(seqs):
        cT[:, :, j] = _cols(c[sidx])
    cT[:, :, 4] = _cols(c_ctx)
    return {"xTd": xTd, "cTd": cT}


def _unpack_out(outT):
    ns = outT.shape[0]
    return np.ascontiguousarray(outT.transpose(0, 2, 1, 3).reshape(ns, 1024, NLAT).transpose(0, 2, 1))


_NC_CACHE = {}


def run(inputs, n_cores=N_CORES, nseq=SEQ_PER_CORE, nlayers=DEPTH, trace=False):
    key = (nseq, nlayers)
    if key not in _NC_CACHE:
        _NC_CACHE[key] = Builder(nseq, nlayers).build()
    nc = _NC_CACHE[key]
    shared = _prep_shared(inputs, nlayers)
    in_maps = []
    for cidx in range(n_cores):
        seqs = list(range(cidx * nseq, (cidx + 1) * nseq))
        m = dict(shared)
        m.update(_prep_core(inputs, seqs))
        in_maps.append(m)
    res = run_bass_kernel_spmd(nc, in_maps, core_ids=list(range(n_cores)), trace=trace)
    outs = [_unpack_out(np.asarray(r["outT"])) for r in res.results]
    return np.concatenate(outs, axis=0), res


def kernel(**inputs):
    out, _ = run(inputs)
    return out.astype(np.float32)
```
